# Optimizing a Trainium2 kernel written in Bass

```python
import math
import jax
import jax.numpy as jnp
from jax import lax
import numpy as np

D_MODEL = 2048
BATCH = 16
SEQ = 2048
DEPTH = 4

CTX_LEN = 256
GRID_W = 64
N_EVEN = (DEPTH + 1) // 2
N_ODD = DEPTH // 2
HALF_MIX = D_MODEL // 2
MIX_WIDTH = 2 * HALF_MIX

S5_CH = HALF_MIX
S5_GROUP_CH = 16
S5_GROUPS = S5_CH // S5_GROUP_CH
S5_STATE = 64
S5_DT_MIN = 1e-3
S5_DT_MAX = 1e-1

NA_HEAD_DIM = 128
NA_HEADS = HALF_MIX // NA_HEAD_DIM
NA_ROWS = 8
NA_COLS = 16
NA_QCOLS = 16
NA_KCOLS = NA_QCOLS + NA_COLS

DIFF_HEAD_DIM = 128
DIFF_HEADS = HALF_MIX // (2 * DIFF_HEAD_DIM)

SWA_HEAD_DIM = 128
SWA_HEADS = HALF_MIX // SWA_HEAD_DIM
SWA_KV_HEADS = SWA_HEADS // 4
SWA_WINDOW = 128
SWA_BLOCK = 128
SWA_SPAN = SWA_BLOCK + 2 * SWA_WINDOW
Q_BLOCK = 128

FFN_HIDDEN = ((8 * D_MODEL // 3 + 255) // 256) * 256
CONV_W = 3

ROPE_BASE = 10000.0
ROPE_AXIS_DIM = 64
ROPE_FREQS = ROPE_AXIS_DIM // 2

NORM_EPS = 1e-6
NEG_INF = -1e30
F32 = jnp.float32

EV_IN = S5_CH + 3 * HALF_MIX
OD_SIZES = (HALF_MIX, HALF_MIX, HALF_MIX, HALF_MIX,
            SWA_KV_HEADS * SWA_HEAD_DIM, SWA_KV_HEADS * SWA_HEAD_DIM)
OD_IN = sum(OD_SIZES)
OD_CUTS = tuple(int(v) for v in np.cumsum(OD_SIZES)[:-1])

kernel_name = 'hybrid_s5_natten_diffattn_swa_dit'


def rms_norm(x, g):
    xf = x.astype(F32)
    y = xf * lax.rsqrt(jnp.mean(xf * xf, axis=-1, keepdims=True) + NORM_EPS)
    return (y * g.astype(F32)).astype(x.dtype)


def rope_2d_tables(L):
    t = jnp.arange(L, dtype=jnp.int32)
    pos = jnp.stack([t // GRID_W, t % GRID_W], axis=-1).astype(F32)
    inv = ROPE_BASE ** (-2.0 * jnp.arange(ROPE_FREQS, dtype=F32) / ROPE_AXIS_DIM)
    ang = pos[:, :, None] * inv
    return jnp.cos(ang), jnp.sin(ang)


def apply_rope_2d(x, cos, sin):
    xs = x.astype(F32).reshape(x.shape[:-1] + (2, 2, ROPE_FREQS))
    x1, x2 = xs[..., 0, :], xs[..., 1, :]
    c, s = cos[:, None], sin[:, None]
    out = jnp.stack([x1 * c - x2 * s, x2 * c + x1 * s], axis=-2)
    return out.reshape(x.shape).astype(x.dtype)


def ctx_self_attention(q, k, v, sink=None):
    B_, C, H, dh = q.shape
    KVH = k.shape[2]
    qg = q.reshape(B_, C, KVH, H // KVH, dh)
    s = jnp.einsum('bqkgd,bskd->bkgqs', qg, k).astype(F32) * dh ** -0.5
    if sink is not None:
        sk = jnp.broadcast_to(sink.astype(F32).reshape(1, KVH, H // KVH, 1, 1), s.shape[:-1] + (1,))
        s = jnp.concatenate([s, sk], axis=-1)
    p = jax.nn.softmax(s, axis=-1)[..., :C].astype(v.dtype)
    o = jnp.einsum('bkgqs,bskd->bqkgd', p, v)
    return o.reshape(B_, C, H * dh)


def s5_discretize(lam_re, lam_im, log_dt, b_re, b_im):
    lam_re = jnp.minimum(lam_re.astype(F32), -1e-4)
    lam_im = lam_im.astype(F32)
    dt = jnp.exp(log_dt.astype(F32))[:, None]
    mag = jnp.exp(lam_re * dt)
    lb_re, lb_im = mag * jnp.cos(lam_im * dt), mag * jnp.sin(lam_im * dt)
    den = lam_re * lam_re + lam_im * lam_im
    num_re = lb_re - 1.0
    coef_re = (num_re * lam_re + lb_im * lam_im) / den
    coef_im = (lb_im * lam_re - num_re * lam_im) / den
    b_re, b_im = b_re.astype(F32), b_im.astype(F32)
    bb_re = coef_re[..., None] * b_re - coef_im[..., None] * b_im
    bb_im = coef_re[..., None] * b_im + coef_im[..., None] * b_re
    return lb_re, lb_im, bb_re, bb_im


def _complex_affine_combine(e1, e2):
    a1r, a1i, b1r, b1i = e1
    a2r, a2i, b2r, b2i = e2
    return (a2r * a1r - a2i * a1i, a2r * a1i + a2i * a1r,
            a2r * b1r - a2i * b1i + b2r, a2r * b1i + a2i * b1r + b2i)


def diag_scan(lb_re, lb_im, bu_re, bu_im, h0=None):
    if h0 is not None:
        bu_re = bu_re.at[0].add(lb_re * h0[0] - lb_im * h0[1])
        bu_im = bu_im.at[0].add(lb_re * h0[1] + lb_im * h0[0])
    shape = (bu_re.shape[0], 1) + lb_re.shape
    elems = (jnp.broadcast_to(lb_re, shape), jnp.broadcast_to(lb_im, shape), bu_re, bu_im)
    _, _, h_re, h_im = lax.associative_scan(_complex_affine_combine, elems, axis=0)
    return h_re, h_im


def s5_direction(u_ctx, u_lat, lb_re, lb_im, bb_re, bb_im, reverse):
    def drive(t):
        pair = (jnp.einsum('tbgh,gph->tbgp', t, bb_re), jnp.einsum('tbgh,gph->tbgp', t, bb_im))
        return (jnp.flip(pair[0], 0), jnp.flip(pair[1], 0)) if reverse else pair
    hc = diag_scan(lb_re, lb_im, *drive(u_ctx))
    hx = diag_scan(lb_re, lb_im, *drive(u_lat), h0=(hc[0][-1], hc[1][-1]))
    if reverse:
        hc = (jnp.flip(hc[0], 0), jnp.flip(hc[1], 0))
        hx = (jnp.flip(hx[0], 0), jnp.flip(hx[1], 0))
    return hc, hx


def s5_readout(h, c_re, c_im):
    return (jnp.einsum('tbgp,ghp->tbgh', h[0], c_re.astype(F32))
            - jnp.einsum('tbgp,ghp->tbgh', h[1], c_im.astype(F32)))


def s5_mixer(u, uc, lam_re, lam_im, log_dt, b_re, b_im, c_re, c_im, d_skip, w_glu, with_ctx):
    def time_major(t):
        return t.astype(F32).reshape(t.shape[0], t.shape[1], S5_GROUPS, S5_GROUP_CH).transpose(1, 0, 2, 3)
    ut, uct = time_major(u), time_major(uc)
    d = d_skip.astype(F32).reshape(S5_GROUPS, S5_GROUP_CH)
    y = d * ut
    yc = d * uct if with_ctx else None
    for r in range(2):
        lb_re, lb_im, bb_re, bb_im = s5_discretize(lam_re[r], lam_im[r], log_dt[r], b_re[r], b_im[r])
        hc, hx = s5_direction(uct, ut, lb_re, lb_im, bb_re, bb_im, reverse=(r == 1))
        y = y + s5_readout(hx, c_re[r], c_im[r])
        if with_ctx:
            yc = yc + s5_readout(hc, c_re[r], c_im[r])

    def glu(t):
        t = jax.nn.gelu(t.transpose(1, 0, 2, 3).reshape(t.shape[1], t.shape[0], S5_CH)).astype(u.dtype)
        return t * jax.nn.sigmoid(t @ w_glu)

    return glu(y), (glu(yc) if with_ctx else None)


def neighborhood_attention(q, k, v, kc, vc, rpb):
    B_, L, H, dh = q.shape
    rows = L // GRID_W
    kr = min(NA_ROWS, rows)
    scale = dh ** -0.5
    nqb = GRID_W // NA_QCOLS
    qg = q.reshape(B_, rows, GRID_W, H, dh)
    kg = k.reshape(B_, rows, GRID_W, H, dh)
    vg = v.reshape(B_, rows, GRID_W, H, dh)
    q_cols = jnp.arange(GRID_W).reshape(nqb, NA_QCOLS)
    win_start = jnp.clip(q_cols - NA_COLS // 2, 0, GRID_W - NA_COLS)
    k_cols = (jnp.clip(q_cols[:, :1] - NA_COLS // 2, 0, GRID_W - NA_KCOLS)
              + jnp.arange(NA_KCOLS))
    col_valid = ((k_cols[:, None, :] >= win_start[..., None])
                 & (k_cols[:, None, :] < win_start[..., None] + NA_COLS))
    col_idx = jnp.clip(k_cols[:, None, :] - q_cols[..., None] + NA_COLS - 1, 0, 2 * NA_COLS - 2)
    rpb_f = rpb.astype(F32)
    n_loc = kr * NA_KCOLS

    def row_step(r):
        rs = jnp.clip(r - kr // 2, 0, rows - kr)
        q_r = lax.dynamic_index_in_dim(qg, r, axis=1, keepdims=False).reshape(B_, nqb, NA_QCOLS, H, dh)
        k_r = lax.dynamic_slice_in_dim(kg, rs, kr, axis=1)[:, :, k_cols]
        v_r = lax.dynamic_slice_in_dim(vg, rs, kr, axis=1)[:, :, k_cols]
        row_idx = rs + jnp.arange(kr) - r + NA_ROWS - 1
        bias = rpb_f[:, row_idx][:, :, col_idx]
        bias = jnp.where(col_valid, bias, NEG_INF).transpose(0, 2, 3, 1, 4)
        s_loc = jnp.einsum('bnqhd,brnkhd->bhnqrk', q_r, k_r).astype(F32) * scale + bias
        s_ctx = jnp.einsum('bnqhd,bchd->bhnqc', q_r, kc).astype(F32) * scale
        logits = jnp.concatenate([s_loc.reshape(B_, H, nqb, NA_QCOLS, n_loc), s_ctx], axis=-1)
        p = jax.nn.softmax(logits, axis=-1).astype(v.dtype)
        o = (jnp.einsum('bhnqrk,brnkhd->bnqhd', p[..., :n_loc].reshape(s_loc.shape), v_r)
             + jnp.einsum('bhnqc,bchd->bnqhd', p[..., n_loc:], vc))
        return o.reshape(B_, GRID_W, H * dh)

    out = lax.map(row_step, jnp.arange(rows))
    return out.transpose(1, 0, 2, 3).reshape(B_, L, H * dh)


def diff_attend(q, k, v, lam):
    s = jnp.einsum('bqhcd,bkhcd->bhcqk', q, k).astype(F32) * q.shape[-1] ** -0.5
    p = jax.nn.softmax(s, axis=-1)
    w = p[:, :, 0] - lam * p[:, :, 1]
    return jnp.einsum('bhqk,bkhe->bqhe', w.astype(v.dtype), v)


def window_attention(q, k, v, kc, vc, sink):
    B_, L, H, dh = q.shape
    KVH = k.shape[2]
    G = H // KVH
    C = kc.shape[1]
    scale = dh ** -0.5
    nb = L // SWA_BLOCK
    pad = ((0, 0), (SWA_WINDOW, SWA_WINDOW), (0, 0), (0, 0))
    kp, vp = jnp.pad(k, pad), jnp.pad(v, pad)
    q_blocks = q.reshape(B_, nb, SWA_BLOCK, KVH, G, dh).transpose(1, 0, 2, 3, 4, 5)
    sink_col = sink.astype(F32).reshape(1, KVH, G, 1, 1)
    offs = jnp.arange(SWA_SPAN) - SWA_WINDOW
    in_band = jnp.abs(offs[None, :] - jnp.arange(SWA_BLOCK)[:, None]) <= SWA_WINDOW

    def block_step(args):
        n, qb = args
        start = n * SWA_BLOCK
        ks = lax.dynamic_slice_in_dim(kp, start, SWA_SPAN, axis=1)
        vs = lax.dynamic_slice_in_dim(vp, start, SWA_SPAN, axis=1)
        kpos = start + offs
        valid = in_band & ((kpos >= 0) & (kpos < L))[None, :]
        s_loc = jnp.where(valid, jnp.einsum('bqkgd,bskd->bkgqs', qb, ks).astype(F32) * scale, NEG_INF)
        s_ctx = jnp.einsum('bqkgd,bckd->bkgqc', qb, kc).astype(F32) * scale
        sinks = jnp.broadcast_to(sink_col, s_ctx.shape[:-1] + (1,))
        p = jax.nn.softmax(jnp.concatenate([s_loc, s_ctx, sinks], axis=-1), axis=-1).astype(v.dtype)
        o = (jnp.einsum('bkgqs,bskd->bqkgd', p[..., :SWA_SPAN], vs)
             + jnp.einsum('bkgqc,bckd->bqkgd', p[..., SWA_SPAN:SWA_SPAN + C], vc))
        return o.reshape(B_, SWA_BLOCK, H * dh)

    out = lax.map(block_step, (jnp.arange(nb), q_blocks))
    return out.transpose(1, 0, 2, 3).reshape(B_, L, H * dh)


def even_mixer(h, hc, w_in, lam_re, lam_im, log_dt, b_re, b_im, c_re, c_im, d_skip, w_glu, rpb, with_ctx):
    B_, L, _ = h.shape
    C = hc.shape[1]
    z = (h @ w_in).reshape(B_, L, 4, HALF_MIX)
    zc = (hc @ w_in).reshape(B_, C, 4, HALF_MIX)

    def heads(t):
        return t.reshape(t.shape[:2] + (NA_HEADS, NA_HEAD_DIM))

    ya, yac = s5_mixer(z[:, :, 0], zc[:, :, 0], lam_re, lam_im, log_dt, b_re, b_im, c_re, c_im,
                       d_skip, w_glu, with_ctx)
    yb = neighborhood_attention(heads(z[:, :, 1]), heads(z[:, :, 2]), heads(z[:, :, 3]),
                                heads(zc[:, :, 2]), heads(zc[:, :, 3]), rpb)
    y = jnp.concatenate([ya, yb], axis=-1)
    if not with_ctx:
        return y, None
    ybc = ctx_self_attention(heads(zc[:, :, 1]), heads(zc[:, :, 2]), heads(zc[:, :, 3]))
    return y, jnp.concatenate([yac, ybc], axis=-1)


def odd_mixer(h, hc, w_in, diff_lambda, subln_g, sink, lam_init, cos, sin, with_ctx):
    B_, L, _ = h.shape

    def parts(t):
        b, T = t.shape[:2]
        dq, dk, dv, wq, wk, wv = jnp.split(t, OD_CUTS, axis=-1)
        return (dq.reshape(b, T, DIFF_HEADS, 2, DIFF_HEAD_DIM), dk.reshape(b, T, DIFF_HEADS, 2, DIFF_HEAD_DIM),
                dv.reshape(b, T, DIFF_HEADS, 2 * DIFF_HEAD_DIM), wq.reshape(b, T, SWA_HEADS, SWA_HEAD_DIM),
                wk.reshape(b, T, SWA_KV_HEADS, SWA_HEAD_DIM), wv.reshape(b, T, SWA_KV_HEADS, SWA_HEAD_DIM))

    dq, dk, dv, wq, wk, wv = parts(h @ w_in)
    dqc, dkc, dvc, wqc, wkc, wvc = parts(hc @ w_in)

    def rope_pairs(t):
        return apply_rope_2d(t.reshape(t.shape[:2] + (2 * DIFF_HEADS, DIFF_HEAD_DIM)), cos, sin).reshape(t.shape)

    dq, dk = rope_pairs(dq), rope_pairs(dk)
    wq, wk = apply_rope_2d(wq, cos, sin), apply_rope_2d(wk, cos, sin)

    lf = diff_lambda.astype(F32)
    lam = jnp.exp(jnp.sum(lf[0] * lf[1])) - jnp.exp(jnp.sum(lf[2] * lf[3])) + lam_init

    def post(o):
        return (rms_norm(o, subln_g) * (1.0 - lam_init)).reshape(o.shape[:2] + (HALF_MIX,))

    k_all = jnp.concatenate([dkc, dk], axis=1)
    v_all = jnp.concatenate([dvc, dv], axis=1)
    nb = L // Q_BLOCK
    q_blocks = dq.reshape(B_, nb, Q_BLOCK, DIFF_HEADS, 2, DIFF_HEAD_DIM).swapaxes(0, 1)
    oc = lax.map(lambda qb: diff_attend(qb, k_all, v_all, lam), q_blocks)
    y_c = post(oc.swapaxes(0, 1).reshape(B_, L, DIFF_HEADS, 2 * DIFF_HEAD_DIM))
    y_d = window_attention(wq, wk, wv, wkc, wvc, sink)
    y = jnp.concatenate([y_c, y_d], axis=-1)
    if not with_ctx:
        return y, None
    yc_c = post(diff_attend(dqc, dkc, dvc, lam))
    yc_d = ctx_self_attention(wqc, wkc, wvc, sink)
    return y, jnp.concatenate([yc_c, yc_d], axis=-1)


def conv_ffn(h, w_up, dw_w, dw_b, w_down):
    u = h @ w_up
    u = lax.conv_general_dilated(u, dw_w[:, None, :].astype(u.dtype), (1,),
                                 [(CONV_W // 2, CONV_W // 2)],
                                 dimension_numbers=('NWC', 'WIO', 'NWC'),
                                 feature_group_count=u.shape[-1]) + dw_b
    gate, val = jnp.split(u, 2, axis=-1)
    return (jax.nn.silu(gate) * val) @ w_down


def setup_inputs(seed: int = 0) -> dict:
    key = jax.random.key(seed)
    ks = jax.random.split(key, 32)

    def nrm(k, shape, scale):
        return jax.random.normal(k, shape, F32) * scale

    def gain(k, shape):
        return 1.0 + nrm(k, shape, 0.02)

    D, F = D_MODEL, FFN_HIDDEN
    s5_shape = (N_EVEN, 2, S5_GROUPS, S5_STATE)
    n_idx = jnp.arange(S5_STATE, dtype=F32)
    return {
        'x': nrm(ks[0], (BATCH, SEQ, D), 1.0),
        'c': nrm(ks[1], (BATCH, D), 1.0),
        'ctx': nrm(ks[2], (BATCH, CTX_LEN, D), 1.0),
        'c_ctx': nrm(ks[3], (D,), 1.0),
        'ada_w': nrm(ks[4], (DEPTH, D, 6 * D), 0.5 * D ** -0.5),
        'ada_b': nrm(ks[5], (DEPTH, 6 * D), 0.01),
        'norm_mix_g': gain(ks[6], (DEPTH, D)),
        'norm_ffn_g': gain(ks[7], (DEPTH, D)),
        'w_out': nrm(ks[8], (DEPTH, MIX_WIDTH, D), MIX_WIDTH ** -0.5),
        'ffn_w_up': nrm(ks[9], (DEPTH, D, 2 * F), D ** -0.5),
        'ffn_dw_w': nrm(ks[10], (DEPTH, CONV_W, 2 * F), CONV_W ** -0.5),
        'ffn_dw_b': nrm(ks[11], (DEPTH, 2 * F), 0.01),
        'ffn_w_down': nrm(ks[12], (DEPTH, F, D), F ** -0.5),
        'ev_w_in': nrm(ks[13], (N_EVEN, D, EV_IN), D ** -0.5),
        's5_lam_re': -0.5 + nrm(ks[14], s5_shape, 0.01),
        's5_lam_im': math.pi * n_idx + nrm(ks[15], s5_shape, 0.01),
        's5_log_dt': jax.random.uniform(ks[16], (N_EVEN, 2, S5_GROUPS), F32,
                                        math.log(S5_DT_MIN), math.log(S5_DT_MAX)),
        's5_b_re': nrm(ks[17], s5_shape + (S5_GROUP_CH,), (2 * S5_GROUP_CH) ** -0.5),
        's5_b_im': nrm(ks[18], s5_shape + (S5_GROUP_CH,), (2 * S5_GROUP_CH) ** -0.5),
        's5_c_re': nrm(ks[19], (N_EVEN, 2, S5_GROUPS, S5_GROUP_CH, S5_STATE), S5_STATE ** -0.5),
        's5_c_im': nrm(ks[20], (N_EVEN, 2, S5_GROUPS, S5_GROUP_CH, S5_STATE), S5_STATE ** -0.5),
        's5_d': nrm(ks[21], (N_EVEN, S5_CH), 1.0),
        's5_w_glu': nrm(ks[22], (N_EVEN, S5_CH, S5_CH), S5_CH ** -0.5),
        'na_rpb': nrm(ks[23], (N_EVEN, NA_HEADS, 2 * NA_ROWS - 1, 2 * NA_COLS - 1), 0.1),
        'od_w_in': nrm(ks[24], (N_ODD, D, OD_IN), D ** -0.5),
        'diff_lambda': nrm(ks[25], (N_ODD, 4, DIFF_HEAD_DIM), 0.1),
        'diff_subln_g': gain(ks[26], (N_ODD, 2 * DIFF_HEAD_DIM)),
        'swa_sink': nrm(ks[27], (N_ODD, SWA_HEADS), 1.0),
        'final_norm_g': gain(ks[28], (D,)),
    }


def reference(x, c, ctx, c_ctx, ada_w, ada_b, norm_mix_g, norm_ffn_g, w_out, ffn_w_up, ffn_dw_w,
              ffn_dw_b, ffn_w_down, ev_w_in, s5_lam_re, s5_lam_im, s5_log_dt, s5_b_re, s5_b_im,
              s5_c_re, s5_c_im, s5_d, s5_w_glu, na_rpb, od_w_in, diff_lambda, diff_subln_g, swa_sink,
              final_norm_g):
    L = x.shape[1]
    cos, sin = rope_2d_tables(L)
    xc = ctx
    s_lat, s_ctx = jax.nn.silu(c), jax.nn.silu(c_ctx)
    for l in range(DEPTH):
        with_ctx = l < DEPTH - 1
        mod = (s_lat @ ada_w[l] + ada_b[l])[:, None, :]
        modc = s_ctx @ ada_w[l] + ada_b[l]
        sh1, sc1, g1, sh2, sc2, g2 = jnp.split(mod, 6, axis=-1)
        sh1c, sc1c, g1c, sh2c, sc2c, g2c = jnp.split(modc, 6, axis=-1)
        h = rms_norm(x, norm_mix_g[l]) * (1 + sc1) + sh1
        hc = rms_norm(xc, norm_mix_g[l]) * (1 + sc1c) + sh1c
        i = l // 2
        if l % 2 == 0:
            y, yc = even_mixer(h, hc, ev_w_in[i], s5_lam_re[i], s5_lam_im[i], s5_log_dt[i], s5_b_re[i],
                               s5_b_im[i], s5_c_re[i], s5_c_im[i], s5_d[i], s5_w_glu[i], na_rpb[i], with_ctx)
        else:
            lam_init = 0.8 - 0.6 * math.exp(-0.3 * l)
            y, yc = odd_mixer(h, hc, od_w_in[i], diff_lambda[i], diff_subln_g[i], swa_sink[i],
                              lam_init, cos, sin, with_ctx)
        x = x + g1 * (y @ w_out[l])
        h = rms_norm(x, norm_ffn_g[l]) * (1 + sc2) + sh2
        x = x + g2 * conv_ffn(h, ffn_w_up[l], ffn_dw_w[l], ffn_dw_b[l], ffn_w_down[l])
        if with_ctx:
            xc = xc + g1c * (yc @ w_out[l])
            hc = rms_norm(xc, norm_ffn_g[l]) * (1 + sc2c) + sh2c
            xc = xc + g2c * conv_ffn(hc, ffn_w_up[l], ffn_dw_w[l], ffn_dw_b[l], ffn_w_down[l])
    return rms_norm(x, final_norm_g)
```

```python
import math
from contextlib import ExitStack
import numpy as np
import concourse.bass as bass
import concourse.mybir as mybir
from concourse.bass_utils import run_bass_kernel_spmd

F32 = mybir.dt.float32
BF16 = mybir.dt.bfloat16
AF = mybir.ActivationFunctionType
ALU = mybir.AluOpType

D = 2048
KC = 16
CTX = 256
LAT = 2048
T = CTX + LAT
FF = 5632
NDS = 12
DEFER_ROPE = True
DEPTH = 4
EPS = 1e-6


class Res:
    __slots__ = ("w", "r")

    def __init__(self):
        self.w = None
        self.r = {}


class KB:
    def __init__(self):
        nc = self.nc = bass.Bass("TRN2", target_bir_lowering=False)
        self.E = {"pe": nc.tensor, "dve": nc.vector, "act": nc.scalar, "pool": nc.gpsimd, "sp": nc.sync}
        self.semh = []
        self.esem = {}
        self.cnt = {}
        for e in ("pe", "dve", "act", "pool"):
            self.esem[e] = len(self.semh)
            self.semh.append(nc.alloc_semaphore("prog_" + e))
            self.cnt[e] = 0
        self.waited = {e: {} for e in self.E}
        self.dsem = {}
        self.dval = {}
        self.dnext = {}
        for q in ("sp", "act", "pool"):
            self.dsem[q] = []
            for i in range(NDS):
                self.dsem[q].append(len(self.semh))
                self.semh.append(nc.alloc_semaphore("d_%s_%d" % (q, i)))
            self.dval[q] = [0] * NDS
            self.dnext[q] = 0
        self.ps = []
        for i in range(8):
            self.ps.append((nc.alloc_psum_tensor("ps%d" % i, [128, 512], F32), Res()))
        self.n_ins = 0

    def _wait(self, e, toks):
        h = self.E[e]
        wd = self.waited[e]
        for tok in toks:
            if tok is None:
                continue
            semid, val, src = tok
            if src == "pe" and e == "pe":
                continue
            if wd.get(semid, 0) >= val:
                continue
            h.wait_ge(self.semh[semid], val)
            wd[semid] = val

    @staticmethod
    def _deps(reads, writes):
        toks = []
        for r in reads:
            toks.append(r.w)
        for w in writes:
            toks.append(w.w)
            toks.extend(w.r.values())
        return toks

    @staticmethod
    def _commit(tok, reads, writes):
        for r in reads:
            r.r[tok[0]] = tok
        for w in writes:
            w.w = tok
            w.r = {}

    def op(self, e, fn, reads=(), writes=()):
        self._wait(e, self._deps(reads, writes))
        ins = fn(self.E[e])
        self.cnt[e] += 1
        ins.then_inc(self.semh[self.esem[e]], 1)
        tok = (self.esem[e], self.cnt[e], e)
        self._commit(tok, reads, writes)
        self.n_ins += 1
        return tok

    def dma(self, q, out, in_, reads=(), writes=()):
        toks = self._deps(reads, writes)
        i = self.dnext[q]
        self.dnext[q] = (i + 1) % NDS
        semid = self.dsem[q][i]
        prev = self.dval[q][i]
        if prev > 0:
            toks.append((semid, prev, "dma"))
        self._wait(q, toks)
        ins = self.E[q].dma_start(out=out, in_=in_)
        self.dval[q][i] = prev + 16
        ins.then_inc(self.semh[semid], 16)
        tok = (semid, prev + 16, "dma")
        self._commit(tok, reads, writes)
        self.n_ins += 1
        return tok

    def mark(self, name):
        if not hasattr(self, "marks"):
            self.marks = []
        self.marks.append((name, self.cnt["pe"]))

    def barrier(self):
        toks = [(self.esem[e], self.cnt[e], e) for e in self.esem if self.cnt[e] > 0]
        for q in self.dsem:
            for i in range(NDS):
                if self.dval[q][i] > 0:
                    toks.append((self.dsem[q][i], self.dval[q][i], "dma"))
        for e in self.E:
            self._wait(e, [t for t in toks if not (t[2] == e)])
        for e in ("dve", "act", "pool"):
            self._wait(e, [t for t in toks if t[2] == e])

    def sb(self, st, name, shape, dtype):
        self.n_sb = getattr(self, "n_sb", 0) + 1
        t = st.enter_context(self.nc.sbuf_tensor("%s_u%d" % (name, self.n_sb), shape, dtype))
        return t, Res()


def fm(v, nch):
    return np.ascontiguousarray(np.asarray(v, np.float32).reshape(nch, 128).T)


def build_program(n_layers=DEPTH, debug_dump=False, mixers=True, mix_parts=("na", "s5")):
    kb = KB()
    nc = kb.nc
    ps = kb.ps

    def din(name, shape, dt=F32):
        return nc.dram_tensor(name, list(shape), dt, kind="ExternalInput").ap()

    def dscr(name, shape, dt):
        return nc.dram_tensor(name, list(shape), dt, kind="Internal").ap()

    xT_in = din("xT", [2, D, T])
    cT_in = din("cT", [128, KC, 3])
    ada_w = din("ada_w", [DEPTH, D, 6 * D])
    adab_in = din("adab", [128, DEPTH, 96])
    gmix_in = din("gmix", [128, DEPTH, KC])
    gffn_in = din("gffn", [128, DEPTH, KC])
    gfin_in = din("gfin", [128, KC])
    w_out = din("w_out", [DEPTH, D, D])
    w_up = din("w_up", [DEPTH, D, 2 * FF])
    w_down = din("w_down", [DEPTH, FF, D])
    dww_in = din("dww", [128, DEPTH, 3, 88])
    dwb_in = din("dwb", [128, DEPTH, 88])
    ident_in = din("ident", [128, 128])
    out_T = nc.dram_tensor("outT", [2, D, LAT], F32, kind="ExternalOutput").ap()

    XT = dscr("XT", [2, D, T], F32)
    YT = dscr("YT", [2, D, T], BF16)
    GT = dscr("GT", [2, FF, T], BF16)

    ex = ExitStack()
    ident_f, r_identf = kb.sb(ex, "ident_f", [128, 128], F32)
    ident_b, r_ident = kb.sb(ex, "ident_b", [128, 128], BF16)
    ones_b, r_ones = kb.sb(ex, "ones_b", [128, 128], BF16)
    mod_sb, r_mod = kb.sb(ex, "mod_sb", [128, DEPTH, 96, 3], F32)
    A1, r_A1 = kb.sb(ex, "A1", [128, DEPTH, KC, 3], F32)
    A2, r_A2 = kb.sb(ex, "A2", [128, DEPTH, KC, 3], F32)
    gfin, r_gfin = kb.sb(ex, "gfin", [128, KC], F32)
    dww, r_dww = kb.sb(ex, "dww", [128, DEPTH, 3, 88], F32)
    dwb, r_dwb = kb.sb(ex, "dwb", [128, DEPTH, 88], F32)
    zero_c, r_zero = kb.sb(ex, "zero_c", [128, 1], F32)

    kb.dma("sp", ident_f[:], ident_in, writes=[r_identf])
    kb.op("dve", lambda e: e.tensor_copy(out=ident_b[:], in_=ident_f[:]), reads=[r_identf], writes=[r_ident])
    kb.op("dve", lambda e: e.memset(ones_b[:], 1.0), writes=[r_ones])
    kb.op("dve", lambda e: e.memset(zero_c[:], 0.0), writes=[r_zero])
    kb.dma("sp", gfin[:], gfin_in, writes=[r_gfin])
    kb.dma("sp", dww[:], dww_in, writes=[r_dww])
    kb.dma("sp", dwb[:], dwb_in, writes=[r_dwb])

    def stage_S0():
        with ExitStack() as st:
            cs, r_cs = kb.sb(st, "cs", [128, KC, 3], F32)
            ssb, r_ssb = kb.sb(st, "ssb", [128, KC, 3], F32)
            adab, r_adab = kb.sb(st, "adab_sb", [128, DEPTH, 96], F32)
            gmix, r_gmix = kb.sb(st, "gmix_sb", [128, DEPTH, KC], F32)
            gffn, r_gffn = kb.sb(st, "gffn_sb", [128, DEPTH, KC], F32)
            wt = [kb.sb(st, "adaw%d" % i, [128, KC, 256], F32) for i in range(2)]
            emitters = []
            if mixers and "s5" in mix_parts:
                for i_ in range((n_layers + 1) // 2):
                    emitters += table_emitters(st, i_, 3)
            kb.dma("sp", cs[:], cT_in, writes=[r_cs])
            kb.dma("sp", adab[:], adab_in, writes=[r_adab])
            kb.dma("sp", gmix[:], gmix_in, writes=[r_gmix])
            kb.dma("sp", gffn[:], gffn_in, writes=[r_gffn])
            kb.op("act", lambda e: e.activation(out=ssb[:], in_=cs[:], func=AF.Silu), reads=[r_cs], writes=[r_ssb])
            it = 0
            for l in range(n_layers):
                pt, r_pt = ps[l % 2]
                for jt in range(48):
                    w_t, r_w = wt[it % 2]
                    it += 1
                    if it % 2 == 0 and emitters:
                        emitters.pop(0)()
                    src = ada_w[l].rearrange("(kc p) n -> p kc n", p=128)[:, :, jt * 256:(jt + 1) * 256]
                    kb.dma("sp", w_t[:], src, writes=[r_w])
                    for jj in range(2):
                        j = jt * 2 + jj
                        for kc in range(KC):
                            kb.op("pe", lambda e, kc=kc, jj=jj, j=j, w_t=w_t, pt=pt: e.matmul(
                                pt[:, 3 * j:3 * j + 3], lhsT=w_t[:, kc, jj * 128:(jj + 1) * 128], rhs=ssb[:, kc, :],
                                start=(kc == 0), stop=(kc == KC - 1)),
                                reads=[r_w, r_ssb], writes=[r_pt])
                kb.op("dve", lambda e, l=l, pt=pt: e.tensor_tensor(
                    out=mod_sb[:, l], in0=pt[:, 0:288].rearrange("p (j r) -> p j r", r=3),
                    in1=adab[:, l].unsqueeze(2).broadcast_to([128, 96, 3]), op=ALU.add),
                    reads=[r_pt, r_adab], writes=[r_mod])
                for (A, rA, g, rg, which) in ((A1, r_A1, gmix, r_gmix, 1), (A2, r_A2, gffn, r_gffn, 4)):
                    kb.op("dve", lambda e, A=A, which=which, l=l: e.tensor_scalar(
                        out=A[:, l], in0=mod_sb[:, l, which * 16:(which + 1) * 16, :], scalar1=1.0, scalar2=None,
                        op0=ALU.add), reads=[r_mod], writes=[rA])
                    kb.op("dve", lambda e, A=A, g=g, l=l: e.tensor_tensor(
                        out=A[:, l], in0=A[:, l], in1=g[:, l].unsqueeze(2).broadcast_to([128, KC, 3]), op=ALU.mult),
                        reads=[rA, rg], writes=[rA])
            while emitters:
                emitters.pop(0)()
            kb.barrier()


    def modv(l, which, kc, row):
        return mod_sb[:, l, which * 16 + kc, row:row + 1]

    TILES_ALL = [(0, 256)] + [(256 + 512 * i, 512) for i in range(4)]
    TILES_LAT = TILES_ALL[1:]

    class WStream:
        def __init__(self, st, name, kcn, ncols, nstage=2, nbf=2, engs=("act", "dve")):
            self.kcn, self.ncols = kcn, ncols
            self.engs = engs
            self.stage = [kb.sb(st, "%s_s%d" % (name, i), [128, kcn, ncols], F32) for i in range(nstage)]
            self.bf = [kb.sb(st, "%s_b%d" % (name, i), [128, kcn, ncols], BF16) for i in range(nbf)]
            self.i = 0
            self.j = 0

        def load(self, W, c0, ncols=None, cast_eng=None, perm=False):
            ncols = ncols or self.ncols
            s_t, r_s = self.stage[self.i % len(self.stage)]
            self.i += 1
            b_t, r_b = self.bf[self.j % len(self.bf)]
            self.j += 1
            src = W.rearrange("(kc p) n -> p kc n", p=128)[:, :, c0:c0 + ncols]
            kb.dma("sp", s_t[:, :, 0:ncols], src, writes=[r_s])
            if cast_eng is None:
                cast_eng = self.engs[self.j % len(self.engs)]
            if not perm:
                if cast_eng == "act":
                    kb.op("act", lambda e: e.activation(out=b_t[:, :, 0:ncols], in_=s_t[:, :, 0:ncols], func=AF.Copy),
                          reads=[r_s], writes=[r_b])
                else:
                    kb.op(cast_eng, lambda e: e.tensor_copy(out=b_t[:, :, 0:ncols], in_=s_t[:, :, 0:ncols]),
                          reads=[r_s], writes=[r_b])
            else:
                for g64 in range(ncols // 64):
                    for h in range(2):
                        o0_ = g64 * 64 + h * 32
                        i0_ = g64 * 64 + (1 - h) * 32
                        kb.op("dve", lambda e, o0_=o0_, i0_=i0_: e.tensor_copy(out=b_t[:, :, o0_:o0_ + 32], in_=s_t[:, :, i0_:i0_ + 32]),
                              reads=[r_s], writes=[r_b])
            return b_t, r_b

    def norm_stage(st_name, src, tiles, Afn, SHfn, hT, r_hT):
        with ExitStack() as st:
            xt = [kb.sb(st, "%s_x%d" % (st_name, i), [128, KC, 512], F32) for i in range(2)]
            sq = [kb.sb(st, "%s_q%d" % (st_name, i), [128, KC, 512], BF16) for i in range(1)]
            rt = [kb.sb(st, "%s_r%d" % (st_name, i), [128, 512], F32) for i in range(2)]
            srcv = src.rearrange("(kc p) t -> p kc t", p=128)
            for ti, (t0, n) in enumerate(tiles):
                is_ctx = t0 < CTX
                x_t, r_x = xt[ti % 2]
                q_t, r_q = sq[0]
                r_t, r_r = rt[ti % 2]
                p_t, r_p = ps[ti % 2]
                kb.dma("sp", x_t[:, :, 0:n], srcv[:, :, t0:t0 + n], writes=[r_x])
                kb.op("act", lambda e: e.activation(out=q_t[:, :, 0:n], in_=x_t[:, :, 0:n], func=AF.Square),
                      reads=[r_x], writes=[r_q])
                for kc in range(KC):
                    kb.op("pe", lambda e, kc=kc: e.matmul(p_t[:, 0:n], lhsT=ones_b[:], rhs=q_t[:, kc, 0:n],
                                                         start=(kc == 0), stop=(kc == KC - 1)),
                          reads=[r_q, r_ones], writes=[r_p])
                kb.op("dve", lambda e: e.tensor_scalar(out=r_t[:, 0:n], in0=p_t[:, 0:n], scalar1=1.0 / D,
                                                       scalar2=EPS, op0=ALU.mult, op1=ALU.add),
                      reads=[r_p], writes=[r_r])
                kb.op("act", lambda e: e.activation(out=r_t[:, 0:n], in_=r_t[:, 0:n], func=AF.Sqrt),
                      reads=[r_r], writes=[r_r])
                kb.op("dve", lambda e: e.reciprocal(out=r_t[:, 0:n], in_=r_t[:, 0:n]), reads=[r_r], writes=[r_r])
                kb.op("dve", lambda e: e.tensor_tensor(
                    out=x_t[:, :, 0:n], in0=x_t[:, :, 0:n],
                    in1=r_t[:, 0:n].unsqueeze(1).broadcast_to([128, KC, n]), op=ALU.mult),
                    reads=[r_x, r_r], writes=[r_x])
                for kc in range(KC):
                    a_ap = Afn(kc, is_ctx)
                    s_ap = SHfn(kc, is_ctx)
                    kb.op("act", lambda e, kc=kc, a_ap=a_ap, s_ap=s_ap: e.activation(
                        out=hT[:, kc, t0:t0 + n], in_=x_t[:, kc, 0:n], func=AF.Identity, scale=a_ap,
                        bias=(s_ap if s_ap is not None else zero_c[:])),
                        reads=[r_x, r_A1, r_A2, r_mod, r_gfin, r_zero], writes=[r_hT])
            kb.barrier()

    def linear_resid(st, name, b, act, r_act, kcn, W, tiles, gatefn, ws, tspan):
        tlo, thi = tspan
        nt = thi - tlo
        xr = [kb.sb(st, "%s_xr%d" % (name, i), [128, nt], F32) for i in range(2)]
        nxt = ws.load(W, 0, 128)
        for blk in range(KC):
            w_b, r_w = nxt
            if blk + 1 < KC:
                nxt = ws.load(W, (blk + 1) * 128, 128)
            x_r, r_xr = xr[blk % 2]
            kb.dma("sp", x_r[:], XT[b, blk * 128:(blk + 1) * 128, tlo:thi], writes=[r_xr])
            for ti, (t0, n) in enumerate(tiles):
                p_t, r_p = ps[(blk * len(tiles) + ti) % 4]
                for kc in range(kcn):
                    kb.op("pe", lambda e, kc=kc, p_t=p_t, w_b=w_b, t0=t0, n=n: e.matmul(
                        p_t[:, 0:n], lhsT=w_b[:, kc, 0:128], rhs=act[:, kc, t0 - tlo:t0 - tlo + n],
                        start=(kc == 0), stop=(kc == kcn - 1)), reads=[r_w, r_act], writes=[r_p])
                is_ctx = t0 < CTX
                g_ap = gatefn(blk, is_ctx)
                kb.op("dve", lambda e, p_t=p_t, x_r=x_r, t0=t0, n=n, g_ap=g_ap: e.scalar_tensor_tensor(
                    out=x_r[:, t0 - tlo:t0 - tlo + n], in0=p_t[:, 0:n], scalar=g_ap,
                    in1=x_r[:, t0 - tlo:t0 - tlo + n], op0=ALU.mult, op1=ALU.add),
                    reads=[r_p, r_xr, r_mod], writes=[r_xr])
            kb.dma("act", XT[b, blk * 128:(blk + 1) * 128, tlo:thi], x_r[:], reads=[r_xr])

    def ffn_up(l, b, hT, r_hT, with_ctx):
        tiles = TILES_ALL if with_ctx else TILES_LAT
        segs = ([(0, CTX)] if with_ctx else []) + [(CTX, T)]
        tlo = 0 if with_ctx else CTX
        with ExitStack() as st:
            ws = WStream(st, "wup", KC, 128, nstage=2, nbf=4, engs=("act",))
            ub = [kb.sb(st, "ub%d" % i, [128, T], F32) for i in range(2)]
            cb = [kb.sb(st, "cb%d" % i, [128, T], F32) for i in range(2)]
            sg, r_sg = kb.sb(st, "sg", [128, T], F32)
            go = [kb.sb(st, "go%d" % i, [128, T], BF16) for i in range(2)]
            npair = FF // 128
            pidx = 0
            nxt = (ws.load(w_up[l], 0, 128), ws.load(w_up[l], FF, 128))
            for j in range(npair):
                (wg, r_wg), (wv, r_wv) = nxt
                if j + 1 < npair:
                    nxt = (ws.load(w_up[l], (j + 1) * 128, 128), ws.load(w_up[l], FF + (j + 1) * 128, 128))
                for jj in range(1):
                    cres = []
                    for hi, (w_b, r_w) in enumerate(((wg, r_wg), (wv, r_wv))):
                        fcol = j if hi == 0 else 44 + j
                        u_t, r_u = ub[hi]
                        c_t, r_c = cb[hi]
                        for ti, (t0, n) in enumerate(tiles):
                            p_t, r_p = ps[(ti + hi * len(tiles) + pidx) % 6]
                            for kc in range(KC):
                                kb.op("pe", lambda e, kc=kc, p_t=p_t, w_b=w_b, t0=t0, n=n, jj=jj: e.matmul(
                                    p_t[:, 0:n], lhsT=w_b[:, kc, jj * 128:(jj + 1) * 128], rhs=hT[:, kc, t0:t0 + n],
                                    start=(kc == 0), stop=(kc == KC - 1)), reads=[r_w, r_hT], writes=[r_p])
                            kb.op("act", lambda e, p_t=p_t, u_t=u_t, t0=t0, n=n: e.activation(
                                out=u_t[:, t0:t0 + n], in_=p_t[:, 0:n], func=AF.Copy), reads=[r_p], writes=[r_u])
                        kb.op("dve", lambda e, u_t=u_t, c_t=c_t, fcol=fcol: e.tensor_scalar(
                            out=c_t[:, tlo:T], in0=u_t[:, tlo:T], scalar1=dww[:, l, 1, fcol:fcol + 1],
                            scalar2=dwb[:, l, fcol:fcol + 1], op0=ALU.mult, op1=ALU.add),
                            reads=[r_u, r_dww, r_dwb], writes=[r_c])
                        for (s0, s1) in segs:
                            kb.op("dve", lambda e, u_t=u_t, c_t=c_t, fcol=fcol, s0=s0, s1=s1: e.scalar_tensor_tensor(
                                out=c_t[:, s0 + 1:s1], in0=u_t[:, s0:s1 - 1], scalar=dww[:, l, 0, fcol:fcol + 1],
                                in1=c_t[:, s0 + 1:s1], op0=ALU.mult, op1=ALU.add),
                                reads=[r_u, r_c, r_dww], writes=[r_c])
                            kb.op("dve", lambda e, u_t=u_t, c_t=c_t, fcol=fcol, s0=s0, s1=s1: e.scalar_tensor_tensor(
                                out=c_t[:, s0:s1 - 1], in0=u_t[:, s0 + 1:s1], scalar=dww[:, l, 2, fcol:fcol + 1],
                                in1=c_t[:, s0:s1 - 1], op0=ALU.mult, op1=ALU.add),
                                reads=[r_u, r_c, r_dww], writes=[r_c])
                        cres.append((c_t, r_c))
                    (cg, r_cg), (cv, r_cv) = cres
                    g_t, r_g = go[pidx % 2]
                    kb.op("act", lambda e, cg=cg: e.activation(out=sg[:, tlo:T], in_=cg[:, tlo:T], func=AF.Silu),
                          reads=[r_cg], writes=[r_sg])
                    kb.op("pool", lambda e, cv=cv, g_t=g_t: e.tensor_tensor(
                        out=g_t[:, tlo:T], in0=sg[:, tlo:T], in1=cv[:, tlo:T], op=ALU.mult),
                        reads=[r_sg, r_cv], writes=[r_g])
                    kb.dma("act", GT[b, j * 128:(j + 1) * 128, tlo:T], g_t[:, tlo:T], reads=[r_g])
                    pidx += 1
            kb.barrier()

    def ffn_down(l, b, with_ctx):
        tlo0 = 0 if with_ctx else CTX
        groups = [(tlo0, CTX + 1024), (CTX + 1024, T)]
        with ExitStack() as st:
            ws = WStream(st, "wdn", 44, 128, nstage=2, nbf=2)
            gsb, r_gsb = kb.sb(st, "gsb", [128, 44, 1280], BF16)
            for gi, (glo, ghi) in enumerate(groups):
                ng = ghi - glo
                kb.dma("sp", gsb[:, :, 0:ng], GT[b].rearrange("(kc p) t -> p kc t", p=128)[:, :, glo:ghi],
                       writes=[r_gsb])
                tiles = []
                t = glo
                while t < ghi:
                    n = min(512, ghi - t)
                    if t < CTX:
                        n = min(n, CTX - t)
                    tiles.append((t, n))
                    t += n
                with ExitStack() as st2:
                    linear_resid(st2, "dn%d" % gi, b, gsb, r_gsb, 44, w_down[l], tiles,
                                 lambda blk, is_ctx: modv(l, 5, blk, 2 if is_ctx else b), ws, (glo, ghi))
                    kb.barrier()

    ev_w_in = din("ev_w_in", [2, D, 4096])
    od_w_in = din("od_w_in", [2, D, 4608])
    w_glu = din("w_glu", [2, 1024, 1024])
    ropeC_in = din("ropeC", [128, LAT])
    ropeS_in = din("ropeS", [128, LAT])
    permM_in = din("permM", [128, 128])
    mprev_in = din("mprev", [128, 128])
    mnext_in = din("mnext", [128, 128])
    dlam_in = din("dlam", [2, 512])
    subg_in = din("subg", [2, 256])
    sink_in = din("sink", [2, 8])
    natb_in = din("natb", [2, 64, 8 * 15 * 64])
    namask_in = din("namask", [64, 64])
    lamA_re_in = din("lamA_re", [2, 128, 1024])
    lamA_im_in = din("lamA_im", [2, 128, 1024])
    dtA_in = din("dtA", [2, 128, 1024])
    bTre_in = din("bTre", [2, 128, 1024])
    bTim_in = din("bTim", [2, 128, 1024])
    lamB_re_in = din("lamB_re", [2, 128, 64])
    lamB_im_in = din("lamB_im", [2, 128, 64])
    dtB_in = din("dtB", [2, 128, 64])
    LC_in = din("LC", [2, 64, 128, 256])
    maskA_in = din("maskA", [128, 8])
    s5d_in = din("s5d", [128, 2, 8])
    ZT = dscr("ZT", [2, 36 * 128, T], BF16)
    VT = dscr("VT", [2, T, 1280], BF16)
    GAT = dscr("GAT", [2, 1024, T], BF16)
    ETAB = dscr("ETAB", [2, 64, 2, 128, T], BF16)
    SCALE = 128.0 ** -0.5
    psn = [0]

    def nps(lo=0, hi=8):
        psn[0] += 1
        return ps[lo + psn[0] % (hi - lo)]

    def proj_stage(b, W, fm_blocks, tm_groups, hT, r_hT, rope):
        with ExitStack() as st:
            ws = WStream(st, "wi", KC, 256, nstage=2, nbf=4)
            rows = [kb.sb(st, "zrow%d" % i, [128, T], BF16) for i in range(2)]
            vrow = [kb.sb(st, "vrow%d" % i, [128, 256], BF16) for i in range(2)]
            if rope:
                rC, r_rC = kb.sb(st, "ropeC", [128, LAT], F32)
                rS, r_rS = kb.sb(st, "ropeS", [128, LAT], F32)
                kb.dma("sp", rC[:], ropeC_in, writes=[r_rC])
                kb.dma("sp", rS[:], ropeS_in, writes=[r_rS])
                t1 = [kb.sb(st, "rt1_%d" % i, [128, 512], F32) for i in range(3)]
                t2 = [kb.sb(st, "rt2_%d" % i, [128, 512], F32) for i in range(2)]
                qbs = [kb.sb(st, "rqb_%d" % i, [128, 512], BF16) for i in range(3)]
                pm_f, r_pmf = kb.sb(st, "permM_f", [128, 128], F32)
                pm_b, r_pmb = kb.sb(st, "permM_b", [128, 128], BF16)
                kb.dma("sp", pm_f[:], permM_in, writes=[r_pmf])
                kb.op("dve", lambda e: e.tensor_copy(out=pm_b[:], in_=pm_f[:]), reads=[r_pmf], writes=[r_pmb])
            pending = []
            stores = []
            rcnt = [0]

            def ldblk(bi_):
                c0_, _, rp_ = fm_blocks[bi_]
                a_ = ws.load(W, c0_, 128)
                return a_, None
            nxt = ldblk(0)
            for bi, (c0, zblk, do_rope) in enumerate(fm_blocks):
                (w_b, r_w), wp_ = nxt
                if bi + 1 < len(fm_blocks):
                    nxt = ldblk(bi + 1)
                row, r_row = rows[bi % 2]
                for ti, (t0, n) in enumerate(TILES_ALL):
                    p_t, r_p = nps(0, 6)
                    for kc in range(KC):
                        kb.op("pe", lambda e, kc=kc: e.matmul(p_t[:, 0:n], lhsT=w_b[:, kc, 0:128], rhs=hT[:, kc, t0:t0 + n],
                                                             start=(kc == 0), stop=(kc == KC - 1)),
                              reads=[r_w, r_hT], writes=[r_p])
                    while pending:
                        pending.pop(0)()
                    while stores:
                        zb_, rw_, r_rw_ = stores.pop(0)
                        kb.dma("act", ZT[b, zb_ * 128:(zb_ + 1) * 128, :], rw_[:], reads=[r_rw_])
                    if do_rope and t0 >= CTX:
                        rc_ = rcnt[0]
                        rcnt[0] += 1
                        a_t, r_a = t1[rc_ % 3]
                        q_b, r_qb = qbs[rc_ % 3]
                        kb.op("act", lambda e: e.activation(out=q_b[:, 0:n], in_=p_t[:, 0:n], func=AF.Copy), reads=[r_p], writes=[r_qb])
                        kb.op("dve", lambda e: e.tensor_tensor(out=a_t[:, 0:n], in0=p_t[:, 0:n], in1=rC[:, t0 - CTX:t0 - CTX + n],
                                                               op=ALU.mult), reads=[r_p, r_rC, r_qb], writes=[r_a])

                        def fin(a_t=a_t, r_a=r_a, q_b=q_b, r_qb=r_qb, t0=t0, n=n, row=row, r_row=r_row, rc_=rc_):
                            p2, r_p2 = ps[6 + rc_ % 2]
                            b_t, r_b = t2[rc_ % 2]
                            kb.op("pe", lambda e: e.matmul(p2[:, 0:n], lhsT=pm_b[:], rhs=q_b[:, 0:n], start=True, stop=True),
                                  reads=[r_pmb, r_qb], writes=[r_p2])
                            kb.op("dve", lambda e: e.tensor_tensor(out=b_t[:, 0:n], in0=p2[:, 0:n], in1=rS[:, t0 - CTX:t0 - CTX + n],
                                                                   op=ALU.mult), reads=[r_p2, r_rS], writes=[r_b])
                            kb.op("pool", lambda e: e.tensor_tensor(out=row[:, t0:t0 + n], in0=a_t[:, 0:n], in1=b_t[:, 0:n],
                                                                    op=ALU.add), reads=[r_a, r_b], writes=[r_row])
                        pending.append(fin)
                        if not DEFER_ROPE:
                            while pending:
                                pending.pop(0)()
                    else:
                        kb.op("act", lambda e: e.activation(out=row[:, t0:t0 + n], in_=p_t[:, 0:n], func=AF.Copy),
                              reads=[r_p], writes=[r_row])
                if bi + 1 == len(fm_blocks):
                    while pending:
                        pending.pop(0)()
                    kb.dma("act", ZT[b, zblk * 128:(zblk + 1) * 128, :], row[:], reads=[r_row])
                else:
                    stores.append((zblk, row, r_row))
            vi = 0
            for (c0, ncols, vc0) in tm_groups:
                for cc in range(0, ncols, 256):
                    w_b, r_w = ws.load(W, c0 + cc, 256)
                    for tt in range(T // 128):
                        p_t, r_p = nps(0, 6)
                        for kc in range(KC):
                            kb.op("pe", lambda e, kc=kc: e.matmul(p_t[:, 0:256], lhsT=hT[:, kc, tt * 128:(tt + 1) * 128],
                                                                 rhs=w_b[:, kc, 0:256], start=(kc == 0), stop=(kc == KC - 1)),
                                  reads=[r_w, r_hT], writes=[r_p])
                        v_t, r_v = vrow[vi % 2]
                        vi += 1
                        kb.op("act", lambda e: e.activation(out=v_t[:], in_=p_t[:, 0:256], func=AF.Copy),
                              reads=[r_p], writes=[r_v])
                        kb.dma("act", VT[b, tt * 128:(tt + 1) * 128, vc0 + cc:vc0 + cc + 256], v_t[:], reads=[r_v])
            kb.barrier()

    def load_v(st, name, b, vc0, dv):
        v, r_v = kb.sb(st, name, [128, T // 128, dv + 1], BF16)
        kb.op("pool", lambda e: e.memset(v[:, :, dv:dv + 1], 1.0), writes=[r_v])
        kb.dma("sp", v[:, :, 0:dv], VT[b].rearrange("(n p) c -> p n c", p=128)[:, :, vc0:vc0 + dv], writes=[r_v])
        return v, r_v

    def finish_o(o_ps, r_o, dv, dst, r_dst, extra=None, r_extra=None, scr=None):
        rc, r_rc = scr
        if extra is not None:
            kb.op("dve", lambda e: e.tensor_tensor(out=rc[:], in0=o_ps[:, dv:dv + 1], in1=extra, op=ALU.add),
                  reads=[r_o, r_extra], writes=[r_rc])
            kb.op("dve", lambda e: e.reciprocal(out=rc[:], in_=rc[:]), reads=[r_rc], writes=[r_rc])
        else:
            kb.op("dve", lambda e: e.reciprocal(out=rc[:], in_=o_ps[:, dv:dv + 1]), reads=[r_o], writes=[r_rc])
        kb.op("dve", lambda e: e.tensor_scalar(out=dst, in0=o_ps[:, 0:dv], scalar1=rc[:, 0:1], scalar2=None, op0=ALU.mult),
              reads=[r_o, r_rc], writes=[r_dst])

    def transpose_out(src_bf, r_src, nq, yrow, r_yrow, col0, psb):
        p_t, r_p = psb
        pv = p_t[:].bitcast(BF16)
        kb.op("pe", lambda e: e.transpose(out=pv[:, 0:nq], in_=src_bf, identity=ident_b[0:nq, 0:nq]),
              reads=[r_src, r_ident], writes=[r_p])
        kb.op("act", lambda e: e.activation(out=yrow[:, col0:col0 + nq], in_=pv[:, 0:nq], func=AF.Copy),
              reads=[r_p], writes=[r_yrow])

    def odd_core(l, b, with_ctx):
        i = l // 2
        lam_init = 0.8 - 0.6 * math.exp(-0.3 * l)
        with ExitStack() as st:
            dl, r_dl = kb.sb(st, "dl", [128, 512], F32)
            sg, r_sg = kb.sb(st, "subg", [128, 256], F32)
            sk, r_sk = kb.sb(st, "sink", [128, 8], F32)
            lamt, r_lam = kb.sb(st, "lamt", [128, 4], F32)
            kb.dma("sp", dl[:], dlam_in[i:i + 1, :].partition_broadcast(128), writes=[r_dl])
            kb.dma("sp", sg[:], subg_in[i:i + 1, :].partition_broadcast(128), writes=[r_sg])
            kb.dma("sp", sk[:], sink_in[i:i + 1, :].partition_broadcast(128), writes=[r_sk])
            kb.op("act", lambda e: e.activation(out=sk[:], in_=sk[:], func=AF.Exp), reads=[r_sk], writes=[r_sk])
            kb.op("dve", lambda e: e.tensor_scalar(out=sg[:], in0=sg[:], scalar1=1.0 - lam_init, scalar2=None, op0=ALU.mult),
                  reads=[r_sg], writes=[r_sg])
            dlv = dl[:].rearrange("p (a d) -> p a d", d=128)
            for k in range(2):
                kb.op("dve", lambda e, k=k: e.tensor_tensor(out=dlv[:, 2 * k, :], in0=dlv[:, 2 * k, :], in1=dlv[:, 2 * k + 1, :],
                                                           op=ALU.mult), reads=[r_dl], writes=[r_dl])
                kb.op("act", lambda e, k=k: e.activation(out=dlv[:, 2 * k + 1, :], in_=dlv[:, 2 * k, :], func=AF.Identity,
                                                        bias=zero_c[:], accum_out=lamt[:, k:k + 1]),
                      reads=[r_dl, r_zero], writes=[r_dl, r_lam])
            kb.op("act", lambda e: e.activation(out=lamt[:, 0:2], in_=lamt[:, 0:2], func=AF.Exp), reads=[r_lam], writes=[r_lam])
            kb.op("dve", lambda e: e.tensor_tensor(out=lamt[:, 2:3], in0=lamt[:, 1:2], in1=lamt[:, 0:1], op=ALU.subtract),
                  reads=[r_lam], writes=[r_lam])
            kb.op("dve", lambda e: e.tensor_scalar(out=lamt[:, 2:3], in0=lamt[:, 2:3], scalar1=-lam_init, scalar2=None,
                                                   op0=ALU.add), reads=[r_lam], writes=[r_lam])
            neglam = lamt[:, 2:3]

            mp_f, r_mpf = kb.sb(st, "mp_f", [128, 2, 128], F32)
            mk, r_mk = kb.sb(st, "mk", [128, 2, 128], BF16)
            kb.dma("sp", mp_f[:, 0, :], mprev_in, writes=[r_mpf])
            kb.dma("sp", mp_f[:, 1, :], mnext_in, writes=[r_mpf])
            kb.op("dve", lambda e: e.tensor_copy(out=mk[:], in_=mp_f[:]), reads=[r_mpf], writes=[r_mk])

            rcs = [kb.sb(st, "rc%d" % j, [128, 1], F32) for j in range(4)]
            ssq = [kb.sb(st, "ssq%d" % j, [128, 1], F32) for j in range(2)]
            pts = [kb.sb(st, "pt%d" % j, [128, 512], BF16) for j in range(4)]
            with ExitStack() as s2:
                kt = [kb.sb(s2, "kt%d" % c, [128, T], BF16) for c in range(2)]
                qt = [kb.sb(s2, "qt%d" % c, [128, T], BF16) for c in range(2)]
                yrow = [kb.sb(s2, "yr%d" % j, [128, T], BF16) for j in range(2)]
                o0, r_o0 = kb.sb(s2, "o0", [128, 4, 256], F32)
                o1qs = [kb.sb(s2, "o1q%d" % j, [128, 4, 256], F32) for j in range(2)]
                ybqs = [kb.sb(s2, "ybq%d" % j, [128, 4, 256], BF16) for j in range(2)]
                sq4s = [kb.sb(s2, "sq4%d" % j, [128, 4], F32) for j in range(2)]
                dq_pending = []
                gcount = [0]
                junk, r_junk = kb.sb(s2, "junk", [128, 256], F32)
                qgroups = ([(0, 256, 2)] if with_ctx else []) + [(CTX + 512 * g, 512, 18) for g in range(4)]
                for h in range(4):
                    v, r_v = load_v(s2, "vC", b, 256 * h, 256) if h == 0 else (v, r_v)
                    if h > 0:
                        kb.dma("sp", v[:, :, 0:256], VT[b].rearrange("(n p) c -> p n c", p=128)[:, :, 256 * h:256 * h + 256],
                               writes=[r_v])
                    for c in range(2):
                        kb.dma("sp", kt[c][0][:], ZT[b, (8 + 2 * h + c) * 128:(9 + 2 * h + c) * 128, :], writes=[kt[c][1]])
                        kb.dma("sp", qt[c][0][:], ZT[b, (2 * h + c) * 128:(2 * h + c + 1) * 128, :], writes=[qt[c][1]])
                    if not with_ctx:
                        for j in range(2):
                            kb.op("pool", lambda e, j=j: e.memset(yrow[j][0][:, 0:CTX], 0.0), writes=[yrow[j][1]])
                    for (q0, nq, nkb) in qgroups:
                        nqs = nq // 128
                        gcount[0] += 1
                        (o1q, r_o1q), (ybq, r_ybq), (sq4, r_sq4) = o1qs[gcount[0] % 2], ybqs[gcount[0] % 2], sq4s[gcount[0] % 2]
                        for c in range(2):
                            obanks = [ps[2 + j] for j in range(nqs)]
                            sbanks = (ps[0], ps[1], ps[6])

                            def s_mm(kk):
                                s_t_, r_s_ = sbanks[kk % 3]
                                kb.op("pe", lambda e: e.matmul(s_t_[:, 0:nq], lhsT=kt[c][0][:, kk * 128:(kk + 1) * 128],
                                                               rhs=qt[c][0][:, q0:q0 + nq], start=True, stop=True),
                                      reads=[kt[c][1], qt[c][1]], writes=[r_s_])
                            s_mm(0)
                            if nkb > 1:
                                s_mm(1)
                            for kbi in range(nkb):
                                s_t, r_s = sbanks[kbi % 3]
                                p_t, r_p = pts[kbi % 4]
                                kb.op("act", lambda e: e.activation(out=p_t[:, 0:nq], in_=s_t[:, 0:nq], func=AF.Exp, scale=SCALE),
                                      reads=[r_s], writes=[r_p])
                                if kbi + 2 < nkb:
                                    s_mm(kbi + 2)
                                if kbi == 1 and c == 0:
                                    while dq_pending:
                                        dq_pending.pop(0)()
                                for qs in range(nqs):
                                    o_t, r_o = obanks[qs]
                                    kb.op("pe", lambda e, qs=qs, o_t=o_t: e.matmul(
                                        o_t[:, 0:257], lhsT=p_t[:, qs * 128:(qs + 1) * 128], rhs=v[:, kbi, 0:257],
                                        start=(kbi == 0), stop=(kbi == nkb - 1)), reads=[r_p, r_v], writes=[r_o])
                            if c == 0:
                                for qs in range(nqs):
                                    o_t, r_o = obanks[qs]
                                    finish_o(o_t, r_o, 256, o0[:, qs, :], r_o0, scr=rcs[qs % 4])
                            else:
                                for qs in range(nqs):
                                    o_t, r_o = obanks[qs]
                                    finish_o(o_t, r_o, 256, o1q[:, qs, :], r_o1q, scr=rcs[qs % 4])
                                kb.op("dve", lambda e: e.scalar_tensor_tensor(out=o1q[:, 0:nqs, :], in0=o1q[:, 0:nqs, :], scalar=neglam,
                                                                              in1=o0[:, 0:nqs, :], op0=ALU.mult, op1=ALU.add),
                                      reads=[r_o1q, r_o0, r_lam], writes=[r_o1q])
                                for qs in range(nqs):
                                    kb.op("act", lambda e, qs=qs: e.activation(out=junk[:], in_=o1q[:, qs, :], func=AF.Square,
                                                                              accum_out=sq4[:, qs:qs + 1]),
                                          reads=[r_o1q], writes=[r_junk, r_sq4])
                                def epi_tail(nqs=nqs, q0=q0, o1q=o1q, r_o1q=r_o1q, ybq=ybq, r_ybq=r_ybq, sq4=sq4, r_sq4=r_sq4):
                                  kb.op("dve", lambda e: e.tensor_scalar(out=sq4[:, 0:nqs], in0=sq4[:, 0:nqs], scalar1=1.0 / 256, scalar2=EPS,
                                                                       op0=ALU.mult, op1=ALU.add), reads=[r_sq4], writes=[r_sq4])
                                  kb.op("act", lambda e: e.activation(out=sq4[:, 0:nqs], in_=sq4[:, 0:nqs], func=AF.Sqrt), reads=[r_sq4], writes=[r_sq4])
                                  kb.op("dve", lambda e: e.reciprocal(out=sq4[:, 0:nqs], in_=sq4[:, 0:nqs]), reads=[r_sq4], writes=[r_sq4])
                                  for qs in range(nqs):
                                      kb.op("dve", lambda e, qs=qs: e.scalar_tensor_tensor(out=ybq[:, qs, :], in0=o1q[:, qs, :], scalar=sq4[:, qs:qs + 1],
                                                                                          in1=sg[:], op0=ALU.mult, op1=ALU.mult),
                                            reads=[r_o1q, r_sq4, r_sg], writes=[r_ybq])
                                  for qs in range(nqs):
                                      for j in range(2):
                                          transpose_out(ybq[:, qs, j * 128:(j + 1) * 128], r_ybq, 128, yrow[j][0], yrow[j][1],
                                                        q0 + qs * 128, ps[7])
                                dq_pending.append(epi_tail)
                    while dq_pending:
                        dq_pending.pop(0)()
                    for j in range(2):
                        kb.dma("act", YT[b, (2 * h + j) * 128:(2 * h + j + 1) * 128, :], yrow[j][0][:], reads=[yrow[j][1]])
                kb.barrier()
            kb.mark("  swa")
            with ExitStack() as s2:
                ktd, r_ktd = kb.sb(s2, "ktd", [128, T], BF16)
                qtd, r_qtd = kb.sb(s2, "qtd", [128, 4, T], BF16)
                yrd = [kb.sb(s2, "yrd%d" % j, [128, T], BF16) for j in range(4)]
                ofl = [kb.sb(s2, "ofl%d" % j, [128, 128], F32) for j in range(2)]
                obf = [kb.sb(s2, "obf%d" % j, [128, 128], BF16) for j in range(2)]
                vd = None
                cnt = 0
                for g in range(2):
                    if vd is None:
                        vd, r_vd = load_v(s2, "vD", b, 1024 + 128 * g, 128)
                    else:
                        kb.dma("sp", vd[:, :, 0:128], VT[b].rearrange("(n p) c -> p n c", p=128)[:, :, 1024 + 128 * g:1152 + 128 * g],
                               writes=[r_vd])
                    kb.dma("sp", ktd[:], ZT[b, (32 + g) * 128:(33 + g) * 128, :], writes=[r_ktd])
                    for j in range(4):
                        kb.dma("sp", qtd[:, j, :], ZT[b, (24 + 4 * g + j) * 128:(25 + 4 * g + j) * 128, :], writes=[r_qtd])
                    if not with_ctx:
                        for j in range(4):
                            kb.op("pool", lambda e, j=j: e.memset(yrd[j][0][:, 0:CTX], 0.0), writes=[yrd[j][1]])
                    qblocks = ([(-2,), (-1,)] if with_ctx else []) + [(n,) for n in range(16)]
                    for (n,) in qblocks:
                        if n < 0:
                            q0 = (n + 2) * 128
                            kbl = [(0, None), (1, None)]
                        else:
                            q0 = CTX + 128 * n
                            kbl = [(0, None), (1, None)]
                            if n > 0:
                                kbl.append((2 + n - 1, 0))
                            kbl.append((2 + n, None))
                            if n < 15:
                                kbl.append((2 + n + 1, 1))
                        def sw_mm(kk):
                            s_t_, r_s_ = ps[kk % 2]
                            kbi_ = kbl[kk][0]
                            for j in range(4):
                                kb.op("pe", lambda e, j=j: e.matmul(s_t_[:, j * 128:(j + 1) * 128],
                                                                   lhsT=ktd[:, kbi_ * 128:(kbi_ + 1) * 128], rhs=qtd[:, j, q0:q0 + 128],
                                                                   start=True, stop=True), reads=[r_ktd, r_qtd], writes=[r_s_])
                        sw_mm(0)
                        for ki, (kbi, msk) in enumerate(kbl):
                            s_t, r_s = ps[ki % 2]
                            p_t, r_p = pts[ki % 3]
                            kb.op("act", lambda e: e.activation(out=p_t[:], in_=s_t[:], func=AF.Exp, scale=SCALE),
                                  reads=[r_s], writes=[r_p])
                            if msk is not None:
                                pv = p_t[:].rearrange("p (j q) -> p j q", q=128)
                                kb.op("dve", lambda e: e.tensor_tensor(out=pv, in0=pv, in1=mk[:, msk, :].unsqueeze(1).broadcast_to([128, 4, 128]),
                                                                       op=ALU.mult), reads=[r_p, r_mk], writes=[r_p])
                            if ki + 1 < len(kbl):
                                sw_mm(ki + 1)
                            for j in range(4):
                                o_t, r_o = ps[2 + j]
                                kb.op("pe", lambda e, j=j, o_t=o_t: e.matmul(o_t[:, 0:129], lhsT=p_t[:, j * 128:(j + 1) * 128],
                                                                            rhs=vd[:, kbi, 0:129], start=(ki == 0),
                                                                            stop=(ki == len(kbl) - 1)), reads=[r_p, r_vd], writes=[r_o])
                        for j in range(4):
                            o_t, r_o = ps[2 + j]
                            of, r_of = ofl[cnt % 2]
                            ob, r_ob = obf[cnt % 2]
                            finish_o(o_t, r_o, 128, ob[:], r_ob, extra=sk[:, 4 * g + j:4 * g + j + 1], r_extra=r_sk, scr=rcs[cnt % 4])
                            transpose_out(ob[:], r_ob, 128, yrd[j][0], yrd[j][1], q0, ps[6 + cnt % 2])
                            cnt += 1
                    for j in range(4):
                        kb.dma("act", YT[b, (8 + 4 * g + j) * 128:(9 + 4 * g + j) * 128, :], yrd[j][0][:], reads=[yrd[j][1]])
                kb.barrier()

    def odd_proj(l, b, hT, r_hT):
        i = l // 2
        fmb = [(128 * k, k, True) for k in range(16)]
        fmb += [(3072 + 128 * k, 24 + k, True) for k in range(8)]
        fmb += [(4096 + 128 * k, 32 + k, True) for k in range(2)]
        tmg = [(2048, 1024, 0), (4352, 256, 1024)]
        proj_stage(b, od_w_in[i], fmb, tmg, hT, r_hT, rope=True)
    def even_proj(l, b, hT, r_hT):
        i = l // 2
        fmb = [(128 * k, k, False) for k in range(24)]
        tmg = [(3072, 1024, 0)]
        proj_stage(b, ev_w_in[i], fmb, tmg, hT, r_hT, rope=False)

    def na_core(l, b, with_ctx):
        i = l // 2
        with ExitStack() as st:
            tb, r_tb = kb.sb(st, "natb", [64, 120, 64], F32)
            nm, r_nm = kb.sb(st, "namask", [64, 64], F32)
            kb.dma("sp", tb[:], natb_in[i].rearrange("k (a q) -> k a q", q=64), writes=[r_tb])
            kb.dma("sp", nm[:], namask_in, writes=[r_nm])
            kb.op("dve", lambda e: e.tensor_tensor(out=tb[:], in0=tb[:], in1=nm[:].unsqueeze(1).broadcast_to([64, 120, 64]),
                                                   op=ALU.add), reads=[r_tb, r_nm], writes=[r_tb])
            kt, r_kt = kb.sb(st, "nkt", [128, T], BF16)
            qt, r_qt = kb.sb(st, "nqt", [128, T], BF16)
            v64, r_v64 = kb.sb(st, "v64", [64, 36, 129], BF16)
            vc, r_vc = kb.sb(st, "vc128", [128, 2, 129], BF16)
            yrow, r_yrow = kb.sb(st, "nyrow", [128, T], BF16)
            sbf = [kb.sb(st, "nsb%d" % j, [64, 512], F32) for j in range(2)]
            ptl = [kb.sb(st, "nptl%d" % j, [64, 512], BF16) for j in range(2)]
            ptc = [kb.sb(st, "nptc%d" % j, [128, 128], BF16) for j in range(2)]
            ofl = [kb.sb(st, "nof%d" % j, [128, 128], F32) for j in range(2)]
            obf = [kb.sb(st, "nob%d" % j, [128, 128], BF16) for j in range(2)]
            rcs = [kb.sb(st, "nrc%d" % j, [128, 1], F32) for j in range(2)]
            pccs = [kb.sb(st, "npcc%d" % j, [128, 256], BF16) for j in range(2)]
            kb.op("pool", lambda e: e.memset(v64[:, :, 128:129], 1.0), writes=[r_v64])
            kb.op("pool", lambda e: e.memset(vc[:, :, 128:129], 1.0), writes=[r_vc])
            cnt = 0
            for h in range(8):
                kb.dma("sp", kt[:], ZT[b, (16 + h) * 128:(17 + h) * 128, :], writes=[r_kt])
                kb.dma("sp", qt[:], ZT[b, (8 + h) * 128:(9 + h) * 128, :], writes=[r_qt])
                kb.dma("sp", v64[:, :, 0:128], VT[b].rearrange("(n p) c -> p n c", p=64)[:, :, 128 * h:128 * h + 128], writes=[r_v64])
                kb.dma("sp", vc[:, :, 0:128], VT[b, 0:CTX, :].rearrange("(n p) c -> p n c", p=128)[:, :, 128 * h:128 * h + 128],
                       writes=[r_vc])
                if with_ctx:
                    for qb in range(2):
                        s_t, r_s = ps[cnt % 2]
                        p_t, r_p = ptc[cnt % 2]
                        o_t, r_o = ps[2 + cnt % 2]
                        of, r_of = ofl[cnt % 2]
                        ob, r_ob = obf[cnt % 2]
                        for c in range(2):
                            kb.op("pe", lambda e, c=c: e.matmul(s_t[:, c * 128:(c + 1) * 128], lhsT=kt[:, c * 128:(c + 1) * 128],
                                                               rhs=qt[:, qb * 128:(qb + 1) * 128], start=True, stop=True),
                                  reads=[r_kt, r_qt], writes=[r_s])
                        pcc, r_pcc = pccs[cnt % 2]
                        kb.op("act", lambda e: e.activation(out=pcc[:], in_=s_t[:, 0:256], func=AF.Exp, scale=SCALE),
                              reads=[r_s], writes=[r_pcc])
                        for c in range(2):
                            kb.op("pe", lambda e, c=c: e.matmul(o_t[:, 0:129], lhsT=pcc[:, c * 128:(c + 1) * 128], rhs=vc[:, c, :],
                                                               start=(c == 0), stop=(c == 1)), reads=[r_pcc, r_vc], writes=[r_o])
                        finish_o(o_t, r_o, 128, ob[:], r_ob, scr=rcs[cnt % 2])
                        transpose_out(ob[:], r_ob, 128, yrow, r_yrow, qb * 128, ps[6 + cnt % 2])
                        cnt += 1
                else:
                    kb.op("pool", lambda e: e.memset(yrow[:, 0:CTX], 0.0), writes=[r_yrow])
                base = cnt

                def na_bufs(r):
                    c_ = base + r
                    return (ps[c_ % 2], ps[4 + c_ % 2], ps[2 + c_ % 2], sbf[c_ % 2], ptl[c_ % 2], ptc[c_ % 2],
                            obf[c_ % 2], rcs[c_ % 2], ps[6 + c_ % 2])

                def na_s1(r):
                    rs = min(max(r - 4, 0), 24)
                    off = rs - r + 7
                    q0 = CTX + 64 * r
                    (s_t, r_s), (s2_t, r_s2), _, (sb_t, r_sb), (pl, r_pl), (pc, r_pc), _, _, _ = na_bufs(r)
                    for j in range(8):
                        k0 = CTX + 64 * (rs + j)
                        kb.op("pe", lambda e, j=j, k0=k0: e.matmul(s_t[0:64, j * 64:(j + 1) * 64], lhsT=kt[:, k0:k0 + 64],
                                                                  rhs=qt[:, q0:q0 + 64], start=True, stop=True),
                              reads=[r_kt, r_qt], writes=[r_s])
                    for c in range(2):
                        kb.op("pe", lambda e, c=c: e.matmul(s2_t[:, c * 64:(c + 1) * 64], lhsT=kt[:, c * 128:(c + 1) * 128],
                                                           rhs=qt[:, q0:q0 + 64], start=True, stop=True),
                              reads=[r_kt, r_qt], writes=[r_s2])
                    kb.op("dve", lambda e: e.scalar_tensor_tensor(
                        out=sb_t[:].rearrange("p (j q) -> p j q", q=64), in0=s_t[0:64, 0:512].rearrange("p (j q) -> p j q", q=64), scalar=SCALE,
                        in1=tb[:, h * 15 + off:h * 15 + off + 8, :], op0=ALU.mult, op1=ALU.add),
                        reads=[r_s, r_tb], writes=[r_sb])
                    kb.op("act", lambda e: e.activation(out=pl[:], in_=sb_t[:], func=AF.Exp), reads=[r_sb], writes=[r_pl])
                    kb.op("act", lambda e: e.activation(out=pc[:], in_=s2_t[:, 0:128], func=AF.Exp, scale=SCALE),
                          reads=[r_s2], writes=[r_pc])

                def na_s2(r):
                    rs = min(max(r - 4, 0), 24)
                    _, _, (o_t, r_o), _, (pl, r_pl), (pc, r_pc), (ob, r_ob), (rc, r_rc), _ = na_bufs(r)
                    for j in range(8):
                        kb.op("pe", lambda e, j=j: e.matmul(o_t[0:64, 0:129], lhsT=pl[:, j * 64:(j + 1) * 64],
                                                           rhs=v64[:, 4 + rs + j, :], start=(j == 0), stop=False),
                              reads=[r_pl, r_v64], writes=[r_o])
                    for c in range(2):
                        kb.op("pe", lambda e, c=c: e.matmul(o_t[0:64, 0:129], lhsT=pc[:, c * 64:(c + 1) * 64], rhs=vc[:, c, :],
                                                           start=False, stop=(c == 1)), reads=[r_pc, r_vc], writes=[r_o])
                    kb.op("dve", lambda e: e.reciprocal(out=rc[0:64, :], in_=o_t[0:64, 128:129]), reads=[r_o], writes=[r_rc])
                    kb.op("dve", lambda e: e.tensor_scalar(out=ob[0:64, :], in0=o_t[0:64, 0:128], scalar1=rc[0:64, 0:1],
                                                           scalar2=None, op0=ALU.mult), reads=[r_o, r_rc], writes=[r_ob])

                def na_s3(r):
                    q0 = CTX + 64 * r
                    _, _, _, _, _, _, (ob, r_ob), _, pst = na_bufs(r)
                    transpose_out(ob[0:64, :], r_ob, 64, yrow, r_yrow, q0, pst)

                na_s1(0)
                for r in range(32):
                    if r + 1 < 32:
                        na_s1(r + 1)
                    na_s2(r)
                    if r >= 1:
                        na_s3(r - 1)
                na_s3(31)
                cnt += 32
                kb.dma("act", YT[b, (8 + h) * 128:(9 + h) * 128, :], yrow[:], reads=[r_yrow])
            kb.barrier()

    def sincos(st, name, th, r_th, shape, s_out, c_out, r_s, r_c):
        n = shape[1]
        v, r_v = kb.sb(st, name + "_v", shape, F32)
        ki, r_ki = kb.sb(st, name + "_ki", shape, mybir.dt.int32)
        kf, r_kf = kb.sb(st, name + "_kf", shape, F32)
        m, r_m = kb.sb(st, name + "_m", shape, F32)
        for (dst, r_dst, shift) in ((s_out, r_s, 0.0), (c_out, r_c, 0.25)):
            kb.op("dve", lambda e: e.tensor_scalar(out=v[:], in0=th, scalar1=1.0 / (2 * math.pi), scalar2=shift,
                                                   op0=ALU.mult, op1=ALU.add), reads=[r_th], writes=[r_v])
            kb.op("dve", lambda e: e.tensor_copy(out=ki[:], in_=v[:]), reads=[r_v], writes=[r_ki])
            kb.op("dve", lambda e: e.tensor_copy(out=kf[:], in_=ki[:]), reads=[r_ki], writes=[r_kf])
            kb.op("dve", lambda e: e.tensor_tensor(out=v[:], in0=v[:], in1=kf[:], op=ALU.subtract), reads=[r_v, r_kf], writes=[r_v])
            kb.op("dve", lambda e: e.tensor_scalar(out=m[:], in0=v[:], scalar1=0.5, scalar2=None, op0=ALU.is_gt),
                  reads=[r_v], writes=[r_m])
            kb.op("dve", lambda e: e.tensor_tensor(out=v[:], in0=v[:], in1=m[:], op=ALU.subtract), reads=[r_v, r_m], writes=[r_v])
            kb.op("dve", lambda e: e.tensor_scalar(out=m[:], in0=v[:], scalar1=-0.5, scalar2=None, op0=ALU.is_lt),
                  reads=[r_v], writes=[r_m])
            kb.op("dve", lambda e: e.tensor_tensor(out=v[:], in0=v[:], in1=m[:], op=ALU.add), reads=[r_v, r_m], writes=[r_v])
            kb.op("act", lambda e: e.activation(out=dst, in_=v[:], func=AF.Sin, scale=2 * math.pi), reads=[r_v], writes=[r_dst])

    tshare = {}

    def table_emitters(st, i, NCH):
        if True:
            c1B, r_c1B = kb.sb(st, "tc1B%d" % i, [128, 64], F32)
            s1B, r_s1B = kb.sb(st, "ts1B%d" % i, [128, 64], F32)
            lim, r_lim = kb.sb(st, "tlim%d" % i, [128, 64], F32)
            dt, r_dt = kb.sb(st, "tdt%d" % i, [128, 64], F32)
            th, r_th = kb.sb(st, "tth%d" % i, [128, 64], F32)
            kb.dma("sp", lim[:], lamB_im_in[i], writes=[r_lim])
            kb.dma("sp", dt[:], dtB_in[i], writes=[r_dt])
            kb.op("act", lambda e: e.activation(out=dt[:], in_=dt[:], func=AF.Exp), reads=[r_dt], writes=[r_dt])
            kb.op("dve", lambda e: e.tensor_tensor(out=th[:], in0=lim[:], in1=dt[:], op=ALU.mult), reads=[r_lim, r_dt], writes=[r_th])
            sincos(st, "tscB%d" % i, th[:], r_th, [128, 64], s1B[:], c1B[:], r_s1B, r_c1B)
            ems = []
            if "chains" not in tshare:
                tshare["chains"] = [(kb.sb(st, "tEc%d" % c, [128, T], F32), kb.sb(st, "tEs%d" % c, [128, T], F32),
                                     [kb.sb(st, "ttsc%d_%d" % (c, j), [128, 1024], F32) for j in range(4)]) for c in range(NCH)]
                tshare["ebufs"] = [(kb.sb(st, "tEbc%d" % c, [128, T], BF16), kb.sb(st, "tEbs%d" % c, [128, T], BF16)) for c in range(NCH)]
            chains = tshare["chains"]
            ebufs = tshare["ebufs"]
            def grp(g0):
                NC_ = min(NCH, 64 - g0)
                for c in range(NC_):
                    col = g0 + c
                    (Ec, r_Ec), (Es, r_Es), tsc = chains[c]
                    kb.op("pool", lambda e: e.memset(Ec[:, 0:1], 1.0), writes=[r_Ec])
                    kb.op("pool", lambda e: e.memset(Es[:, 0:1], 0.0), writes=[r_Es])
                    kb.op("act", lambda e: e.activation(out=Ec[:, 1:2], in_=c1B[:, col:col + 1], func=AF.Copy), reads=[r_c1B], writes=[r_Ec])
                    kb.op("act", lambda e: e.activation(out=Es[:, 1:2], in_=s1B[:, col:col + 1], func=AF.Copy), reads=[r_s1B], writes=[r_Es])
                n = 2
                while n < T:
                    m = min(n - 1, T - n)
                    allp = []
                    for c in range(NC_):
                        (Ec, r_Ec), (Es, r_Es), tsc = chains[c]
                        cs_ap = Ec[:, n - 1:n]
                        sn_ap = Es[:, n - 1:n]
                        prods = []
                        for k_, (src_, sc_) in enumerate(((Ec, cs_ap), (Es, sn_ap), (Es, cs_ap), (Ec, sn_ap))):
                            tk, r_tk = tsc[k_]
                            if k_ < 2:
                                kb.op("act", lambda e, tk=tk, src_=src_, sc_=sc_: e.activation(
                                    out=tk[:, 0:m], in_=src_[:, 1:1 + m], func=AF.Identity, scale=sc_, bias=zero_c[:]),
                                    reads=[r_Ec, r_Es, r_zero], writes=[r_tk])
                            else:
                                kb.op("dve", lambda e, tk=tk, src_=src_, sc_=sc_: e.tensor_scalar(
                                    out=tk[:, 0:m], in0=src_[:, 1:1 + m], scalar1=sc_, scalar2=None, op0=ALU.mult),
                                    reads=[r_Ec, r_Es], writes=[r_tk])
                            prods.append((tk, r_tk))
                        allp.append(prods)
                    for c in range(NC_):
                        (Ec, r_Ec), (Es, r_Es), tsc = chains[c]
                        prods = allp[c]
                        kb.op("pool", lambda e: e.tensor_tensor(out=Ec[:, n:n + m], in0=prods[0][0][:, 0:m], in1=prods[1][0][:, 0:m],
                                                                op=ALU.subtract), reads=[prods[0][1], prods[1][1]], writes=[r_Ec])
                        kb.op("pool", lambda e: e.tensor_tensor(out=Es[:, n:n + m], in0=prods[2][0][:, 0:m], in1=prods[3][0][:, 0:m],
                                                                op=ALU.add), reads=[prods[2][1], prods[3][1]], writes=[r_Es])
                    n += m
                for c in range(NC_):
                    col = g0 + c
                    (Ec, r_Ec), (Es, r_Es), tsc = chains[c]
                    (ebc, r_ebc), (ebs, r_ebs) = ebufs[c]
                    kb.op("act", lambda e: e.activation(out=ebc[:], in_=Ec[:], func=AF.Copy), reads=[r_Ec], writes=[r_ebc])
                    kb.op("dve", lambda e: e.tensor_copy(out=ebs[:], in_=Es[:]), reads=[r_Es], writes=[r_ebs])
                    kb.dma("act", ETAB[i, col, 0], ebc[:], reads=[r_ebc])
                    kb.dma("act", ETAB[i, col, 1], ebs[:], reads=[r_ebs])
            for g0_ in range(0, 64, NCH):
                ems.append(lambda g0_=g0_: grp(g0_))
            return ems

    def s5_core(l, b):
        i = l // 2
        with ExitStack() as st:
            lbre = kb.sb(st, "bbre", [128, 1024], F32)
            lbim = kb.sb(st, "bbim", [128, 1024], F32)
            mA, r_mA = kb.sb(st, "maskA", [128, 8], F32)
            kb.dma("sp", mA[:], maskA_in, writes=[r_mA])
            rhoB, r_rhoB = kb.sb(st, "rhoB", [128, 64], F32)
            c1B, r_c1B = kb.sb(st, "c1B", [128, 64], F32)
            s1B, r_s1B = kb.sb(st, "s1B", [128, 64], F32)
            s5d, r_s5d = kb.sb(st, "s5d", [128, 2, 8], F32)
            kb.dma("sp", s5d[:], s5d_in, writes=[r_s5d])
            with ExitStack() as sp_:
                def ld(name, src, n):
                    t_, r_ = kb.sb(sp_, name, [128, n], F32)
                    kb.dma("sp", t_[:], src, writes=[r_])
                    return t_, r_
                for (lay, n, lre_in, lim_in, dt_in) in (("A", 1024, lamA_re_in[i], lamA_im_in[i], dtA_in[i]),
                                                          ("B", 64, lamB_re_in[i], lamB_im_in[i], dtB_in[i])):
                    lre, r_lre = ld("lre" + lay, lre_in, n)
                    lim, r_lim = ld("lim" + lay, lim_in, n)
                    dt, r_dt = ld("dt" + lay, dt_in, n)
                    th, r_th = kb.sb(sp_, "th" + lay, [128, n], F32)
                    mag, r_mag = kb.sb(sp_, "mag" + lay, [128, n], F32)
                    sn, r_sn = kb.sb(sp_, "sn" + lay, [128, n], F32)
                    cs_, r_cs_ = kb.sb(sp_, "cs" + lay, [128, n], F32)
                    kb.op("dve", lambda e: e.tensor_scalar(out=lre[:], in0=lre[:], scalar1=-1e-4, scalar2=None, op0=ALU.min),
                          reads=[r_lre], writes=[r_lre])
                    kb.op("act", lambda e: e.activation(out=dt[:], in_=dt[:], func=AF.Exp), reads=[r_dt], writes=[r_dt])
                    kb.op("dve", lambda e: e.tensor_tensor(out=mag[:], in0=lre[:], in1=dt[:], op=ALU.mult), reads=[r_lre, r_dt], writes=[r_mag])
                    kb.op("act", lambda e: e.activation(out=mag[:], in_=mag[:], func=AF.Exp), reads=[r_mag], writes=[r_mag])
                    kb.op("dve", lambda e: e.tensor_tensor(out=th[:], in0=lim[:], in1=dt[:], op=ALU.mult), reads=[r_lim, r_dt], writes=[r_th])
                    if lay == "B":
                        sincos(sp_, "scB", th[:], r_th, [128, n], s1B[:], c1B[:], r_s1B, r_c1B)
                        kb.op("dve", lambda e: e.tensor_copy(out=rhoB[:], in_=mag[:]), reads=[r_mag], writes=[r_rhoB])
                        continue
                    sincos(sp_, "scA", th[:], r_th, [128, n], sn[:], cs_[:], r_sn, r_cs_)
                    bre, r_bre = ld("bre", bTre_in[i], n)
                    bim, r_bim = ld("bim", bTim_in[i], n)
                    den, cre, cim, tmp = [kb.sb(sp_, nm_, [128, n], F32) for nm_ in ("den", "cre", "cim", "tmpA")]

                    def tt(o, a, b_, op):
                        kb.op("dve", lambda e: e.tensor_tensor(out=o[0][:], in0=a[0][:], in1=b_[0][:], op=op),
                              reads=[a[1], b_[1]], writes=[o[1]])
                    LRE, LIM, MAG, SN, CS = (lre, r_lre), (lim, r_lim), (mag, r_mag), (sn, r_sn), (cs_, r_cs_)
                    BRE, BIM = (bre, r_bre), (bim, r_bim)
                    tt(lbre, MAG, CS, ALU.mult)
                    tt(lbim, MAG, SN, ALU.mult)
                    tt(den, LRE, LRE, ALU.mult)
                    tt(tmp, LIM, LIM, ALU.mult)
                    tt(den, den, tmp, ALU.add)
                    kb.op("dve", lambda e: e.reciprocal(out=den[0][:], in_=den[0][:]), reads=[den[1]], writes=[den[1]])
                    kb.op("dve", lambda e: e.tensor_scalar(out=lbre[0][:], in0=lbre[0][:], scalar1=-1.0, scalar2=None, op0=ALU.add),
                          reads=[lbre[1]], writes=[lbre[1]])
                    tt(cre, lbre, LRE, ALU.mult)
                    tt(tmp, lbim, LIM, ALU.mult)
                    tt(cre, cre, tmp, ALU.add)
                    tt(cre, cre, den, ALU.mult)
                    tt(cim, lbim, LRE, ALU.mult)
                    tt(tmp, lbre, LIM, ALU.mult)
                    tt(cim, cim, tmp, ALU.subtract)
                    tt(cim, cim, den, ALU.mult)
                    tt(lbre, cre, BRE, ALU.mult)
                    tt(tmp, cim, BIM, ALU.mult)
                    tt(lbre, lbre, tmp, ALU.subtract)
                    tt(lbim, cre, BIM, ALU.mult)
                    tt(tmp, cim, BRE, ALU.mult)
                    tt(lbim, lbim, tmp, ALU.add)
                kb.barrier()
            EB = [(kb.sb(st, "Ebc%d" % j, [128, T], BF16), kb.sb(st, "Ebs%d" % j, [128, T], BF16)) for j in range(2)]
            PB = [(kb.sb(st, "pbr%d" % j, [128, T], BF16), kb.sb(st, "pbi%d" % j, [128, T], BF16)) for j in range(2)]
            RHO = [kb.sb(st, "rho%d" % j, [128, T], F32) for j in range(2)]
            UU = [(kb.sb(st, "uT%d" % j, [128, T], BF16), kb.sb(st, "uR%d" % j, [128, T], BF16)) for j in range(2)]
            xre, r_xre = kb.sb(st, "xre", [128, T], BF16)
            xim, r_xim = kb.sb(st, "xim", [128, T], BF16)
            wre, r_wre = kb.sb(st, "wre", [128, T], BF16)
            wim, r_wim = kb.sb(st, "wim", [128, T], BF16)
            mm = [kb.sb(st, "mm%d" % j, [128, T], BF16) for j in range(4)]
            hre, r_hre = kb.sb(st, "hre", [128, T], BF16)
            him, r_him = kb.sb(st, "him", [128, T], BF16)
            hre2, r_hre2 = kb.sb(st, "hre2", [128, T], BF16)
            him2, r_him2 = kb.sb(st, "him2", [128, T], BF16)
            yv, r_yv = kb.sb(st, "yv", [128, T], F32)
            g1, r_g1 = kb.sb(st, "g1", [128, T], F32)
            ga, r_ga = kb.sb(st, "ga", [128, T], BF16)
            lcs = [kb.sb(st, "lcs%d" % j, [128, 256], F32) for j in range(2)]
            lcb = [kb.sb(st, "lcb%d" % j, [128, 2, 128], BF16) for j in range(2)]
            ldb = [kb.sb(st, "ldb%d" % j, [128, 2, 128], BF16) for j in range(2)]
            segs = [(0, CTX), (CTX, T)]
            yacc = [ps[3 + j] for j in range(5)]

            def kidx(k):
                cbk, rem = k // 8, k % 8
                r, gpl = rem // 4, rem % 4
                return cbk, r, gpl, cbk * 4 + gpl, r * 32 + cbk * 4 + gpl, r * 8 + cbk

            def load_u(cbk):
                (uT, r_uT), (uR, r_uR) = UU[cbk % 2]
                kb.dma("sp", uT[:], ZT[b, cbk * 128:(cbk + 1) * 128, :], writes=[r_uT])
                for (s0, s1) in segs:
                    kb.op("dve", lambda e, s0=s0, s1=s1: e.tensor_copy(out=uR[:, s0:s1], in_=uT[:, s0:s1][:, ::-1]),
                          reads=[r_uT], writes=[r_uR])

            def fetch(k):
                col = kidx(k)[4]
                (Ec_, r_Ec_), (Es_, r_Es_) = EB[k % 2]
                kb.dma("sp", Ec_[:], ETAB[i, col, 0], writes=[r_Ec_])
                kb.dma("sp", Es_[:], ETAB[i, col, 1], writes=[r_Es_])

            def stageA1(k):
                cbk, r, gpl, gp, col, rc = kidx(k)
                (Ebc, r_Ebc), (Ebs, r_Ebs) = EB[k % 2]
                (pbr, r_pbr), (pbi, r_pbi) = PB[k % 2]
                rho, r_rho = RHO[k % 2]
                (uT, r_uT), (uR, r_uR) = UU[cbk % 2]
                usrc, r_usrc = (uT, r_uT) if r == 0 else (uR, r_uR)
                kb.op("act", lambda e: e.activation(out=rho[:], in_=uT[:], func=AF.Identity, scale=0.0, bias=rhoB[:, col:col + 1]),
                      reads=[r_uT, r_rhoB], writes=[r_rho])
                l_d, r_ld = ldb[k % 2]
                for gi in range(2):
                    for ri, src in enumerate((lbre, lbim)):
                        kb.op("pool", lambda e, src=src, ri=ri, gi=gi: e.tensor_scalar(
                            out=l_d[:, ri, gi * 64:(gi + 1) * 64], in0=src[0][:, rc * 64:(rc + 1) * 64],
                            scalar1=mA[:, 2 * gpl + gi:2 * gpl + gi + 1], scalar2=0.0, op0=ALU.mult, op1=ALU.add),
                            reads=[src[1], r_mA], writes=[r_ld])
                l_s, r_ls = lcs[k % 2]
                l_b, r_lb = lcb[k % 2]
                kb.dma("sp", l_s[:], LC_in[i, col], writes=[r_ls])
                kb.op("pool", lambda e: e.tensor_scalar(out=l_b[:, 0, :], in0=l_s[:, 0:128], scalar1=1.0, scalar2=0.0, op0=ALU.mult, op1=ALU.add),
                      reads=[r_ls], writes=[r_lb])
                kb.op("pool", lambda e: e.tensor_scalar(out=l_b[:, 1, :], in0=l_s[:, 128:256], scalar1=-1.0, scalar2=0.0, op0=ALU.mult, op1=ALU.add),
                      reads=[r_ls], writes=[r_lb])
                for ti, (t0, n_) in enumerate(TILES_ALL):
                    pr, r_pr = ps[ti % 2]
                    pi_, r_pi = ps[2]
                    sl = slice(t0, t0 + n_)
                    kb.op("pe", lambda e: e.matmul(pr[:, 0:n_], lhsT=l_d[:, 0, :], rhs=usrc[:, t0:t0 + n_], start=True, stop=True),
                          reads=[r_ld, r_usrc], writes=[r_pr])
                    kb.op("pe", lambda e: e.matmul(pi_[:, 0:n_], lhsT=l_d[:, 1, :], rhs=usrc[:, t0:t0 + n_], start=True, stop=True),
                          reads=[r_ld, r_usrc], writes=[r_pi])
                    kb.op("act", lambda e: e.activation(out=pbr[:, sl], in_=pr[:, 0:n_], func=AF.Copy), reads=[r_pr], writes=[r_pbr])
                    kb.op("act", lambda e: e.activation(out=pbi[:, sl], in_=pi_[:, 0:n_], func=AF.Copy), reads=[r_pi], writes=[r_pbi])

            def stageA2(k):
                (Ebc, r_Ebc), (Ebs, r_Ebs) = EB[k % 2]
                (pbr, r_pbr), (pbi, r_pbi) = PB[k % 2]
                (a_t, r_a), (b_t, r_b), (c_t, r_c), (d_t, r_d) = mm
                kb.op("dve", lambda e: e.tensor_tensor(out=a_t[:], in0=pbr[:], in1=Ebc[:], op=ALU.mult), reads=[r_pbr, r_Ebc], writes=[r_a])
                kb.op("dve", lambda e: e.tensor_tensor(out=c_t[:], in0=pbi[:], in1=Ebs[:], op=ALU.mult), reads=[r_pbi, r_Ebs], writes=[r_c])
                kb.op("dve", lambda e: e.tensor_tensor(out=xre[:], in0=c_t[:], in1=a_t[:], op=ALU.add), reads=[r_c, r_a], writes=[r_xre])
                kb.op("dve", lambda e: e.tensor_tensor(out=b_t[:], in0=pbr[:], in1=Ebs[:], op=ALU.mult), reads=[r_pbr, r_Ebs], writes=[r_b])
                kb.op("dve", lambda e: e.tensor_tensor(out=d_t[:], in0=pbi[:], in1=Ebc[:], op=ALU.mult), reads=[r_pbi, r_Ebc], writes=[r_d])
                kb.op("dve", lambda e: e.tensor_tensor(out=xim[:], in0=d_t[:], in1=b_t[:], op=ALU.subtract), reads=[r_d, r_b], writes=[r_xim])

            def stageB(k):
                cbk, r, gpl, gp, col, rc = kidx(k)
                l_b, r_lb = lcb[k % 2]
                (Ebc, r_Ebc), (Ebs, r_Ebs) = EB[k % 2]
                rho, r_rho = RHO[k % 2]
                kb.op("dve", lambda e: e.tensor_tensor_scan(out=wre[:], data0=rho[:], data1=xre[:], initial=0.0,
                                                            op0=ALU.mult, op1=ALU.add), reads=[r_rho, r_xre], writes=[r_wre])
                kb.op("dve", lambda e: e.tensor_tensor_scan(out=wim[:], data0=rho[:], data1=xim[:], initial=0.0,
                                                            op0=ALU.mult, op1=ALU.add), reads=[r_rho, r_xim], writes=[r_wim])
                for k_, (wa, r_wa, eb, r_eb) in enumerate(((wre, r_wre, Ebc, r_Ebc), (wim, r_wim, Ebs, r_Ebs),
                                                           (wre, r_wre, Ebs, r_Ebs), (wim, r_wim, Ebc, r_Ebc))):
                    kb.op("dve", lambda e, k_=k_, wa=wa, eb=eb: e.tensor_tensor(out=mm[k_][0][:], in0=wa[:], in1=eb[:], op=ALU.mult),
                          reads=[r_wa, r_eb], writes=[mm[k_][1]])
                kb.op("dve", lambda e: e.tensor_tensor(out=hre[:], in0=mm[0][0][:], in1=mm[1][0][:], op=ALU.subtract),
                      reads=[mm[0][1], mm[1][1]], writes=[r_hre])
                kb.op("dve", lambda e: e.tensor_tensor(out=him[:], in0=mm[2][0][:], in1=mm[3][0][:], op=ALU.add),
                      reads=[mm[2][1], mm[3][1]], writes=[r_him])
                hr_, r_hr_, hi_, r_hi_ = hre, r_hre, him, r_him
                if r == 1:
                    for (s0, s1) in segs:
                        kb.op("dve", lambda e, s0=s0, s1=s1: e.tensor_copy(out=hre2[:, s0:s1], in_=hre[:, s0:s1][:, ::-1]),
                              reads=[r_hre], writes=[r_hre2])
                        kb.op("dve", lambda e, s0=s0, s1=s1: e.tensor_copy(out=him2[:, s0:s1], in_=him[:, s0:s1][:, ::-1]),
                              reads=[r_him], writes=[r_him2])
                    hr_, r_hr_, hi_, r_hi_ = hre2, r_hre2, him2, r_him2
                first = (r == 0 and gpl == 0)
                last = (r == 1 and gpl == 3)
                for ti, (t0, n_) in enumerate(TILES_ALL):
                    ya, r_ya = yacc[ti]
                    kb.op("pe", lambda e: e.matmul(ya[:, 0:n_], lhsT=l_b[:, 0, :], rhs=hr_[:, t0:t0 + n_], start=first, stop=False),
                          reads=[r_lb, r_hr_], writes=[r_ya])
                    kb.op("pe", lambda e: e.matmul(ya[:, 0:n_], lhsT=l_b[:, 1, :], rhs=hi_[:, t0:t0 + n_], start=False, stop=last),
                          reads=[r_lb, r_hi_], writes=[r_ya])

            def epilogue(cbk):
                (uT, r_uT), _ = UU[cbk % 2]
                for ti, (t0, n_) in enumerate(TILES_ALL):
                    ya, r_ya = yacc[ti]
                    kb.op("dve", lambda e: e.scalar_tensor_tensor(out=yv[:, t0:t0 + n_], in0=uT[:, t0:t0 + n_], scalar=s5d[:, i, cbk:cbk + 1],
                                                                  in1=ya[:, 0:n_], op0=ALU.mult, op1=ALU.add),
                          reads=[r_uT, r_s5d, r_ya], writes=[r_yv])
                kb.op("act", lambda e: e.activation(out=g1[:], in_=yv[:], func=AF.Square), reads=[r_yv], writes=[r_g1])
                kb.op("pool", lambda e: e.tensor_scalar(out=g1[:], in0=g1[:], scalar1=0.044715, scalar2=1.0, op0=ALU.mult, op1=ALU.add),
                      reads=[r_g1], writes=[r_g1])
                kb.op("pool", lambda e: e.tensor_tensor(out=g1[:], in0=g1[:], in1=yv[:], op=ALU.mult), reads=[r_g1, r_yv], writes=[r_g1])
                kb.op("act", lambda e: e.activation(out=g1[:], in_=g1[:], func=AF.Sigmoid, scale=2.0 * math.sqrt(2.0 / math.pi)),
                      reads=[r_g1], writes=[r_g1])
                kb.op("pool", lambda e: e.tensor_tensor(out=ga[:], in0=g1[:], in1=yv[:], op=ALU.mult), reads=[r_g1, r_yv], writes=[r_ga])
                kb.dma("act", GAT[b, cbk * 128:(cbk + 1) * 128, :], ga[:], reads=[r_ga])

            load_u(0)
            fetch(0)
            stageA1(0)
            for k in range(64):
                if k + 1 < 64:
                    if (k + 1) % 8 == 0:
                        load_u((k + 1) // 8)
                    fetch(k + 1)
                    stageA1(k + 1)
                stageA2(k)
                stageB(k)
                if k % 8 == 7:
                    epilogue(k // 8)
            kb.barrier()
        kb.mark("  glu")
        with ExitStack() as st:
            ta, r_ta = kb.sb(st, "ta", [128, 8, T], BF16)
            kb.dma("sp", ta[:], GAT[b].rearrange("(kc p) t -> p kc t", p=128), writes=[r_ta])
            ws = WStream(st, "wglu", 8, 128)
            sgm = [kb.sb(st, "sgm%d" % j, [128, 512], F32) for j in range(2)]
            yrow = [kb.sb(st, "gyrow%d" % j, [128, T], BF16) for j in range(2)]
            for ob in range(8):
                w_b, r_w = ws.load(w_glu[i], ob * 128, 128)
                yr, r_yr = yrow[ob % 2]
                for ti, (t0, n_) in enumerate(TILES_ALL):
                    p_t, r_p = nps(0, 4)
                    for kc in range(8):
                        kb.op("pe", lambda e, kc=kc: e.matmul(p_t[:, 0:n_], lhsT=w_b[:, kc, 0:128], rhs=ta[:, kc, t0:t0 + n_],
                                                             start=(kc == 0), stop=(kc == 7)), reads=[r_w, r_ta], writes=[r_p])
                    s_t, r_s = sgm[ti % 2]
                    kb.op("act", lambda e: e.activation(out=s_t[:, 0:n_], in_=p_t[:, 0:n_], func=AF.Sigmoid), reads=[r_p], writes=[r_s])
                    kb.op("dve", lambda e: e.tensor_tensor(out=yr[:, t0:t0 + n_], in0=s_t[:, 0:n_], in1=ta[:, ob, t0:t0 + n_], op=ALU.mult),
                          reads=[r_s, r_ta], writes=[r_yr])
                kb.dma("act", YT[b, ob * 128:(ob + 1) * 128, :], yr[:], reads=[r_yr])
            kb.barrier()
    stage_S0()
    for l in range(n_layers):
        with_ctx = l < DEPTH - 1
        for b in range(2):
            if l == 0:
                for kc in range(KC):
                    kb.dma("sp", XT[b, kc * 128:(kc + 1) * 128, :], xT_in[b, kc * 128:(kc + 1) * 128, :])
                kb.barrier()
            if mixers:
                with ExitStack() as stA:
                    hT, r_hT = kb.sb(stA, "hT", [128, KC, T], BF16)
                    kb.mark("L%d b%d norm1" % (l, b))
                    norm_stage("n1", XT[b], TILES_ALL,
                               lambda kc, is_ctx: A1[:, l, kc, (2 if is_ctx else b):(3 if is_ctx else b + 1)],
                               lambda kc, is_ctx: modv(l, 0, kc, 2 if is_ctx else b), hT, r_hT)
                    kb.mark("L%d b%d proj" % (l, b))
                    if l % 2 == 0:
                        even_proj(l, b, hT, r_hT)
                    else:
                        odd_proj(l, b, hT, r_hT)
                if l % 2 == 0:
                    kb.mark("L%d b%d na" % (l, b))
                    if "na" in mix_parts:
                        na_core(l, b, with_ctx)
                    if "s5" in mix_parts:
                        kb.mark("L%d b%d s5" % (l, b))
                        s5_core(l, b)
                else:
                    kb.mark("L%d b%d oddcore" % (l, b))
                    odd_core(l, b, with_ctx)
                kb.mark("L%d b%d wout" % (l, b))
                with ExitStack() as stA:
                    hT, r_hT = kb.sb(stA, "yTsb", [128, KC, T], BF16)
                    kb.dma("sp", hT[:], YT[b].rearrange("(kc p) t -> p kc t", p=128), writes=[r_hT])
                    tiles = TILES_ALL if with_ctx else TILES_LAT
                    ws = WStream(stA, "wo", KC, 128)
                    linear_resid(stA, "wo", b, hT, r_hT, KC, w_out[l], tiles,
                                 lambda blk, is_ctx: modv(l, 2, blk, 2 if is_ctx else b), ws, (0, T))
                    kb.barrier()
            with ExitStack() as stA:
                hT, r_hT = kb.sb(stA, "h2T", [128, KC, T], BF16)
                tiles = TILES_ALL if with_ctx else TILES_LAT
                kb.mark("L%d b%d norm2" % (l, b))
                norm_stage("n2", XT[b], tiles,
                           lambda kc, is_ctx: A2[:, l, kc, (2 if is_ctx else b):(3 if is_ctx else b + 1)],
                           lambda kc, is_ctx: modv(l, 3, kc, 2 if is_ctx else b), hT, r_hT)
                kb.mark("L%d b%d ffn_up" % (l, b))
                ffn_up(l, b, hT, r_hT, with_ctx)
            kb.mark("L%d b%d ffn_down" % (l, b))
            ffn_down(l, b, with_ctx)

    kb.mark("final")
    for b in range(2):
        with ExitStack() as st:
            xt = [kb.sb(st, "fx%d" % i, [128, KC, 512], F32) for i in range(2)]
            sq = [kb.sb(st, "fq%d" % i, [128, KC, 512], BF16) for i in range(2)]
            rt = [kb.sb(st, "fr%d" % i, [128, 512], F32) for i in range(2)]
            srcv = XT[b].rearrange("(kc p) t -> p kc t", p=128)
            dstv = out_T[b].rearrange("(kc p) t -> p kc t", p=128)
            for ti, (t0, n) in enumerate(TILES_LAT):
                x_t, r_x = xt[ti % 2]
                q_t, r_q = sq[ti % 2]
                r_t, r_r = rt[ti % 2]
                p_t, r_p = ps[ti % 2]
                kb.dma("sp", x_t[:], srcv[:, :, t0:t0 + n], writes=[r_x])
                kb.op("act", lambda e: e.activation(out=q_t[:], in_=x_t[:], func=AF.Square), reads=[r_x], writes=[r_q])
                for kc in range(KC):
                    kb.op("pe", lambda e, kc=kc: e.matmul(p_t[:], lhsT=ones_b[:], rhs=q_t[:, kc, :],
                                                         start=(kc == 0), stop=(kc == KC - 1)),
                          reads=[r_q, r_ones], writes=[r_p])
                kb.op("dve", lambda e: e.tensor_scalar(out=r_t[:], in0=p_t[:], scalar1=1.0 / D, scalar2=EPS,
                                                       op0=ALU.mult, op1=ALU.add), reads=[r_p], writes=[r_r])
                kb.op("act", lambda e: e.activation(out=r_t[:], in_=r_t[:], func=AF.Sqrt), reads=[r_r], writes=[r_r])
                kb.op("dve", lambda e: e.reciprocal(out=r_t[:], in_=r_t[:]), reads=[r_r], writes=[r_r])
                kb.op("dve", lambda e: e.tensor_tensor(out=x_t[:], in0=x_t[:],
                                                       in1=r_t[:].unsqueeze(1).broadcast_to([128, KC, 512]),
                                                       op=ALU.mult), reads=[r_x, r_r], writes=[r_x])
                for kc in range(KC):
                    kb.op("act", lambda e, kc=kc: e.activation(out=x_t[:, kc, :], in_=x_t[:, kc, :], func=AF.Identity,
                                                               scale=gfin[:, kc:kc + 1], bias=zero_c[:]),
                          reads=[r_x, r_gfin, r_zero], writes=[r_x])
                kb.dma("act", dstv[:, :, t0 - CTX:t0 - CTX + n], x_t[:], reads=[r_x])
            kb.barrier()
    ex.close()
    return kb


_CACHE = {}


def host_inputs(inp, core, n_cores=8):
    b0 = 2 * core
    m = {}
    xT = np.empty((2, D, T), np.float32)
    for i in range(2):
        xT[i, :, :CTX] = inp["ctx"][b0 + i].T
        xT[i, :, CTX:] = inp["x"][b0 + i].T
    m["xT"] = xT
    cv = np.stack([inp["c"][b0], inp["c"][b0 + 1], inp["c_ctx"]], axis=-1)
    m["cT"] = np.ascontiguousarray(cv.reshape(KC, 128, 3).transpose(1, 0, 2))
    return m


def shared_inputs(inp):
    m = {}
    m["ada_w"] = np.ascontiguousarray(inp["ada_w"], np.float32)
    m["adab"] = np.ascontiguousarray(np.asarray(inp["ada_b"], np.float32).reshape(DEPTH, 96, 128).transpose(2, 0, 1))
    m["gmix"] = np.ascontiguousarray(np.asarray(inp["norm_mix_g"], np.float32).reshape(DEPTH, KC, 128).transpose(2, 0, 1))
    m["gffn"] = np.ascontiguousarray(np.asarray(inp["norm_ffn_g"], np.float32).reshape(DEPTH, KC, 128).transpose(2, 0, 1))
    m["gfin"] = fm(inp["final_norm_g"], KC)
    m["w_out"] = np.ascontiguousarray(inp["w_out"], np.float32)
    m["w_up"] = np.ascontiguousarray(inp["ffn_w_up"], np.float32)
    m["w_down"] = np.ascontiguousarray(inp["ffn_w_down"], np.float32)
    m["dww"] = np.ascontiguousarray(np.asarray(inp["ffn_dw_w"], np.float32).reshape(DEPTH, 3, 88, 128).transpose(3, 0, 1, 2))
    m["dwb"] = np.ascontiguousarray(np.asarray(inp["ffn_dw_b"], np.float32).reshape(DEPTH, 88, 128).transpose(2, 0, 1))
    m["ident"] = np.eye(128, dtype=np.float32)
    f32 = lambda a: np.ascontiguousarray(np.asarray(a, np.float32))
    m["ev_w_in"] = f32(inp["ev_w_in"])
    m["od_w_in"] = f32(inp["od_w_in"])
    m["w_glu"] = f32(inp["s5_w_glu"])
    t = np.arange(LAT)
    pos = np.stack([t // 64, t % 64], 0).astype(np.float32)
    inv = (10000.0 ** (-2.0 * np.arange(32, dtype=np.float32) / 64)).astype(np.float32)
    d = np.arange(128)
    axis, half, fr = d // 64, (d % 64) // 32, d % 32
    ang = pos[axis] * inv[fr][:, None]
    m["ropeC"] = f32(np.cos(ang))
    m["ropeS"] = f32(np.where(half[:, None] == 0, -np.sin(ang), np.sin(ang)))
    partner = np.where(half == 0, d + 32, d - 32)
    pm = np.zeros((128, 128), np.float32)
    pm[partner, d] = 1.0
    m["permM"] = pm
    jj, ii = np.meshgrid(np.arange(128), np.arange(128), indexing="ij")
    m["mprev"] = f32(jj >= ii)
    m["mnext"] = f32(jj <= ii)
    m["dlam"] = f32(np.asarray(inp["diff_lambda"]).reshape(2, 512))
    m["subg"] = f32(inp["diff_subln_g"])
    m["sink"] = f32(inp["swa_sink"])
    kc, qc = np.meshgrid(np.arange(64), np.arange(64), indexing="ij")
    cidx = np.clip(kc - qc + 15, 0, 30)
    rpb = np.asarray(inp["na_rpb"], np.float32)
    tbg = rpb[:, :, :, cidx]
    m["natb"] = f32(tbg.transpose(0, 3, 1, 2, 4).reshape(2, 64, 8 * 15 * 64))
    ws_ = np.clip(qc - 8, 0, 48)
    m["namask"] = f32(np.where((kc >= ws_) & (kc < ws_ + 16), 0.0, -1e30))
    lre = np.asarray(inp["s5_lam_re"], np.float32)
    lim = np.asarray(inp["s5_lam_im"], np.float32)
    ldt = np.asarray(inp["s5_log_dt"], np.float32)
    bre = np.asarray(inp["s5_b_re"], np.float32)
    bim = np.asarray(inp["s5_b_im"], np.float32)

    def layA(a):
        a = a.reshape(2, 2, 8, 8, 64)
        a = np.broadcast_to(a[:, :, :, :, None, :], (2, 2, 8, 8, 16, 64))
        return f32(a.transpose(0, 3, 4, 1, 2, 5).reshape(2, 128, 1024))

    def layB(a):
        a = a.reshape(2, 2, 32, 2, 64)
        return f32(a.transpose(0, 3, 4, 1, 2).reshape(2, 128, 64))

    ldt4 = np.broadcast_to(ldt[:, :, :, None], (2, 2, 64, 64))
    m["lamA_re"], m["lamA_im"], m["dtA"] = layA(lre), layA(lim), layA(ldt4)
    m["lamB_re"], m["lamB_im"], m["dtB"] = layB(lre), layB(lim), layB(ldt4)

    def layBT(a):
        a = a.reshape(2, 2, 8, 8, 64, 16)
        return f32(a.transpose(0, 3, 5, 1, 2, 4).reshape(2, 128, 1024))
    m["bTre"], m["bTim"] = layBT(bre), layBT(bim)
    cre = np.asarray(inp["s5_c_re"], np.float32)
    cim = np.asarray(inp["s5_c_im"], np.float32)
    LC = np.zeros((2, 2, 32, 2, 64, 2, 128), np.float32)
    for g in range(64):
        gp, gi = g // 2, g % 2
        c0 = 16 * (g % 8)
        LC[:, :, gp, gi, :, 0, c0:c0 + 16] = cre[:, :, g].transpose(0, 1, 3, 2)
        LC[:, :, gp, gi, :, 1, c0:c0 + 16] = cim[:, :, g].transpose(0, 1, 3, 2)
    m["LC"] = f32(LC.reshape(2, 64, 128, 256))
    kk = np.arange(128)[:, None] // 16
    m["maskA"] = f32(kk == np.arange(8)[None, :])
    m["s5d"] = f32(np.asarray(inp["s5_d"], np.float32).reshape(2, 8, 128).transpose(2, 0, 1))
    return m


def kernel(**inputs):
    inp = {k: np.asarray(v) for k, v in inputs.items()}
    if "kb" not in _CACHE:
        _CACHE["kb"] = build_program()
    kb = _CACHE["kb"]
    sh = shared_inputs(inp)
    in_maps = []
    for c in range(8):
        m = dict(sh)
        m.update(host_inputs(inp, c))
        in_maps.append(m)
    res = run_bass_kernel_spmd(kb.nc, in_maps, core_ids=list(range(8)))
    out = np.empty((16, LAT, D), np.float32)
    for c in range(8):
        o = np.asarray(res.results[c]["outT"])
        for i in range(2):
            out[2 * c + i] = o[i].T
    return out
```

```python
import math
from contextlib import ExitStack
import numpy as np
import concourse.bass as bass
import concourse.mybir as mybir
from concourse.bass_utils import run_bass_kernel_spmd

F32 = mybir.dt.float32
BF16 = mybir.dt.bfloat16
AF = mybir.ActivationFunctionType
ALU = mybir.AluOpType

D = 2048
KC = 16
CTX = 256
LAT = 2048
T = CTX + LAT
FF = 5632
NDS = 12
DEFER_ROPE = True
DEPTH = 4
EPS = 1e-6


class Res:
    __slots__ = ("w", "r")

    def __init__(self):
        self.w = None
        self.r = {}


class KB:
    def __init__(self):
        nc = self.nc = bass.Bass("TRN2", target_bir_lowering=False)
        self.E = {"pe": nc.tensor, "dve": nc.vector, "act": nc.scalar, "pool": nc.gpsimd, "sp": nc.sync}
        self.semh = []
        self.esem = {}
        self.cnt = {}
        for e in ("pe", "dve", "act", "pool"):
            self.esem[e] = len(self.semh)
            self.semh.append(nc.alloc_semaphore("prog_" + e))
            self.cnt[e] = 0
        self.waited = {e: {} for e in self.E}
        self.dsem = {}
        self.dval = {}
        self.dnext = {}
        for q in ("sp", "act", "pool"):
            self.dsem[q] = []
            for i in range(NDS):
                self.dsem[q].append(len(self.semh))
                self.semh.append(nc.alloc_semaphore("d_%s_%d" % (q, i)))
            self.dval[q] = [0] * NDS
            self.dnext[q] = 0
        self.ps = []
        for i in range(8):
            self.ps.append((nc.alloc_psum_tensor("ps%d" % i, [128, 512], F32), Res()))
        self.n_ins = 0

    def _wait(self, e, toks):
        h = self.E[e]
        wd = self.waited[e]
        for tok in toks:
            if tok is None:
                continue
            semid, val, src = tok
            if src == "pe" and e == "pe":
                continue
            if wd.get(semid, 0) >= val:
                continue
            h.wait_ge(self.semh[semid], val)
            wd[semid] = val

    @staticmethod
    def _deps(reads, writes):
        toks = []
        for r in reads:
            toks.append(r.w)
        for w in writes:
            toks.append(w.w)
            toks.extend(w.r.values())
        return toks

    @staticmethod
    def _commit(tok, reads, writes):
        for r in reads:
            r.r[tok[0]] = tok
        for w in writes:
            w.w = tok
            w.r = {}

    def op(self, e, fn, reads=(), writes=()):
        self._wait(e, self._deps(reads, writes))
        ins = fn(self.E[e])
        self.cnt[e] += 1
        ins.then_inc(self.semh[self.esem[e]], 1)
        tok = (self.esem[e], self.cnt[e], e)
        self._commit(tok, reads, writes)
        self.n_ins += 1
        return tok

    def dma(self, q, out, in_, reads=(), writes=()):
        toks = self._deps(reads, writes)
        i = self.dnext[q]
        self.dnext[q] = (i + 1) % NDS
        semid = self.dsem[q][i]
        prev = self.dval[q][i]
        if prev > 0:
            toks.append((semid, prev, "dma"))
        self._wait(q, toks)
        ins = self.E[q].dma_start(out=out, in_=in_)
        self.dval[q][i] = prev + 16
        ins.then_inc(self.semh[semid], 16)
        tok = (semid, prev + 16, "dma")
        self._commit(tok, reads, writes)
        self.n_ins += 1
        return tok

    def mark(self, name):
        if not hasattr(self, "marks"):
            self.marks = []
        self.marks.append((name, self.cnt["pe"]))

    def barrier(self):
        toks = [(self.esem[e], self.cnt[e], e) for e in self.esem if self.cnt[e] > 0]
        for q in self.dsem:
            for i in range(NDS):
                if self.dval[q][i] > 0:
                    toks.append((self.dsem[q][i], self.dval[q][i], "dma"))
        for e in self.E:
            self._wait(e, [t for t in toks if not (t[2] == e)])
        for e in ("dve", "act", "pool"):
            self._wait(e, [t for t in toks if t[2] == e])

    def sb(self, st, name, shape, dtype):
        self.n_sb = getattr(self, "n_sb", 0) + 1
        t = st.enter_context(self.nc.sbuf_tensor("%s_u%d" % (name, self.n_sb), shape, dtype))
        return t, Res()


def fm(v, nch):
    return np.ascontiguousarray(np.asarray(v, np.float32).reshape(nch, 128).T)


def build_program(n_layers=DEPTH, debug_dump=False, mixers=True, mix_parts=("na", "s5")):
    kb = KB()
    nc = kb.nc
    ps = kb.ps

    def din(name, shape, dt=F32):
        return nc.dram_tensor(name, list(shape), dt, kind="ExternalInput").ap()

    def dscr(name, shape, dt):
        return nc.dram_tensor(name, list(shape), dt, kind="Internal").ap()

    xT_in = din("xT", [2, D, T])
    cT_in = din("cT", [128, KC, 3])
    ada_w = din("ada_w", [DEPTH, D, 6 * D])
    adab_in = din("adab", [128, DEPTH, 96])
    gmix_in = din("gmix", [128, DEPTH, KC])
    gffn_in = din("gffn", [128, DEPTH, KC])
    gfin_in = din("gfin", [128, KC])
    w_out = din("w_out", [DEPTH, D, D])
    w_up = din("w_up", [DEPTH, D, 2 * FF])
    w_down = din("w_down", [DEPTH, FF, D])
    dww_in = din("dww", [128, DEPTH, 3, 88])
    dwb_in = din("dwb", [128, DEPTH, 88])
    ident_in = din("ident", [128, 128])
    out_T = nc.dram_tensor("outT", [2, D, LAT], F32, kind="ExternalOutput").ap()

    XT = dscr("XT", [2, D, T], F32)
    YT = dscr("YT", [2, D, T], BF16)
    GT = dscr("GT", [2, FF, T], BF16)

    ex = ExitStack()
    ident_f, r_identf = kb.sb(ex, "ident_f", [128, 128], F32)
    ident_b, r_ident = kb.sb(ex, "ident_b", [128, 128], BF16)
    ones_b, r_ones = kb.sb(ex, "ones_b", [128, 128], BF16)
    mod_sb, r_mod = kb.sb(ex, "mod_sb", [128, DEPTH, 96, 3], F32)
    A1, r_A1 = kb.sb(ex, "A1", [128, DEPTH, KC, 3], F32)
    A2, r_A2 = kb.sb(ex, "A2", [128, DEPTH, KC, 3], F32)
    gfin, r_gfin = kb.sb(ex, "gfin", [128, KC], F32)
    dww, r_dww = kb.sb(ex, "dww", [128, DEPTH, 3, 88], F32)
    dwb, r_dwb = kb.sb(ex, "dwb", [128, DEPTH, 88], F32)
    zero_c, r_zero = kb.sb(ex, "zero_c", [128, 1], F32)

    kb.dma("sp", ident_f[:], ident_in, writes=[r_identf])
    kb.op("dve", lambda e: e.tensor_copy(out=ident_b[:], in_=ident_f[:]), reads=[r_identf], writes=[r_ident])
    kb.op("dve", lambda e: e.memset(ones_b[:], 1.0), writes=[r_ones])
    kb.op("dve", lambda e: e.memset(zero_c[:], 0.0), writes=[r_zero])
    kb.dma("sp", gfin[:], gfin_in, writes=[r_gfin])
    kb.dma("sp", dww[:], dww_in, writes=[r_dww])
    kb.dma("sp", dwb[:], dwb_in, writes=[r_dwb])

    def stage_S0():
        with ExitStack() as st:
            cs, r_cs = kb.sb(st, "cs", [128, KC, 3], F32)
            ssb, r_ssb = kb.sb(st, "ssb", [128, KC, 3], F32)
            adab, r_adab = kb.sb(st, "adab_sb", [128, DEPTH, 96], F32)
            gmix, r_gmix = kb.sb(st, "gmix_sb", [128, DEPTH, KC], F32)
            gffn, r_gffn = kb.sb(st, "gffn_sb", [128, DEPTH, KC], F32)
            wt = [kb.sb(st, "adaw%d" % i, [128, KC, 256], F32) for i in range(2)]
            emitters = []
            if mixers and "s5" in mix_parts:
                for i_ in range((n_layers + 1) // 2):
                    emitters += table_emitters(st, i_, 3)
            kb.dma("sp", cs[:], cT_in, writes=[r_cs])
            kb.dma("sp", adab[:], adab_in, writes=[r_adab])
            kb.dma("sp", gmix[:], gmix_in, writes=[r_gmix])
            kb.dma("sp", gffn[:], gffn_in, writes=[r_gffn])
            kb.op("act", lambda e: e.activation(out=ssb[:], in_=cs[:], func=AF.Silu), reads=[r_cs], writes=[r_ssb])
            it = 0
            for l in range(n_layers):
                pt, r_pt = ps[l % 2]
                for jt in range(48):
                    w_t, r_w = wt[it % 2]
                    it += 1
                    if it % 2 == 0 and emitters:
                        emitters.pop(0)()
                    src = ada_w[l].rearrange("(kc p) n -> p kc n", p=128)[:, :, jt * 256:(jt + 1) * 256]
                    kb.dma("sp", w_t[:], src, writes=[r_w])
                    for jj in range(2):
                        j = jt * 2 + jj
                        for kc in range(KC):
                            kb.op("pe", lambda e, kc=kc, jj=jj, j=j, w_t=w_t, pt=pt: e.matmul(
                                pt[:, 3 * j:3 * j + 3], lhsT=w_t[:, kc, jj * 128:(jj + 1) * 128], rhs=ssb[:, kc, :],
                                start=(kc == 0), stop=(kc == KC - 1)),
                                reads=[r_w, r_ssb], writes=[r_pt])
                kb.op("dve", lambda e, l=l, pt=pt: e.tensor_tensor(
                    out=mod_sb[:, l], in0=pt[:, 0:288].rearrange("p (j r) -> p j r", r=3),
                    in1=adab[:, l].unsqueeze(2).broadcast_to([128, 96, 3]), op=ALU.add),
                    reads=[r_pt, r_adab], writes=[r_mod])
                for (A, rA, g, rg, which) in ((A1, r_A1, gmix, r_gmix, 1), (A2, r_A2, gffn, r_gffn, 4)):
                    kb.op("dve", lambda e, A=A, which=which, l=l: e.tensor_scalar(
                        out=A[:, l], in0=mod_sb[:, l, which * 16:(which + 1) * 16, :], scalar1=1.0, scalar2=None,
                        op0=ALU.add), reads=[r_mod], writes=[rA])
                    kb.op("dve", lambda e, A=A, g=g, l=l: e.tensor_tensor(
                        out=A[:, l], in0=A[:, l], in1=g[:, l].unsqueeze(2).broadcast_to([128, KC, 3]), op=ALU.mult),
                        reads=[rA, rg], writes=[rA])
            while emitters:
                emitters.pop(0)()
            kb.barrier()


    def modv(l, which, kc, row):
        return mod_sb[:, l, which * 16 + kc, row:row + 1]

    TILES_ALL = [(0, 256)] + [(256 + 512 * i, 512) for i in range(4)]
    TILES_LAT = TILES_ALL[1:]

    class WStream:
        def __init__(self, st, name, kcn, ncols, nstage=2, nbf=2, engs=("act", "dve")):
            self.kcn, self.ncols = kcn, ncols
            self.engs = engs
            self.stage = [kb.sb(st, "%s_s%d" % (name, i), [128, kcn, ncols], F32) for i in range(nstage)]
            self.bf = [kb.sb(st, "%s_b%d" % (name, i), [128, kcn, ncols], BF16) for i in range(nbf)]
            self.i = 0
            self.j = 0

        def load(self, W, c0, ncols=None, cast_eng=None, perm=False):
            ncols = ncols or self.ncols
            s_t, r_s = self.stage[self.i % len(self.stage)]
            self.i += 1
            b_t, r_b = self.bf[self.j % len(self.bf)]
            self.j += 1
            src = W.rearrange("(kc p) n -> p kc n", p=128)[:, :, c0:c0 + ncols]
            kb.dma("sp", s_t[:, :, 0:ncols], src, writes=[r_s])
            if cast_eng is None:
                cast_eng = self.engs[self.j % len(self.engs)]
            if not perm:
                if cast_eng == "act":
                    kb.op("act", lambda e: e.activation(out=b_t[:, :, 0:ncols], in_=s_t[:, :, 0:ncols], func=AF.Copy),
                          reads=[r_s], writes=[r_b])
                else:
                    kb.op(cast_eng, lambda e: e.tensor_copy(out=b_t[:, :, 0:ncols], in_=s_t[:, :, 0:ncols]),
                          reads=[r_s], writes=[r_b])
            else:
                for g64 in range(ncols // 64):
                    for h in range(2):
                        o0_ = g64 * 64 + h * 32
                        i0_ = g64 * 64 + (1 - h) * 32
                        kb.op("dve", lambda e, o0_=o0_, i0_=i0_: e.tensor_copy(out=b_t[:, :, o0_:o0_ + 32], in_=s_t[:, :, i0_:i0_ + 32]),
                              reads=[r_s], writes=[r_b])
            return b_t, r_b

    def norm_stage(st_name, src, tiles, Afn, SHfn, hT, r_hT):
        with ExitStack() as st:
            xt = [kb.sb(st, "%s_x%d" % (st_name, i), [128, KC, 512], F32) for i in range(2)]
            sq = [kb.sb(st, "%s_q%d" % (st_name, i), [128, KC, 512], BF16) for i in range(1)]
            rt = [kb.sb(st, "%s_r%d" % (st_name, i), [128, 512], F32) for i in range(2)]
            srcv = src.rearrange("(kc p) t -> p kc t", p=128)
            for ti, (t0, n) in enumerate(tiles):
                is_ctx = t0 < CTX
                x_t, r_x = xt[ti % 2]
                q_t, r_q = sq[0]
                r_t, r_r = rt[ti % 2]
                p_t, r_p = ps[ti % 2]
                kb.dma("sp", x_t[:, :, 0:n], srcv[:, :, t0:t0 + n], writes=[r_x])
                kb.op("act", lambda e: e.activation(out=q_t[:, :, 0:n], in_=x_t[:, :, 0:n], func=AF.Square),
                      reads=[r_x], writes=[r_q])
                for kc in range(KC):
                    kb.op("pe", lambda e, kc=kc: e.matmul(p_t[:, 0:n], lhsT=ones_b[:], rhs=q_t[:, kc, 0:n],
                                                         start=(kc == 0), stop=(kc == KC - 1)),
                          reads=[r_q, r_ones], writes=[r_p])
                kb.op("dve", lambda e: e.tensor_scalar(out=r_t[:, 0:n], in0=p_t[:, 0:n], scalar1=1.0 / D,
                                                       scalar2=EPS, op0=ALU.mult, op1=ALU.add),
                      reads=[r_p], writes=[r_r])
                kb.op("act", lambda e: e.activation(out=r_t[:, 0:n], in_=r_t[:, 0:n], func=AF.Sqrt),
                      reads=[r_r], writes=[r_r])
                kb.op("dve", lambda e: e.reciprocal(out=r_t[:, 0:n], in_=r_t[:, 0:n]), reads=[r_r], writes=[r_r])
                kb.op("dve", lambda e: e.tensor_tensor(
                    out=x_t[:, :, 0:n], in0=x_t[:, :, 0:n],
                    in1=r_t[:, 0:n].unsqueeze(1).broadcast_to([128, KC, n]), op=ALU.mult),
                    reads=[r_x, r_r], writes=[r_x])
                for kc in range(KC):
                    a_ap = Afn(kc, is_ctx)
                    s_ap = SHfn(kc, is_ctx)
                    kb.op("act", lambda e, kc=kc, a_ap=a_ap, s_ap=s_ap: e.activation(
                        out=hT[:, kc, t0:t0 + n], in_=x_t[:, kc, 0:n], func=AF.Identity, scale=a_ap,
                        bias=(s_ap if s_ap is not None else zero_c[:])),
                        reads=[r_x, r_A1, r_A2, r_mod, r_gfin, r_zero], writes=[r_hT])
            kb.barrier()

    def linear_resid(st, name, b, act, r_act, kcn, W, tiles, gatefn, ws, tspan):
        tlo, thi = tspan
        nt = thi - tlo
        if isinstance(r_act, list):
            def res_for(kc):
                for (lo_, hi_, r_) in r_act:
                    if lo_ <= kc < hi_:
                        return r_
        else:
            def res_for(kc):
                return r_act
        xr = [kb.sb(st, "%s_xr%d" % (name, i), [128, nt], F32) for i in range(2)]
        nxt = ws.load(W, 0, 128)
        for blk in range(KC):
            w_b, r_w = nxt
            if blk + 1 < KC:
                nxt = ws.load(W, (blk + 1) * 128, 128)
            x_r, r_xr = xr[blk % 2]
            kb.dma("sp", x_r[:], XT[b, blk * 128:(blk + 1) * 128, tlo:thi], writes=[r_xr])
            for ti, (t0, n) in enumerate(tiles):
                p_t, r_p = ps[(blk * len(tiles) + ti) % 4]
                for kc in range(kcn):
                    kb.op("pe", lambda e, kc=kc, p_t=p_t, w_b=w_b, t0=t0, n=n: e.matmul(
                        p_t[:, 0:n], lhsT=w_b[:, kc, 0:128], rhs=act[:, kc, t0 - tlo:t0 - tlo + n],
                        start=(kc == 0), stop=(kc == kcn - 1)), reads=[r_w, res_for(kc)], writes=[r_p])
                is_ctx = t0 < CTX
                g_ap = gatefn(blk, is_ctx)
                kb.op("dve", lambda e, p_t=p_t, x_r=x_r, t0=t0, n=n, g_ap=g_ap: e.scalar_tensor_tensor(
                    out=x_r[:, t0 - tlo:t0 - tlo + n], in0=p_t[:, 0:n], scalar=g_ap,
                    in1=x_r[:, t0 - tlo:t0 - tlo + n], op0=ALU.mult, op1=ALU.add),
                    reads=[r_p, r_xr, r_mod], writes=[r_xr])
            kb.dma("act", XT[b, blk * 128:(blk + 1) * 128, tlo:thi], x_r[:], reads=[r_xr])

    def ffn_up(l, b, hT, r_hT, with_ctx):
        tiles = TILES_ALL if with_ctx else TILES_LAT
        segs = ([(0, CTX)] if with_ctx else []) + [(CTX, T)]
        tlo = 0 if with_ctx else CTX
        with ExitStack() as st:
            ws = WStream(st, "wup", KC, 128, nstage=2, nbf=4, engs=("act",))
            ub = [kb.sb(st, "ub%d" % i, [128, T], F32) for i in range(2)]
            cb = [kb.sb(st, "cb%d" % i, [128, T], F32) for i in range(2)]
            sg, r_sg = kb.sb(st, "sg", [128, T], F32)
            go = [kb.sb(st, "go%d" % i, [128, T], BF16) for i in range(2)]
            npair = FF // 128
            pidx = 0
            nxt = (ws.load(w_up[l], 0, 128), ws.load(w_up[l], FF, 128))
            for j in range(npair):
                (wg, r_wg), (wv, r_wv) = nxt
                if j + 1 < npair:
                    nxt = (ws.load(w_up[l], (j + 1) * 128, 128), ws.load(w_up[l], FF + (j + 1) * 128, 128))
                for jj in range(1):
                    cres = []
                    for hi, (w_b, r_w) in enumerate(((wg, r_wg), (wv, r_wv))):
                        fcol = j if hi == 0 else 44 + j
                        u_t, r_u = ub[hi]
                        c_t, r_c = cb[hi]
                        for ti, (t0, n) in enumerate(tiles):
                            p_t, r_p = ps[(ti + hi * len(tiles) + pidx) % 6]
                            for kc in range(KC):
                                kb.op("pe", lambda e, kc=kc, p_t=p_t, w_b=w_b, t0=t0, n=n, jj=jj: e.matmul(
                                    p_t[:, 0:n], lhsT=w_b[:, kc, jj * 128:(jj + 1) * 128], rhs=hT[:, kc, t0:t0 + n],
                                    start=(kc == 0), stop=(kc == KC - 1)), reads=[r_w, r_hT], writes=[r_p])
                            kb.op("act", lambda e, p_t=p_t, u_t=u_t, t0=t0, n=n: e.activation(
                                out=u_t[:, t0:t0 + n], in_=p_t[:, 0:n], func=AF.Copy), reads=[r_p], writes=[r_u])
                        kb.op("dve", lambda e, u_t=u_t, c_t=c_t, fcol=fcol: e.tensor_scalar(
                            out=c_t[:, tlo:T], in0=u_t[:, tlo:T], scalar1=dww[:, l, 1, fcol:fcol + 1],
                            scalar2=dwb[:, l, fcol:fcol + 1], op0=ALU.mult, op1=ALU.add),
                            reads=[r_u, r_dww, r_dwb], writes=[r_c])
                        for (s0, s1) in segs:
                            kb.op("dve", lambda e, u_t=u_t, c_t=c_t, fcol=fcol, s0=s0, s1=s1: e.scalar_tensor_tensor(
                                out=c_t[:, s0 + 1:s1], in0=u_t[:, s0:s1 - 1], scalar=dww[:, l, 0, fcol:fcol + 1],
                                in1=c_t[:, s0 + 1:s1], op0=ALU.mult, op1=ALU.add),
                                reads=[r_u, r_c, r_dww], writes=[r_c])
                            kb.op("dve", lambda e, u_t=u_t, c_t=c_t, fcol=fcol, s0=s0, s1=s1: e.scalar_tensor_tensor(
                                out=c_t[:, s0:s1 - 1], in0=u_t[:, s0 + 1:s1], scalar=dww[:, l, 2, fcol:fcol + 1],
                                in1=c_t[:, s0:s1 - 1], op0=ALU.mult, op1=ALU.add),
                                reads=[r_u, r_c, r_dww], writes=[r_c])
                        cres.append((c_t, r_c))
                    (cg, r_cg), (cv, r_cv) = cres
                    g_t, r_g = go[pidx % 2]
                    kb.op("act", lambda e, cg=cg: e.activation(out=sg[:, tlo:T], in_=cg[:, tlo:T], func=AF.Silu),
                          reads=[r_cg], writes=[r_sg])
                    kb.op("pool", lambda e, cv=cv, g_t=g_t: e.tensor_tensor(
                        out=g_t[:, tlo:T], in0=sg[:, tlo:T], in1=cv[:, tlo:T], op=ALU.mult),
                        reads=[r_sg, r_cv], writes=[r_g])
                    kb.dma("act", GT[b, j * 128:(j + 1) * 128, tlo:T], g_t[:, tlo:T], reads=[r_g])
                    pidx += 1
            kb.barrier()

    def ffn_down(l, b, with_ctx):
        tlo0 = 0 if with_ctx else CTX
        groups = [(tlo0, CTX + 1024), (CTX + 1024, T)]
        with ExitStack() as st:
            ws = WStream(st, "wdn", 44, 128, nstage=2, nbf=2)
            gsb, r_gsb = kb.sb(st, "gsb", [128, 44, 1280], BF16)
            gq = [(11 * q, 11 * (q + 1), Res()) for q in range(4)]
            for gi, (glo, ghi) in enumerate(groups):
                ng = ghi - glo
                for (lo_, hi_, r_q) in gq:
                    kb.dma("sp", gsb[:, lo_:hi_, 0:ng], GT[b].rearrange("(kc p) t -> p kc t", p=128)[:, lo_:hi_, glo:ghi],
                           writes=[r_q])
                tiles = []
                t = glo
                while t < ghi:
                    n = min(512, ghi - t)
                    if t < CTX:
                        n = min(n, CTX - t)
                    tiles.append((t, n))
                    t += n
                with ExitStack() as st2:
                    linear_resid(st2, "dn%d" % gi, b, gsb, gq, 44, w_down[l], tiles,
                                 lambda blk, is_ctx: modv(l, 5, blk, 2 if is_ctx else b), ws, (glo, ghi))
                    kb.barrier()

    ev_w_in = din("ev_w_in", [2, D, 4096])
    od_w_in = din("od_w_in", [2, D, 4608])
    w_glu = din("w_glu", [2, 1024, 1024])
    ropeC_in = din("ropeC", [128, LAT])
    ropeS_in = din("ropeS", [128, LAT])
    permM_in = din("permM", [128, 128])
    mprev_in = din("mprev", [128, 128])
    mnext_in = din("mnext", [128, 128])
    dlam_in = din("dlam", [2, 512])
    subg_in = din("subg", [2, 256])
    sink_in = din("sink", [2, 8])
    natb_in = din("natb", [2, 64, 8 * 15 * 64])
    namask_in = din("namask", [64, 64])
    lamA_re_in = din("lamA_re", [2, 128, 1024])
    lamA_im_in = din("lamA_im", [2, 128, 1024])
    dtA_in = din("dtA", [2, 128, 1024])
    bTre_in = din("bTre", [2, 128, 1024])
    bTim_in = din("bTim", [2, 128, 1024])
    lamB_re_in = din("lamB_re", [2, 128, 64])
    lamB_im_in = din("lamB_im", [2, 128, 64])
    dtB_in = din("dtB", [2, 128, 64])
    LC_in = din("LC", [2, 64, 128, 256])
    maskA_in = din("maskA", [128, 8])
    s5d_in = din("s5d", [128, 2, 8])
    ZT = dscr("ZT", [2, 36 * 128, T], BF16)
    VT = dscr("VT", [2, T, 1280], BF16)
    GAT = dscr("GAT", [2, 1024, T], BF16)
    ETAB = dscr("ETAB", [2, 64, 2, 128, T], BF16)
    SCALE = 128.0 ** -0.5
    psn = [0]

    def nps(lo=0, hi=8):
        psn[0] += 1
        return ps[lo + psn[0] % (hi - lo)]

    def proj_stage(b, W, fm_blocks, tm_groups, hT, r_hT, rope):
        with ExitStack() as st:
            ws = WStream(st, "wi", KC, 256, nstage=2, nbf=4)
            rows = [kb.sb(st, "zrow%d" % i, [128, T], BF16) for i in range(2)]
            vrow = [kb.sb(st, "vrow%d" % i, [128, 256], BF16) for i in range(2)]
            if rope:
                rC, r_rC = kb.sb(st, "ropeC", [128, LAT], F32)
                rS, r_rS = kb.sb(st, "ropeS", [128, LAT], F32)
                kb.dma("sp", rC[:], ropeC_in, writes=[r_rC])
                kb.dma("sp", rS[:], ropeS_in, writes=[r_rS])
                t1 = [kb.sb(st, "rt1_%d" % i, [128, 512], F32) for i in range(3)]
                t2 = [kb.sb(st, "rt2_%d" % i, [128, 512], F32) for i in range(2)]
                qbs = [kb.sb(st, "rqb_%d" % i, [128, 512], BF16) for i in range(3)]
                pm_f, r_pmf = kb.sb(st, "permM_f", [128, 128], F32)
                pm_b, r_pmb = kb.sb(st, "permM_b", [128, 128], BF16)
                kb.dma("sp", pm_f[:], permM_in, writes=[r_pmf])
                kb.op("dve", lambda e: e.tensor_copy(out=pm_b[:], in_=pm_f[:]), reads=[r_pmf], writes=[r_pmb])
            pending = []
            stores = []
            rcnt = [0]

            def ldblk(bi_):
                c0_, _, rp_ = fm_blocks[bi_]
                a_ = ws.load(W, c0_, 128)
                return a_, None
            nxt = ldblk(0)
            for bi, (c0, zblk, do_rope) in enumerate(fm_blocks):
                (w_b, r_w), wp_ = nxt
                if bi + 1 < len(fm_blocks):
                    nxt = ldblk(bi + 1)
                row, r_row = rows[bi % 2]
                for ti, (t0, n) in enumerate(TILES_ALL):
                    p_t, r_p = nps(0, 6)
                    for kc in range(KC):
                        kb.op("pe", lambda e, kc=kc: e.matmul(p_t[:, 0:n], lhsT=w_b[:, kc, 0:128], rhs=hT[:, kc, t0:t0 + n],
                                                             start=(kc == 0), stop=(kc == KC - 1)),
                              reads=[r_w, r_hT], writes=[r_p])
                    while pending:
                        pending.pop(0)()
                    while stores:
                        zb_, rw_, r_rw_ = stores.pop(0)
                        kb.dma("act", ZT[b, zb_ * 128:(zb_ + 1) * 128, :], rw_[:], reads=[r_rw_])
                    if do_rope and t0 >= CTX:
                        rc_ = rcnt[0]
                        rcnt[0] += 1
                        a_t, r_a = t1[rc_ % 3]
                        q_b, r_qb = qbs[rc_ % 3]
                        kb.op("act", lambda e: e.activation(out=q_b[:, 0:n], in_=p_t[:, 0:n], func=AF.Copy), reads=[r_p], writes=[r_qb])
                        kb.op("dve", lambda e: e.tensor_tensor(out=a_t[:, 0:n], in0=p_t[:, 0:n], in1=rC[:, t0 - CTX:t0 - CTX + n],
                                                               op=ALU.mult), reads=[r_p, r_rC, r_qb], writes=[r_a])

                        def fin(a_t=a_t, r_a=r_a, q_b=q_b, r_qb=r_qb, t0=t0, n=n, row=row, r_row=r_row, rc_=rc_):
                            p2, r_p2 = ps[6 + rc_ % 2]
                            b_t, r_b = t2[rc_ % 2]
                            kb.op("pe", lambda e: e.matmul(p2[:, 0:n], lhsT=pm_b[:], rhs=q_b[:, 0:n], start=True, stop=True),
                                  reads=[r_pmb, r_qb], writes=[r_p2])
                            kb.op("dve", lambda e: e.tensor_tensor(out=b_t[:, 0:n], in0=p2[:, 0:n], in1=rS[:, t0 - CTX:t0 - CTX + n],
                                                                   op=ALU.mult), reads=[r_p2, r_rS], writes=[r_b])
                            kb.op("pool", lambda e: e.tensor_tensor(out=row[:, t0:t0 + n], in0=a_t[:, 0:n], in1=b_t[:, 0:n],
                                                                    op=ALU.add), reads=[r_a, r_b], writes=[r_row])
                        pending.append(fin)
                        if not DEFER_ROPE:
                            while pending:
                                pending.pop(0)()
                    else:
                        kb.op("act", lambda e: e.activation(out=row[:, t0:t0 + n], in_=p_t[:, 0:n], func=AF.Copy),
                              reads=[r_p], writes=[r_row])
                if bi + 1 == len(fm_blocks):
                    while pending:
                        pending.pop(0)()
                    kb.dma("act", ZT[b, zblk * 128:(zblk + 1) * 128, :], row[:], reads=[r_row])
                else:
                    stores.append((zblk, row, r_row))
            vi = 0
            for (c0, ncols, vc0) in tm_groups:
                for cc in range(0, ncols, 256):
                    w_b, r_w = ws.load(W, c0 + cc, 256)
                    for tt in range(T // 128):
                        p_t, r_p = nps(0, 6)
                        for kc in range(KC):
                            kb.op("pe", lambda e, kc=kc: e.matmul(p_t[:, 0:256], lhsT=hT[:, kc, tt * 128:(tt + 1) * 128],
                                                                 rhs=w_b[:, kc, 0:256], start=(kc == 0), stop=(kc == KC - 1)),
                                  reads=[r_w, r_hT], writes=[r_p])
                        v_t, r_v = vrow[vi % 2]
                        vi += 1
                        kb.op("act", lambda e: e.activation(out=v_t[:], in_=p_t[:, 0:256], func=AF.Copy),
                              reads=[r_p], writes=[r_v])
                        kb.dma("act", VT[b, tt * 128:(tt + 1) * 128, vc0 + cc:vc0 + cc + 256], v_t[:], reads=[r_v])
            kb.barrier()

    def load_v(st, name, b, vc0, dv):
        v, r_v = kb.sb(st, name, [128, T // 128, dv + 1], BF16)
        kb.op("pool", lambda e: e.memset(v[:, :, dv:dv + 1], 1.0), writes=[r_v])
        kb.dma("sp", v[:, :, 0:dv], VT[b].rearrange("(n p) c -> p n c", p=128)[:, :, vc0:vc0 + dv], writes=[r_v])
        return v, r_v

    def finish_o(o_ps, r_o, dv, dst, r_dst, extra=None, r_extra=None, scr=None):
        rc, r_rc = scr
        if extra is not None:
            kb.op("dve", lambda e: e.tensor_tensor(out=rc[:], in0=o_ps[:, dv:dv + 1], in1=extra, op=ALU.add),
                  reads=[r_o, r_extra], writes=[r_rc])
            kb.op("dve", lambda e: e.reciprocal(out=rc[:], in_=rc[:]), reads=[r_rc], writes=[r_rc])
        else:
            kb.op("dve", lambda e: e.reciprocal(out=rc[:], in_=o_ps[:, dv:dv + 1]), reads=[r_o], writes=[r_rc])
        kb.op("dve", lambda e: e.tensor_scalar(out=dst, in0=o_ps[:, 0:dv], scalar1=rc[:, 0:1], scalar2=None, op0=ALU.mult),
              reads=[r_o, r_rc], writes=[r_dst])

    def transpose_out(src_bf, r_src, nq, yrow, r_yrow, col0, psb):
        p_t, r_p = psb
        pv = p_t[:].bitcast(BF16)
        kb.op("pe", lambda e: e.transpose(out=pv[:, 0:nq], in_=src_bf, identity=ident_b[0:nq, 0:nq]),
              reads=[r_src, r_ident], writes=[r_p])
        kb.op("act", lambda e: e.activation(out=yrow[:, col0:col0 + nq], in_=pv[:, 0:nq], func=AF.Copy),
              reads=[r_p], writes=[r_yrow])

    def odd_core(l, b, with_ctx):
        i = l // 2
        lam_init = 0.8 - 0.6 * math.exp(-0.3 * l)
        with ExitStack() as st:
            dl, r_dl = kb.sb(st, "dl", [128, 512], F32)
            sg, r_sg = kb.sb(st, "subg", [128, 256], F32)
            sk, r_sk = kb.sb(st, "sink", [128, 8], F32)
            lamt, r_lam = kb.sb(st, "lamt", [128, 4], F32)
            kb.dma("sp", dl[:], dlam_in[i:i + 1, :].partition_broadcast(128), writes=[r_dl])
            kb.dma("sp", sg[:], subg_in[i:i + 1, :].partition_broadcast(128), writes=[r_sg])
            kb.dma("sp", sk[:], sink_in[i:i + 1, :].partition_broadcast(128), writes=[r_sk])
            kb.op("act", lambda e: e.activation(out=sk[:], in_=sk[:], func=AF.Exp), reads=[r_sk], writes=[r_sk])
            kb.op("dve", lambda e: e.tensor_scalar(out=sg[:], in0=sg[:], scalar1=1.0 - lam_init, scalar2=None, op0=ALU.mult),
                  reads=[r_sg], writes=[r_sg])
            dlv = dl[:].rearrange("p (a d) -> p a d", d=128)
            for k in range(2):
                kb.op("dve", lambda e, k=k: e.tensor_tensor(out=dlv[:, 2 * k, :], in0=dlv[:, 2 * k, :], in1=dlv[:, 2 * k + 1, :],
                                                           op=ALU.mult), reads=[r_dl], writes=[r_dl])
                kb.op("act", lambda e, k=k: e.activation(out=dlv[:, 2 * k + 1, :], in_=dlv[:, 2 * k, :], func=AF.Identity,
                                                        bias=zero_c[:], accum_out=lamt[:, k:k + 1]),
                      reads=[r_dl, r_zero], writes=[r_dl, r_lam])
            kb.op("act", lambda e: e.activation(out=lamt[:, 0:2], in_=lamt[:, 0:2], func=AF.Exp), reads=[r_lam], writes=[r_lam])
            kb.op("dve", lambda e: e.tensor_tensor(out=lamt[:, 2:3], in0=lamt[:, 1:2], in1=lamt[:, 0:1], op=ALU.subtract),
                  reads=[r_lam], writes=[r_lam])
            kb.op("dve", lambda e: e.tensor_scalar(out=lamt[:, 2:3], in0=lamt[:, 2:3], scalar1=-lam_init, scalar2=None,
                                                   op0=ALU.add), reads=[r_lam], writes=[r_lam])
            neglam = lamt[:, 2:3]

            mp_f, r_mpf = kb.sb(st, "mp_f", [128, 2, 128], F32)
            mk, r_mk = kb.sb(st, "mk", [128, 2, 128], BF16)
            kb.dma("sp", mp_f[:, 0, :], mprev_in, writes=[r_mpf])
            kb.dma("sp", mp_f[:, 1, :], mnext_in, writes=[r_mpf])
            kb.op("dve", lambda e: e.tensor_copy(out=mk[:], in_=mp_f[:]), reads=[r_mpf], writes=[r_mk])

            rcs = [kb.sb(st, "rc%d" % j, [128, 1], F32) for j in range(4)]
            ssq = [kb.sb(st, "ssq%d" % j, [128, 1], F32) for j in range(2)]
            pts = [kb.sb(st, "pt%d" % j, [128, 512], BF16) for j in range(4)]
            with ExitStack() as s2:
                kt = [kb.sb(s2, "kt%d" % c, [128, T], BF16) for c in range(2)]
                qt = [kb.sb(s2, "qt%d" % c, [128, T], BF16) for c in range(2)]
                yrow = [kb.sb(s2, "yr%d" % j, [128, T], BF16) for j in range(2)]
                o0, r_o0 = kb.sb(s2, "o0", [128, 4, 256], F32)
                o1q, r_o1q = kb.sb(s2, "o1q", [128, 4, 256], F32)
                ybq, r_ybq = kb.sb(s2, "ybq", [128, 4, 256], BF16)
                sq4, r_sq4 = kb.sb(s2, "sq4", [128, 4], F32)
                junk, r_junk = kb.sb(s2, "junk", [128, 256], F32)
                qgroups = ([(0, 256, 2)] if with_ctx else []) + [(CTX + 512 * g, 512, 18) for g in range(4)]
                for h in range(4):
                    v, r_v = load_v(s2, "vC", b, 256 * h, 256) if h == 0 else (v, r_v)
                    if h > 0:
                        kb.dma("sp", v[:, :, 0:256], VT[b].rearrange("(n p) c -> p n c", p=128)[:, :, 256 * h:256 * h + 256],
                               writes=[r_v])
                    for c in range(2):
                        kb.dma("sp", kt[c][0][:], ZT[b, (8 + 2 * h + c) * 128:(9 + 2 * h + c) * 128, :], writes=[kt[c][1]])
                        kb.dma("sp", qt[c][0][:], ZT[b, (2 * h + c) * 128:(2 * h + c + 1) * 128, :], writes=[qt[c][1]])
                    if not with_ctx:
                        for j in range(2):
                            kb.op("pool", lambda e, j=j: e.memset(yrow[j][0][:, 0:CTX], 0.0), writes=[yrow[j][1]])
                    for (q0, nq, nkb) in qgroups:
                        nqs = nq // 128
                        for c in range(2):
                            obanks = [ps[2 + j] for j in range(nqs)]
                            sbanks = (ps[0], ps[1], ps[6])

                            def s_mm(kk):
                                s_t_, r_s_ = sbanks[kk % 3]
                                kb.op("pe", lambda e: e.matmul(s_t_[:, 0:nq], lhsT=kt[c][0][:, kk * 128:(kk + 1) * 128],
                                                               rhs=qt[c][0][:, q0:q0 + nq], start=True, stop=True),
                                      reads=[kt[c][1], qt[c][1]], writes=[r_s_])
                            s_mm(0)
                            if nkb > 1:
                                s_mm(1)
                            for kbi in range(nkb):
                                s_t, r_s = sbanks[kbi % 3]
                                p_t, r_p = pts[kbi % 4]
                                kb.op("act", lambda e: e.activation(out=p_t[:, 0:nq], in_=s_t[:, 0:nq], func=AF.Exp, scale=SCALE),
                                      reads=[r_s], writes=[r_p])
                                if kbi + 2 < nkb:
                                    s_mm(kbi + 2)
                                for qs in range(nqs):
                                    o_t, r_o = obanks[qs]
                                    kb.op("pe", lambda e, qs=qs, o_t=o_t: e.matmul(
                                        o_t[:, 0:257], lhsT=p_t[:, qs * 128:(qs + 1) * 128], rhs=v[:, kbi, 0:257],
                                        start=(kbi == 0), stop=(kbi == nkb - 1)), reads=[r_p, r_v], writes=[r_o])
                            if c == 0:
                                for qs in range(nqs):
                                    o_t, r_o = obanks[qs]
                                    finish_o(o_t, r_o, 256, o0[:, qs, :], r_o0, scr=rcs[qs % 4])
                            else:
                                for qs in range(nqs):
                                    o_t, r_o = obanks[qs]
                                    finish_o(o_t, r_o, 256, o1q[:, qs, :], r_o1q, scr=rcs[qs % 4])
                                kb.op("dve", lambda e: e.scalar_tensor_tensor(out=o1q[:, 0:nqs, :], in0=o1q[:, 0:nqs, :], scalar=neglam,
                                                                              in1=o0[:, 0:nqs, :], op0=ALU.mult, op1=ALU.add),
                                      reads=[r_o1q, r_o0, r_lam], writes=[r_o1q])
                                for qs in range(nqs):
                                    kb.op("act", lambda e, qs=qs: e.activation(out=junk[:], in_=o1q[:, qs, :], func=AF.Square,
                                                                              accum_out=sq4[:, qs:qs + 1]),
                                          reads=[r_o1q], writes=[r_junk, r_sq4])
                                kb.op("dve", lambda e: e.tensor_scalar(out=sq4[:, 0:nqs], in0=sq4[:, 0:nqs], scalar1=1.0 / 256, scalar2=EPS,
                                                                       op0=ALU.mult, op1=ALU.add), reads=[r_sq4], writes=[r_sq4])
                                kb.op("act", lambda e: e.activation(out=sq4[:, 0:nqs], in_=sq4[:, 0:nqs], func=AF.Sqrt), reads=[r_sq4], writes=[r_sq4])
                                kb.op("dve", lambda e: e.reciprocal(out=sq4[:, 0:nqs], in_=sq4[:, 0:nqs]), reads=[r_sq4], writes=[r_sq4])
                                for qs in range(nqs):
                                    kb.op("dve", lambda e, qs=qs: e.scalar_tensor_tensor(out=ybq[:, qs, :], in0=o1q[:, qs, :], scalar=sq4[:, qs:qs + 1],
                                                                                        in1=sg[:], op0=ALU.mult, op1=ALU.mult),
                                          reads=[r_o1q, r_sq4, r_sg], writes=[r_ybq])
                                for qs in range(nqs):
                                    for j in range(2):
                                        transpose_out(ybq[:, qs, j * 128:(j + 1) * 128], r_ybq, 128, yrow[j][0], yrow[j][1],
                                                      q0 + qs * 128, ps[7])
                    for j in range(2):
                        kb.dma("act", YT[b, (2 * h + j) * 128:(2 * h + j + 1) * 128, :], yrow[j][0][:], reads=[yrow[j][1]])
                kb.barrier()
            kb.mark("  swa")
            with ExitStack() as s2:
                ktd, r_ktd = kb.sb(s2, "ktd", [128, T], BF16)
                qtd, r_qtd = kb.sb(s2, "qtd", [128, 4, T], BF16)
                yrd = [kb.sb(s2, "yrd%d" % j, [128, T], BF16) for j in range(4)]
                ofl = [kb.sb(s2, "ofl%d" % j, [128, 128], F32) for j in range(2)]
                obf = [kb.sb(s2, "obf%d" % j, [128, 128], BF16) for j in range(2)]
                vd = None
                cnt = 0
                for g in range(2):
                    if vd is None:
                        vd, r_vd = load_v(s2, "vD", b, 1024 + 128 * g, 128)
                    else:
                        kb.dma("sp", vd[:, :, 0:128], VT[b].rearrange("(n p) c -> p n c", p=128)[:, :, 1024 + 128 * g:1152 + 128 * g],
                               writes=[r_vd])
                    kb.dma("sp", ktd[:], ZT[b, (32 + g) * 128:(33 + g) * 128, :], writes=[r_ktd])
                    for j in range(4):
                        kb.dma("sp", qtd[:, j, :], ZT[b, (24 + 4 * g + j) * 128:(25 + 4 * g + j) * 128, :], writes=[r_qtd])
                    if not with_ctx:
                        for j in range(4):
                            kb.op("pool", lambda e, j=j: e.memset(yrd[j][0][:, 0:CTX], 0.0), writes=[yrd[j][1]])
                    qblocks = ([(-2,), (-1,)] if with_ctx else []) + [(n,) for n in range(16)]
                    for (n,) in qblocks:
                        if n < 0:
                            q0 = (n + 2) * 128
                            kbl = [(0, None), (1, None)]
                        else:
                            q0 = CTX + 128 * n
                            kbl = [(0, None), (1, None)]
                            if n > 0:
                                kbl.append((2 + n - 1, 0))
                            kbl.append((2 + n, None))
                            if n < 15:
                                kbl.append((2 + n + 1, 1))
                        def sw_mm(kk):
                            s_t_, r_s_ = ps[kk % 2]
                            kbi_ = kbl[kk][0]
                            for j in range(4):
                                kb.op("pe", lambda e, j=j: e.matmul(s_t_[:, j * 128:(j + 1) * 128],
                                                                   lhsT=ktd[:, kbi_ * 128:(kbi_ + 1) * 128], rhs=qtd[:, j, q0:q0 + 128],
                                                                   start=True, stop=True), reads=[r_ktd, r_qtd], writes=[r_s_])
                        sw_mm(0)
                        for ki, (kbi, msk) in enumerate(kbl):
                            s_t, r_s = ps[ki % 2]
                            p_t, r_p = pts[ki % 3]
                            kb.op("act", lambda e: e.activation(out=p_t[:], in_=s_t[:], func=AF.Exp, scale=SCALE),
                                  reads=[r_s], writes=[r_p])
                            if msk is not None:
                                pv = p_t[:].rearrange("p (j q) -> p j q", q=128)
                                kb.op("dve", lambda e: e.tensor_tensor(out=pv, in0=pv, in1=mk[:, msk, :].unsqueeze(1).broadcast_to([128, 4, 128]),
                                                                       op=ALU.mult), reads=[r_p, r_mk], writes=[r_p])
                            if ki + 1 < len(kbl):
                                sw_mm(ki + 1)
                            for j in range(4):
                                o_t, r_o = ps[2 + j]
                                kb.op("pe", lambda e, j=j, o_t=o_t: e.matmul(o_t[:, 0:129], lhsT=p_t[:, j * 128:(j + 1) * 128],
                                                                            rhs=vd[:, kbi, 0:129], start=(ki == 0),
                                                                            stop=(ki == len(kbl) - 1)), reads=[r_p, r_vd], writes=[r_o])
                        for j in range(4):
                            o_t, r_o = ps[2 + j]
                            of, r_of = ofl[cnt % 2]
                            ob, r_ob = obf[cnt % 2]
                            finish_o(o_t, r_o, 128, ob[:], r_ob, extra=sk[:, 4 * g + j:4 * g + j + 1], r_extra=r_sk, scr=rcs[cnt % 4])
                            transpose_out(ob[:], r_ob, 128, yrd[j][0], yrd[j][1], q0, ps[6 + cnt % 2])
                            cnt += 1
                    for j in range(4):
                        kb.dma("act", YT[b, (8 + 4 * g + j) * 128:(9 + 4 * g + j) * 128, :], yrd[j][0][:], reads=[yrd[j][1]])
                kb.barrier()

    def odd_proj(l, b, hT, r_hT):
        i = l // 2
        fmb = [(128 * k, k, True) for k in range(16)]
        fmb += [(3072 + 128 * k, 24 + k, True) for k in range(8)]
        fmb += [(4096 + 128 * k, 32 + k, True) for k in range(2)]
        tmg = [(2048, 1024, 0), (4352, 256, 1024)]
        proj_stage(b, od_w_in[i], fmb, tmg, hT, r_hT, rope=True)
    def even_proj(l, b, hT, r_hT):
        i = l // 2
        fmb = [(128 * k, k, False) for k in range(24)]
        tmg = [(3072, 1024, 0)]
        proj_stage(b, ev_w_in[i], fmb, tmg, hT, r_hT, rope=False)

    def na_core(l, b, with_ctx):
        i = l // 2
        with ExitStack() as st:
            tb, r_tb = kb.sb(st, "natb", [64, 120, 64], F32)
            nm, r_nm = kb.sb(st, "namask", [64, 64], F32)
            kb.dma("sp", tb[:], natb_in[i].rearrange("k (a q) -> k a q", q=64), writes=[r_tb])
            kb.dma("sp", nm[:], namask_in, writes=[r_nm])
            kb.op("dve", lambda e: e.tensor_tensor(out=tb[:], in0=tb[:], in1=nm[:].unsqueeze(1).broadcast_to([64, 120, 64]),
                                                   op=ALU.add), reads=[r_tb, r_nm], writes=[r_tb])
            kt, r_kt = kb.sb(st, "nkt", [128, T], BF16)
            qt, r_qt = kb.sb(st, "nqt", [128, T], BF16)
            v64, r_v64 = kb.sb(st, "v64", [64, 36, 129], BF16)
            vc, r_vc = kb.sb(st, "vc128", [128, 2, 129], BF16)
            yrow, r_yrow = kb.sb(st, "nyrow", [128, T], BF16)
            sbf = [kb.sb(st, "nsb%d" % j, [64, 512], F32) for j in range(2)]
            ptl = [kb.sb(st, "nptl%d" % j, [64, 512], BF16) for j in range(2)]
            ptc = [kb.sb(st, "nptc%d" % j, [128, 128], BF16) for j in range(2)]
            ofl = [kb.sb(st, "nof%d" % j, [128, 128], F32) for j in range(2)]
            obf = [kb.sb(st, "nob%d" % j, [128, 128], BF16) for j in range(2)]
            rcs = [kb.sb(st, "nrc%d" % j, [128, 1], F32) for j in range(2)]
            pccs = [kb.sb(st, "npcc%d" % j, [128, 256], BF16) for j in range(2)]
            kb.op("pool", lambda e: e.memset(v64[:, :, 128:129], 1.0), writes=[r_v64])
            kb.op("pool", lambda e: e.memset(vc[:, :, 128:129], 1.0), writes=[r_vc])
            cnt = 0
            for h in range(8):
                kb.dma("sp", kt[:], ZT[b, (16 + h) * 128:(17 + h) * 128, :], writes=[r_kt])
                kb.dma("sp", qt[:], ZT[b, (8 + h) * 128:(9 + h) * 128, :], writes=[r_qt])
                kb.dma("sp", v64[:, :, 0:128], VT[b].rearrange("(n p) c -> p n c", p=64)[:, :, 128 * h:128 * h + 128], writes=[r_v64])
                kb.dma("sp", vc[:, :, 0:128], VT[b, 0:CTX, :].rearrange("(n p) c -> p n c", p=128)[:, :, 128 * h:128 * h + 128],
                       writes=[r_vc])
                if with_ctx:
                    for qb in range(2):
                        s_t, r_s = ps[cnt % 2]
                        p_t, r_p = ptc[cnt % 2]
                        o_t, r_o = ps[2 + cnt % 2]
                        of, r_of = ofl[cnt % 2]
                        ob, r_ob = obf[cnt % 2]
                        for c in range(2):
                            kb.op("pe", lambda e, c=c: e.matmul(s_t[:, c * 128:(c + 1) * 128], lhsT=kt[:, c * 128:(c + 1) * 128],
                                                               rhs=qt[:, qb * 128:(qb + 1) * 128], start=True, stop=True),
                                  reads=[r_kt, r_qt], writes=[r_s])
                        pcc, r_pcc = pccs[cnt % 2]
                        kb.op("act", lambda e: e.activation(out=pcc[:], in_=s_t[:, 0:256], func=AF.Exp, scale=SCALE),
                              reads=[r_s], writes=[r_pcc])
                        for c in range(2):
                            kb.op("pe", lambda e, c=c: e.matmul(o_t[:, 0:129], lhsT=pcc[:, c * 128:(c + 1) * 128], rhs=vc[:, c, :],
                                                               start=(c == 0), stop=(c == 1)), reads=[r_pcc, r_vc], writes=[r_o])
                        finish_o(o_t, r_o, 128, ob[:], r_ob, scr=rcs[cnt % 2])
                        transpose_out(ob[:], r_ob, 128, yrow, r_yrow, qb * 128, ps[6 + cnt % 2])
                        cnt += 1
                else:
                    kb.op("pool", lambda e: e.memset(yrow[:, 0:CTX], 0.0), writes=[r_yrow])
                base = cnt

                def na_bufs(r):
                    c_ = base + r
                    return (ps[c_ % 2], ps[4 + c_ % 2], ps[2 + c_ % 2], sbf[c_ % 2], ptl[c_ % 2], ptc[c_ % 2],
                            obf[c_ % 2], rcs[c_ % 2], ps[6 + c_ % 2])

                def na_s1(r):
                    rs = min(max(r - 4, 0), 24)
                    off = rs - r + 7
                    q0 = CTX + 64 * r
                    (s_t, r_s), (s2_t, r_s2), _, (sb_t, r_sb), (pl, r_pl), (pc, r_pc), _, _, _ = na_bufs(r)
                    for j in range(8):
                        k0 = CTX + 64 * (rs + j)
                        kb.op("pe", lambda e, j=j, k0=k0: e.matmul(s_t[0:64, j * 64:(j + 1) * 64], lhsT=kt[:, k0:k0 + 64],
                                                                  rhs=qt[:, q0:q0 + 64], start=True, stop=True),
                              reads=[r_kt, r_qt], writes=[r_s])
                    for c in range(2):
                        kb.op("pe", lambda e, c=c: e.matmul(s2_t[:, c * 64:(c + 1) * 64], lhsT=kt[:, c * 128:(c + 1) * 128],
                                                           rhs=qt[:, q0:q0 + 64], start=True, stop=True),
                              reads=[r_kt, r_qt], writes=[r_s2])
                    kb.op("dve", lambda e: e.scalar_tensor_tensor(
                        out=sb_t[:].rearrange("p (j q) -> p j q", q=64), in0=s_t[0:64, 0:512].rearrange("p (j q) -> p j q", q=64), scalar=SCALE,
                        in1=tb[:, h * 15 + off:h * 15 + off + 8, :], op0=ALU.mult, op1=ALU.add),
                        reads=[r_s, r_tb], writes=[r_sb])
                    kb.op("act", lambda e: e.activation(out=pl[:], in_=sb_t[:], func=AF.Exp), reads=[r_sb], writes=[r_pl])
                    kb.op("act", lambda e: e.activation(out=pc[:], in_=s2_t[:, 0:128], func=AF.Exp, scale=SCALE),
                          reads=[r_s2], writes=[r_pc])

                def na_s2(r):
                    rs = min(max(r - 4, 0), 24)
                    _, _, (o_t, r_o), _, (pl, r_pl), (pc, r_pc), (ob, r_ob), (rc, r_rc), _ = na_bufs(r)
                    for j in range(8):
                        kb.op("pe", lambda e, j=j: e.matmul(o_t[0:64, 0:129], lhsT=pl[:, j * 64:(j + 1) * 64],
                                                           rhs=v64[:, 4 + rs + j, :], start=(j == 0), stop=False),
                              reads=[r_pl, r_v64], writes=[r_o])
                    for c in range(2):
                        kb.op("pe", lambda e, c=c: e.matmul(o_t[0:64, 0:129], lhsT=pc[:, c * 64:(c + 1) * 64], rhs=vc[:, c, :],
                                                           start=False, stop=(c == 1)), reads=[r_pc, r_vc], writes=[r_o])
                    kb.op("dve", lambda e: e.reciprocal(out=rc[0:64, :], in_=o_t[0:64, 128:129]), reads=[r_o], writes=[r_rc])
                    kb.op("dve", lambda e: e.tensor_scalar(out=ob[0:64, :], in0=o_t[0:64, 0:128], scalar1=rc[0:64, 0:1],
                                                           scalar2=None, op0=ALU.mult), reads=[r_o, r_rc], writes=[r_ob])

                def na_s3(r):
                    q0 = CTX + 64 * r
                    _, _, _, _, _, _, (ob, r_ob), _, pst = na_bufs(r)
                    transpose_out(ob[0:64, :], r_ob, 64, yrow, r_yrow, q0, pst)

                na_s1(0)
                for r in range(32):
                    if r + 1 < 32:
                        na_s1(r + 1)
                    na_s2(r)
                    if r >= 1:
                        na_s3(r - 1)
                na_s3(31)
                cnt += 32
                kb.dma("act", YT[b, (8 + h) * 128:(9 + h) * 128, :], yrow[:], reads=[r_yrow])
            kb.barrier()

    def sincos(st, name, th, r_th, shape, s_out, c_out, r_s, r_c):
        n = shape[1]
        v, r_v = kb.sb(st, name + "_v", shape, F32)
        ki, r_ki = kb.sb(st, name + "_ki", shape, mybir.dt.int32)
        kf, r_kf = kb.sb(st, name + "_kf", shape, F32)
        m, r_m = kb.sb(st, name + "_m", shape, F32)
        for (dst, r_dst, shift) in ((s_out, r_s, 0.0), (c_out, r_c, 0.25)):
            kb.op("dve", lambda e: e.tensor_scalar(out=v[:], in0=th, scalar1=1.0 / (2 * math.pi), scalar2=shift,
                                                   op0=ALU.mult, op1=ALU.add), reads=[r_th], writes=[r_v])
            kb.op("dve", lambda e: e.tensor_copy(out=ki[:], in_=v[:]), reads=[r_v], writes=[r_ki])
            kb.op("dve", lambda e: e.tensor_copy(out=kf[:], in_=ki[:]), reads=[r_ki], writes=[r_kf])
            kb.op("dve", lambda e: e.tensor_tensor(out=v[:], in0=v[:], in1=kf[:], op=ALU.subtract), reads=[r_v, r_kf], writes=[r_v])
            kb.op("dve", lambda e: e.tensor_scalar(out=m[:], in0=v[:], scalar1=0.5, scalar2=None, op0=ALU.is_gt),
                  reads=[r_v], writes=[r_m])
            kb.op("dve", lambda e: e.tensor_tensor(out=v[:], in0=v[:], in1=m[:], op=ALU.subtract), reads=[r_v, r_m], writes=[r_v])
            kb.op("dve", lambda e: e.tensor_scalar(out=m[:], in0=v[:], scalar1=-0.5, scalar2=None, op0=ALU.is_lt),
                  reads=[r_v], writes=[r_m])
            kb.op("dve", lambda e: e.tensor_tensor(out=v[:], in0=v[:], in1=m[:], op=ALU.add), reads=[r_v, r_m], writes=[r_v])
            kb.op("act", lambda e: e.activation(out=dst, in_=v[:], func=AF.Sin, scale=2 * math.pi), reads=[r_v], writes=[r_dst])

    tshare = {}

    def table_emitters(st, i, NCH):
        if True:
            c1B, r_c1B = kb.sb(st, "tc1B%d" % i, [128, 64], F32)
            s1B, r_s1B = kb.sb(st, "ts1B%d" % i, [128, 64], F32)
            lim, r_lim = kb.sb(st, "tlim%d" % i, [128, 64], F32)
            dt, r_dt = kb.sb(st, "tdt%d" % i, [128, 64], F32)
            th, r_th = kb.sb(st, "tth%d" % i, [128, 64], F32)
            kb.dma("sp", lim[:], lamB_im_in[i], writes=[r_lim])
            kb.dma("sp", dt[:], dtB_in[i], writes=[r_dt])
            kb.op("act", lambda e: e.activation(out=dt[:], in_=dt[:], func=AF.Exp), reads=[r_dt], writes=[r_dt])
            kb.op("dve", lambda e: e.tensor_tensor(out=th[:], in0=lim[:], in1=dt[:], op=ALU.mult), reads=[r_lim, r_dt], writes=[r_th])
            sincos(st, "tscB%d" % i, th[:], r_th, [128, 64], s1B[:], c1B[:], r_s1B, r_c1B)
            ems = []
            if "chains" not in tshare:
                tshare["chains"] = [(kb.sb(st, "tEc%d" % c, [128, T], F32), kb.sb(st, "tEs%d" % c, [128, T], F32),
                                     [kb.sb(st, "ttsc%d_%d" % (c, j), [128, 1024], F32) for j in range(4)]) for c in range(NCH)]
                tshare["ebufs"] = [(kb.sb(st, "tEbc%d" % c, [128, T], BF16), kb.sb(st, "tEbs%d" % c, [128, T], BF16)) for c in range(NCH)]
            chains = tshare["chains"]
            ebufs = tshare["ebufs"]
            def grp(g0):
                NC_ = min(NCH, 64 - g0)
                for c in range(NC_):
                    col = g0 + c
                    (Ec, r_Ec), (Es, r_Es), tsc = chains[c]
                    kb.op("pool", lambda e: e.memset(Ec[:, 0:1], 1.0), writes=[r_Ec])
                    kb.op("pool", lambda e: e.memset(Es[:, 0:1], 0.0), writes=[r_Es])
                    kb.op("act", lambda e: e.activation(out=Ec[:, 1:2], in_=c1B[:, col:col + 1], func=AF.Copy), reads=[r_c1B], writes=[r_Ec])
                    kb.op("act", lambda e: e.activation(out=Es[:, 1:2], in_=s1B[:, col:col + 1], func=AF.Copy), reads=[r_s1B], writes=[r_Es])
                n = 2
                while n < T:
                    m = min(n - 1, T - n)
                    allp = []
                    for c in range(NC_):
                        (Ec, r_Ec), (Es, r_Es), tsc = chains[c]
                        cs_ap = Ec[:, n - 1:n]
                        sn_ap = Es[:, n - 1:n]
                        prods = []
                        for k_, (src_, sc_) in enumerate(((Ec, cs_ap), (Es, sn_ap), (Es, cs_ap), (Ec, sn_ap))):
                            tk, r_tk = tsc[k_]
                            if k_ < 2:
                                kb.op("act", lambda e, tk=tk, src_=src_, sc_=sc_: e.activation(
                                    out=tk[:, 0:m], in_=src_[:, 1:1 + m], func=AF.Identity, scale=sc_, bias=zero_c[:]),
                                    reads=[r_Ec, r_Es, r_zero], writes=[r_tk])
                            else:
                                kb.op("dve", lambda e, tk=tk, src_=src_, sc_=sc_: e.tensor_scalar(
                                    out=tk[:, 0:m], in0=src_[:, 1:1 + m], scalar1=sc_, scalar2=None, op0=ALU.mult),
                                    reads=[r_Ec, r_Es], writes=[r_tk])
                            prods.append((tk, r_tk))
                        allp.append(prods)
                    for c in range(NC_):
                        (Ec, r_Ec), (Es, r_Es), tsc = chains[c]
                        prods = allp[c]
                        kb.op("pool", lambda e: e.tensor_tensor(out=Ec[:, n:n + m], in0=prods[0][0][:, 0:m], in1=prods[1][0][:, 0:m],
                                                                op=ALU.subtract), reads=[prods[0][1], prods[1][1]], writes=[r_Ec])
                        kb.op("pool", lambda e: e.tensor_tensor(out=Es[:, n:n + m], in0=prods[2][0][:, 0:m], in1=prods[3][0][:, 0:m],
                                                                op=ALU.add), reads=[prods[2][1], prods[3][1]], writes=[r_Es])
                    n += m
                for c in range(NC_):
                    col = g0 + c
                    (Ec, r_Ec), (Es, r_Es), tsc = chains[c]
                    (ebc, r_ebc), (ebs, r_ebs) = ebufs[c]
                    kb.op("act", lambda e: e.activation(out=ebc[:], in_=Ec[:], func=AF.Copy), reads=[r_Ec], writes=[r_ebc])
                    kb.op("dve", lambda e: e.tensor_copy(out=ebs[:], in_=Es[:]), reads=[r_Es], writes=[r_ebs])
                    kb.dma("act", ETAB[i, col, 0], ebc[:], reads=[r_ebc])
                    kb.dma("act", ETAB[i, col, 1], ebs[:], reads=[r_ebs])
            for g0_ in range(0, 64, NCH):
                ems.append(lambda g0_=g0_: grp(g0_))
            return ems

    def s5_core(l, b):
        i = l // 2
        with ExitStack() as st:
            lbre = kb.sb(st, "bbre", [128, 1024], F32)
            lbim = kb.sb(st, "bbim", [128, 1024], F32)
            mA, r_mA = kb.sb(st, "maskA", [128, 8], F32)
            kb.dma("sp", mA[:], maskA_in, writes=[r_mA])
            rhoB, r_rhoB = kb.sb(st, "rhoB", [128, 64], F32)
            c1B, r_c1B = kb.sb(st, "c1B", [128, 64], F32)
            s1B, r_s1B = kb.sb(st, "s1B", [128, 64], F32)
            s5d, r_s5d = kb.sb(st, "s5d", [128, 2, 8], F32)
            kb.dma("sp", s5d[:], s5d_in, writes=[r_s5d])
            with ExitStack() as sp_:
                def ld(name, src, n):
                    t_, r_ = kb.sb(sp_, name, [128, n], F32)
                    kb.dma("sp", t_[:], src, writes=[r_])
                    return t_, r_
                for (lay, n, lre_in, lim_in, dt_in) in (("A", 1024, lamA_re_in[i], lamA_im_in[i], dtA_in[i]),
                                                          ("B", 64, lamB_re_in[i], lamB_im_in[i], dtB_in[i])):
                    lre, r_lre = ld("lre" + lay, lre_in, n)
                    lim, r_lim = ld("lim" + lay, lim_in, n)
                    dt, r_dt = ld("dt" + lay, dt_in, n)
                    th, r_th = kb.sb(sp_, "th" + lay, [128, n], F32)
                    mag, r_mag = kb.sb(sp_, "mag" + lay, [128, n], F32)
                    sn, r_sn = kb.sb(sp_, "sn" + lay, [128, n], F32)
                    cs_, r_cs_ = kb.sb(sp_, "cs" + lay, [128, n], F32)
                    kb.op("dve", lambda e: e.tensor_scalar(out=lre[:], in0=lre[:], scalar1=-1e-4, scalar2=None, op0=ALU.min),
                          reads=[r_lre], writes=[r_lre])
                    kb.op("act", lambda e: e.activation(out=dt[:], in_=dt[:], func=AF.Exp), reads=[r_dt], writes=[r_dt])
                    kb.op("dve", lambda e: e.tensor_tensor(out=mag[:], in0=lre[:], in1=dt[:], op=ALU.mult), reads=[r_lre, r_dt], writes=[r_mag])
                    kb.op("act", lambda e: e.activation(out=mag[:], in_=mag[:], func=AF.Exp), reads=[r_mag], writes=[r_mag])
                    kb.op("dve", lambda e: e.tensor_tensor(out=th[:], in0=lim[:], in1=dt[:], op=ALU.mult), reads=[r_lim, r_dt], writes=[r_th])
                    if lay == "B":
                        sincos(sp_, "scB", th[:], r_th, [128, n], s1B[:], c1B[:], r_s1B, r_c1B)
                        kb.op("dve", lambda e: e.tensor_copy(out=rhoB[:], in_=mag[:]), reads=[r_mag], writes=[r_rhoB])
                        continue
                    sincos(sp_, "scA", th[:], r_th, [128, n], sn[:], cs_[:], r_sn, r_cs_)
                    bre, r_bre = ld("bre", bTre_in[i], n)
                    bim, r_bim = ld("bim", bTim_in[i], n)
                    den, cre, cim, tmp = [kb.sb(sp_, nm_, [128, n], F32) for nm_ in ("den", "cre", "cim", "tmpA")]

                    def tt(o, a, b_, op):
                        kb.op("dve", lambda e: e.tensor_tensor(out=o[0][:], in0=a[0][:], in1=b_[0][:], op=op),
                              reads=[a[1], b_[1]], writes=[o[1]])
                    LRE, LIM, MAG, SN, CS = (lre, r_lre), (lim, r_lim), (mag, r_mag), (sn, r_sn), (cs_, r_cs_)
                    BRE, BIM = (bre, r_bre), (bim, r_bim)
                    tt(lbre, MAG, CS, ALU.mult)
                    tt(lbim, MAG, SN, ALU.mult)
                    tt(den, LRE, LRE, ALU.mult)
                    tt(tmp, LIM, LIM, ALU.mult)
                    tt(den, den, tmp, ALU.add)
                    kb.op("dve", lambda e: e.reciprocal(out=den[0][:], in_=den[0][:]), reads=[den[1]], writes=[den[1]])
                    kb.op("dve", lambda e: e.tensor_scalar(out=lbre[0][:], in0=lbre[0][:], scalar1=-1.0, scalar2=None, op0=ALU.add),
                          reads=[lbre[1]], writes=[lbre[1]])
                    tt(cre, lbre, LRE, ALU.mult)
                    tt(tmp, lbim, LIM, ALU.mult)
                    tt(cre, cre, tmp, ALU.add)
                    tt(cre, cre, den, ALU.mult)
                    tt(cim, lbim, LRE, ALU.mult)
                    tt(tmp, lbre, LIM, ALU.mult)
                    tt(cim, cim, tmp, ALU.subtract)
                    tt(cim, cim, den, ALU.mult)
                    tt(lbre, cre, BRE, ALU.mult)
                    tt(tmp, cim, BIM, ALU.mult)
                    tt(lbre, lbre, tmp, ALU.subtract)
                    tt(lbim, cre, BIM, ALU.mult)
                    tt(tmp, cim, BRE, ALU.mult)
                    tt(lbim, lbim, tmp, ALU.add)
                kb.barrier()
            EB = [(kb.sb(st, "Ebc%d" % j, [128, T], BF16), kb.sb(st, "Ebs%d" % j, [128, T], BF16)) for j in range(2)]
            PB = [(kb.sb(st, "pbr%d" % j, [128, T], BF16), kb.sb(st, "pbi%d" % j, [128, T], BF16)) for j in range(2)]
            RHO = [kb.sb(st, "rho%d" % j, [128, T], F32) for j in range(2)]
            UU = [(kb.sb(st, "uT%d" % j, [128, T], BF16), kb.sb(st, "uR%d" % j, [128, T], BF16)) for j in range(2)]
            xre, r_xre = kb.sb(st, "xre", [128, T], BF16)
            xim, r_xim = kb.sb(st, "xim", [128, T], BF16)
            wre, r_wre = kb.sb(st, "wre", [128, T], BF16)
            wim, r_wim = kb.sb(st, "wim", [128, T], BF16)
            mm = [kb.sb(st, "mm%d" % j, [128, T], BF16) for j in range(4)]
            hre, r_hre = kb.sb(st, "hre", [128, T], BF16)
            him, r_him = kb.sb(st, "him", [128, T], BF16)
            hre2, r_hre2 = kb.sb(st, "hre2", [128, T], BF16)
            him2, r_him2 = kb.sb(st, "him2", [128, T], BF16)
            yv, r_yv = kb.sb(st, "yv", [128, T], F32)
            g1, r_g1 = kb.sb(st, "g1", [128, T], F32)
            ga, r_ga = kb.sb(st, "ga", [128, T], BF16)
            lcs = [kb.sb(st, "lcs%d" % j, [128, 256], F32) for j in range(2)]
            lcb = [kb.sb(st, "lcb%d" % j, [128, 2, 128], BF16) for j in range(2)]
            ldb = [kb.sb(st, "ldb%d" % j, [128, 2, 128], BF16) for j in range(2)]
            segs = [(0, CTX), (CTX, T)]
            yacc = [ps[3 + j] for j in range(5)]

            def kidx(k):
                cbk, rem = k // 8, k % 8
                r, gpl = rem // 4, rem % 4
                return cbk, r, gpl, cbk * 4 + gpl, r * 32 + cbk * 4 + gpl, r * 8 + cbk

            def load_u(cbk):
                (uT, r_uT), (uR, r_uR) = UU[cbk % 2]
                kb.dma("sp", uT[:], ZT[b, cbk * 128:(cbk + 1) * 128, :], writes=[r_uT])
                for (s0, s1) in segs:
                    kb.op("dve", lambda e, s0=s0, s1=s1: e.tensor_copy(out=uR[:, s0:s1], in_=uT[:, s0:s1][:, ::-1]),
                          reads=[r_uT], writes=[r_uR])

            def fetch(k):
                col = kidx(k)[4]
                (Ec_, r_Ec_), (Es_, r_Es_) = EB[k % 2]
                kb.dma("sp", Ec_[:], ETAB[i, col, 0], writes=[r_Ec_])
                kb.dma("sp", Es_[:], ETAB[i, col, 1], writes=[r_Es_])

            def stageA1(k):
                cbk, r, gpl, gp, col, rc = kidx(k)
                (Ebc, r_Ebc), (Ebs, r_Ebs) = EB[k % 2]
                (pbr, r_pbr), (pbi, r_pbi) = PB[k % 2]
                rho, r_rho = RHO[k % 2]
                (uT, r_uT), (uR, r_uR) = UU[cbk % 2]
                usrc, r_usrc = (uT, r_uT) if r == 0 else (uR, r_uR)
                kb.op("act", lambda e: e.activation(out=rho[:], in_=uT[:], func=AF.Identity, scale=0.0, bias=rhoB[:, col:col + 1]),
                      reads=[r_uT, r_rhoB], writes=[r_rho])
                l_d, r_ld = ldb[k % 2]
                for gi in range(2):
                    for ri, src in enumerate((lbre, lbim)):
                        kb.op("pool", lambda e, src=src, ri=ri, gi=gi: e.tensor_scalar(
                            out=l_d[:, ri, gi * 64:(gi + 1) * 64], in0=src[0][:, rc * 64:(rc + 1) * 64],
                            scalar1=mA[:, 2 * gpl + gi:2 * gpl + gi + 1], scalar2=0.0, op0=ALU.mult, op1=ALU.add),
                            reads=[src[1], r_mA], writes=[r_ld])
                l_s, r_ls = lcs[k % 2]
                l_b, r_lb = lcb[k % 2]
                kb.dma("sp", l_s[:], LC_in[i, col], writes=[r_ls])
                kb.op("pool", lambda e: e.tensor_scalar(out=l_b[:, 0, :], in0=l_s[:, 0:128], scalar1=1.0, scalar2=0.0, op0=ALU.mult, op1=ALU.add),
                      reads=[r_ls], writes=[r_lb])
                kb.op("pool", lambda e: e.tensor_scalar(out=l_b[:, 1, :], in0=l_s[:, 128:256], scalar1=-1.0, scalar2=0.0, op0=ALU.mult, op1=ALU.add),
                      reads=[r_ls], writes=[r_lb])
                for ti, (t0, n_) in enumerate(TILES_ALL):
                    pr, r_pr = ps[ti % 2]
                    pi_, r_pi = ps[2]
                    sl = slice(t0, t0 + n_)
                    kb.op("pe", lambda e: e.matmul(pr[:, 0:n_], lhsT=l_d[:, 0, :], rhs=usrc[:, t0:t0 + n_], start=True, stop=True),
                          reads=[r_ld, r_usrc], writes=[r_pr])
                    kb.op("pe", lambda e: e.matmul(pi_[:, 0:n_], lhsT=l_d[:, 1, :], rhs=usrc[:, t0:t0 + n_], start=True, stop=True),
                          reads=[r_ld, r_usrc], writes=[r_pi])
                    kb.op("act", lambda e: e.activation(out=pbr[:, sl], in_=pr[:, 0:n_], func=AF.Copy), reads=[r_pr], writes=[r_pbr])
                    kb.op("act", lambda e: e.activation(out=pbi[:, sl], in_=pi_[:, 0:n_], func=AF.Copy), reads=[r_pi], writes=[r_pbi])

            def stageA2(k):
                (Ebc, r_Ebc), (Ebs, r_Ebs) = EB[k % 2]
                (pbr, r_pbr), (pbi, r_pbi) = PB[k % 2]
                (a_t, r_a), (b_t, r_b), (c_t, r_c), (d_t, r_d) = mm
                kb.op("dve", lambda e: e.tensor_tensor(out=a_t[:], in0=pbr[:], in1=Ebc[:], op=ALU.mult), reads=[r_pbr, r_Ebc], writes=[r_a])
                kb.op("dve", lambda e: e.tensor_tensor(out=c_t[:], in0=pbi[:], in1=Ebs[:], op=ALU.mult), reads=[r_pbi, r_Ebs], writes=[r_c])
                kb.op("dve", lambda e: e.tensor_tensor(out=xre[:], in0=c_t[:], in1=a_t[:], op=ALU.add), reads=[r_c, r_a], writes=[r_xre])
                kb.op("dve", lambda e: e.tensor_tensor(out=b_t[:], in0=pbr[:], in1=Ebs[:], op=ALU.mult), reads=[r_pbr, r_Ebs], writes=[r_b])
                kb.op("dve", lambda e: e.tensor_tensor(out=d_t[:], in0=pbi[:], in1=Ebc[:], op=ALU.mult), reads=[r_pbi, r_Ebc], writes=[r_d])
                kb.op("dve", lambda e: e.tensor_tensor(out=xim[:], in0=d_t[:], in1=b_t[:], op=ALU.subtract), reads=[r_d, r_b], writes=[r_xim])

            def stageB(k):
                cbk, r, gpl, gp, col, rc = kidx(k)
                l_b, r_lb = lcb[k % 2]
                (Ebc, r_Ebc), (Ebs, r_Ebs) = EB[k % 2]
                rho, r_rho = RHO[k % 2]
                kb.op("dve", lambda e: e.tensor_tensor_scan(out=wre[:], data0=rho[:], data1=xre[:], initial=0.0,
                                                            op0=ALU.mult, op1=ALU.add), reads=[r_rho, r_xre], writes=[r_wre])
                kb.op("dve", lambda e: e.tensor_tensor_scan(out=wim[:], data0=rho[:], data1=xim[:], initial=0.0,
                                                            op0=ALU.mult, op1=ALU.add), reads=[r_rho, r_xim], writes=[r_wim])
                for k_, (wa, r_wa, eb, r_eb) in enumerate(((wre, r_wre, Ebc, r_Ebc), (wim, r_wim, Ebs, r_Ebs),
                                                           (wre, r_wre, Ebs, r_Ebs), (wim, r_wim, Ebc, r_Ebc))):
                    kb.op("dve", lambda e, k_=k_, wa=wa, eb=eb: e.tensor_tensor(out=mm[k_][0][:], in0=wa[:], in1=eb[:], op=ALU.mult),
                          reads=[r_wa, r_eb], writes=[mm[k_][1]])
                kb.op("dve", lambda e: e.tensor_tensor(out=hre[:], in0=mm[0][0][:], in1=mm[1][0][:], op=ALU.subtract),
                      reads=[mm[0][1], mm[1][1]], writes=[r_hre])
                kb.op("dve", lambda e: e.tensor_tensor(out=him[:], in0=mm[2][0][:], in1=mm[3][0][:], op=ALU.add),
                      reads=[mm[2][1], mm[3][1]], writes=[r_him])
                hr_, r_hr_, hi_, r_hi_ = hre, r_hre, him, r_him
                if r == 1:
                    for (s0, s1) in segs:
                        kb.op("dve", lambda e, s0=s0, s1=s1: e.tensor_copy(out=hre2[:, s0:s1], in_=hre[:, s0:s1][:, ::-1]),
                              reads=[r_hre], writes=[r_hre2])
                        kb.op("dve", lambda e, s0=s0, s1=s1: e.tensor_copy(out=him2[:, s0:s1], in_=him[:, s0:s1][:, ::-1]),
                              reads=[r_him], writes=[r_him2])
                    hr_, r_hr_, hi_, r_hi_ = hre2, r_hre2, him2, r_him2
                first = (r == 0 and gpl == 0)
                last = (r == 1 and gpl == 3)
                for ti, (t0, n_) in enumerate(TILES_ALL):
                    ya, r_ya = yacc[ti]
                    kb.op("pe", lambda e: e.matmul(ya[:, 0:n_], lhsT=l_b[:, 0, :], rhs=hr_[:, t0:t0 + n_], start=first, stop=False),
                          reads=[r_lb, r_hr_], writes=[r_ya])
                    kb.op("pe", lambda e: e.matmul(ya[:, 0:n_], lhsT=l_b[:, 1, :], rhs=hi_[:, t0:t0 + n_], start=False, stop=last),
                          reads=[r_lb, r_hi_], writes=[r_ya])

            def epilogue(cbk):
                (uT, r_uT), _ = UU[cbk % 2]
                for ti, (t0, n_) in enumerate(TILES_ALL):
                    ya, r_ya = yacc[ti]
                    kb.op("dve", lambda e: e.scalar_tensor_tensor(out=yv[:, t0:t0 + n_], in0=uT[:, t0:t0 + n_], scalar=s5d[:, i, cbk:cbk + 1],
                                                                  in1=ya[:, 0:n_], op0=ALU.mult, op1=ALU.add),
                          reads=[r_uT, r_s5d, r_ya], writes=[r_yv])
                kb.op("act", lambda e: e.activation(out=g1[:], in_=yv[:], func=AF.Square), reads=[r_yv], writes=[r_g1])
                kb.op("pool", lambda e: e.tensor_scalar(out=g1[:], in0=g1[:], scalar1=0.044715, scalar2=1.0, op0=ALU.mult, op1=ALU.add),
                      reads=[r_g1], writes=[r_g1])
                kb.op("pool", lambda e: e.tensor_tensor(out=g1[:], in0=g1[:], in1=yv[:], op=ALU.mult), reads=[r_g1, r_yv], writes=[r_g1])
                kb.op("act", lambda e: e.activation(out=g1[:], in_=g1[:], func=AF.Sigmoid, scale=2.0 * math.sqrt(2.0 / math.pi)),
                      reads=[r_g1], writes=[r_g1])
                kb.op("pool", lambda e: e.tensor_tensor(out=ga[:], in0=g1[:], in1=yv[:], op=ALU.mult), reads=[r_g1, r_yv], writes=[r_ga])
                kb.dma("act", GAT[b, cbk * 128:(cbk + 1) * 128, :], ga[:], reads=[r_ga])

            load_u(0)
            fetch(0)
            stageA1(0)
            for k in range(64):
                if k + 1 < 64:
                    if (k + 1) % 8 == 0:
                        load_u((k + 1) // 8)
                    fetch(k + 1)
                    stageA1(k + 1)
                stageA2(k)
                stageB(k)
                if k % 8 == 7:
                    epilogue(k // 8)
            kb.barrier()
        kb.mark("  glu")
        with ExitStack() as st:
            ta, r_ta = kb.sb(st, "ta", [128, 8, T], BF16)
            kb.dma("sp", ta[:], GAT[b].rearrange("(kc p) t -> p kc t", p=128), writes=[r_ta])
            ws = WStream(st, "wglu", 8, 128)
            sgm = [kb.sb(st, "sgm%d" % j, [128, 512], F32) for j in range(2)]
            yrow = [kb.sb(st, "gyrow%d" % j, [128, T], BF16) for j in range(2)]
            for ob in range(8):
                w_b, r_w = ws.load(w_glu[i], ob * 128, 128)
                yr, r_yr = yrow[ob % 2]
                for ti, (t0, n_) in enumerate(TILES_ALL):
                    p_t, r_p = nps(0, 4)
                    for kc in range(8):
                        kb.op("pe", lambda e, kc=kc: e.matmul(p_t[:, 0:n_], lhsT=w_b[:, kc, 0:128], rhs=ta[:, kc, t0:t0 + n_],
                                                             start=(kc == 0), stop=(kc == 7)), reads=[r_w, r_ta], writes=[r_p])
                    s_t, r_s = sgm[ti % 2]
                    kb.op("act", lambda e: e.activation(out=s_t[:, 0:n_], in_=p_t[:, 0:n_], func=AF.Sigmoid), reads=[r_p], writes=[r_s])
                    kb.op("dve", lambda e: e.tensor_tensor(out=yr[:, t0:t0 + n_], in0=s_t[:, 0:n_], in1=ta[:, ob, t0:t0 + n_], op=ALU.mult),
                          reads=[r_s, r_ta], writes=[r_yr])
                kb.dma("act", YT[b, ob * 128:(ob + 1) * 128, :], yr[:], reads=[r_yr])
            kb.barrier()
    stage_S0()
    for l in range(n_layers):
        with_ctx = l < DEPTH - 1
        for b in range(2):
            if l == 0:
                for kc in range(KC):
                    kb.dma("sp", XT[b, kc * 128:(kc + 1) * 128, :], xT_in[b, kc * 128:(kc + 1) * 128, :])
                kb.barrier()
            if mixers:
                with ExitStack() as stA:
                    hT, r_hT = kb.sb(stA, "hT", [128, KC, T], BF16)
                    kb.mark("L%d b%d norm1" % (l, b))
                    norm_stage("n1", XT[b], TILES_ALL,
                               lambda kc, is_ctx: A1[:, l, kc, (2 if is_ctx else b):(3 if is_ctx else b + 1)],
                               lambda kc, is_ctx: modv(l, 0, kc, 2 if is_ctx else b), hT, r_hT)
                    kb.mark("L%d b%d proj" % (l, b))
                    if l % 2 == 0:
                        even_proj(l, b, hT, r_hT)
                    else:
                        odd_proj(l, b, hT, r_hT)
                if l % 2 == 0:
                    kb.mark("L%d b%d na" % (l, b))
                    if "na" in mix_parts:
                        na_core(l, b, with_ctx)
                    if "s5" in mix_parts:
                        kb.mark("L%d b%d s5" % (l, b))
                        s5_core(l, b)
                else:
                    kb.mark("L%d b%d oddcore" % (l, b))
                    odd_core(l, b, with_ctx)
                kb.mark("L%d b%d wout" % (l, b))
                with ExitStack() as stA:
                    hT, r_hT = kb.sb(stA, "yTsb", [128, KC, T], BF16)
                    yq = [(4 * q, 4 * (q + 1), Res()) for q in range(4)]
                    for (lo_, hi_, r_q) in yq:
                        kb.dma("sp", hT[:, lo_:hi_, :], YT[b].rearrange("(kc p) t -> p kc t", p=128)[:, lo_:hi_, :], writes=[r_q])
                    tiles = TILES_ALL if with_ctx else TILES_LAT
                    ws = WStream(stA, "wo", KC, 128)
                    linear_resid(stA, "wo", b, hT, yq, KC, w_out[l], tiles,
                                 lambda blk, is_ctx: modv(l, 2, blk, 2 if is_ctx else b), ws, (0, T))
                    kb.barrier()
            with ExitStack() as stA:
                hT, r_hT = kb.sb(stA, "h2T", [128, KC, T], BF16)
                tiles = TILES_ALL if with_ctx else TILES_LAT
                kb.mark("L%d b%d norm2" % (l, b))
                norm_stage("n2", XT[b], tiles,
                           lambda kc, is_ctx: A2[:, l, kc, (2 if is_ctx else b):(3 if is_ctx else b + 1)],
                           lambda kc, is_ctx: modv(l, 3, kc, 2 if is_ctx else b), hT, r_hT)
                kb.mark("L%d b%d ffn_up" % (l, b))
                ffn_up(l, b, hT, r_hT, with_ctx)
            kb.mark("L%d b%d ffn_down" % (l, b))
            ffn_down(l, b, with_ctx)

    kb.mark("final")
    for b in range(2):
        with ExitStack() as st:
            xt = [kb.sb(st, "fx%d" % i, [128, KC, 512], F32) for i in range(2)]
            sq = [kb.sb(st, "fq%d" % i, [128, KC, 512], BF16) for i in range(2)]
            rt = [kb.sb(st, "fr%d" % i, [128, 512], F32) for i in range(2)]
            srcv = XT[b].rearrange("(kc p) t -> p kc t", p=128)
            dstv = out_T[b].rearrange("(kc p) t -> p kc t", p=128)
            for ti, (t0, n) in enumerate(TILES_LAT):
                x_t, r_x = xt[ti % 2]
                q_t, r_q = sq[ti % 2]
                r_t, r_r = rt[ti % 2]
                p_t, r_p = ps[ti % 2]
                kb.dma("sp", x_t[:], srcv[:, :, t0:t0 + n], writes=[r_x])
                kb.op("act", lambda e: e.activation(out=q_t[:], in_=x_t[:], func=AF.Square), reads=[r_x], writes=[r_q])
                for kc in range(KC):
                    kb.op("pe", lambda e, kc=kc: e.matmul(p_t[:], lhsT=ones_b[:], rhs=q_t[:, kc, :],
                                                         start=(kc == 0), stop=(kc == KC - 1)),
                          reads=[r_q, r_ones], writes=[r_p])
                kb.op("dve", lambda e: e.tensor_scalar(out=r_t[:], in0=p_t[:], scalar1=1.0 / D, scalar2=EPS,
                                                       op0=ALU.mult, op1=ALU.add), reads=[r_p], writes=[r_r])
                kb.op("act", lambda e: e.activation(out=r_t[:], in_=r_t[:], func=AF.Sqrt), reads=[r_r], writes=[r_r])
                kb.op("dve", lambda e: e.reciprocal(out=r_t[:], in_=r_t[:]), reads=[r_r], writes=[r_r])
                kb.op("dve", lambda e: e.tensor_tensor(out=x_t[:], in0=x_t[:],
                                                       in1=r_t[:].unsqueeze(1).broadcast_to([128, KC, 512]),
                                                       op=ALU.mult), reads=[r_x, r_r], writes=[r_x])
                for kc in range(KC):
                    kb.op("act", lambda e, kc=kc: e.activation(out=x_t[:, kc, :], in_=x_t[:, kc, :], func=AF.Identity,
                                                               scale=gfin[:, kc:kc + 1], bias=zero_c[:]),
                          reads=[r_x, r_gfin, r_zero], writes=[r_x])
                kb.dma("act", dstv[:, :, t0 - CTX:t0 - CTX + n], x_t[:], reads=[r_x])
            kb.barrier()
    ex.close()
    return kb


_CACHE = {}


def host_inputs(inp, core, n_cores=8):
    b0 = 2 * core
    m = {}
    xT = np.empty((2, D, T), np.float32)
    for i in range(2):
        xT[i, :, :CTX] = inp["ctx"][b0 + i].T
        xT[i, :, CTX:] = inp["x"][b0 + i].T
    m["xT"] = xT
    cv = np.stack([inp["c"][b0], inp["c"][b0 + 1], inp["c_ctx"]], axis=-1)
    m["cT"] = np.ascontiguousarray(cv.reshape(KC, 128, 3).transpose(1, 0, 2))
    return m


def shared_inputs(inp):
    m = {}
    m["ada_w"] = np.ascontiguousarray(inp["ada_w"], np.float32)
    m["adab"] = np.ascontiguousarray(np.asarray(inp["ada_b"], np.float32).reshape(DEPTH, 96, 128).transpose(2, 0, 1))
    m["gmix"] = np.ascontiguousarray(np.asarray(inp["norm_mix_g"], np.float32).reshape(DEPTH, KC, 128).transpose(2, 0, 1))
    m["gffn"] = np.ascontiguousarray(np.asarray(inp["norm_ffn_g"], np.float32).reshape(DEPTH, KC, 128).transpose(2, 0, 1))
    m["gfin"] = fm(inp["final_norm_g"], KC)
    m["w_out"] = np.ascontiguousarray(inp["w_out"], np.float32)
    m["w_up"] = np.ascontiguousarray(inp["ffn_w_up"], np.float32)
    m["w_down"] = np.ascontiguousarray(inp["ffn_w_down"], np.float32)
    m["dww"] = np.ascontiguousarray(np.asarray(inp["ffn_dw_w"], np.float32).reshape(DEPTH, 3, 88, 128).transpose(3, 0, 1, 2))
    m["dwb"] = np.ascontiguousarray(np.asarray(inp["ffn_dw_b"], np.float32).reshape(DEPTH, 88, 128).transpose(2, 0, 1))
    m["ident"] = np.eye(128, dtype=np.float32)
    f32 = lambda a: np.ascontiguousarray(np.asarray(a, np.float32))
    m["ev_w_in"] = f32(inp["ev_w_in"])
    m["od_w_in"] = f32(inp["od_w_in"])
    m["w_glu"] = f32(inp["s5_w_glu"])
    t = np.arange(LAT)
    pos = np.stack([t // 64, t % 64], 0).astype(np.float32)
    inv = (10000.0 ** (-2.0 * np.arange(32, dtype=np.float32) / 64)).astype(np.float32)
    d = np.arange(128)
    axis, half, fr = d // 64, (d % 64) // 32, d % 32
    ang = pos[axis] * inv[fr][:, None]
    m["ropeC"] = f32(np.cos(ang))
    m["ropeS"] = f32(np.where(half[:, None] == 0, -np.sin(ang), np.sin(ang)))
    partner = np.where(half == 0, d + 32, d - 32)
    pm = np.zeros((128, 128), np.float32)
    pm[partner, d] = 1.0
    m["permM"] = pm
    jj, ii = np.meshgrid(np.arange(128), np.arange(128), indexing="ij")
    m["mprev"] = f32(jj >= ii)
    m["mnext"] = f32(jj <= ii)
    m["dlam"] = f32(np.asarray(inp["diff_lambda"]).reshape(2, 512))
    m["subg"] = f32(inp["diff_subln_g"])
    m["sink"] = f32(inp["swa_sink"])
    kc, qc = np.meshgrid(np.arange(64), np.arange(64), indexing="ij")
    cidx = np.clip(kc - qc + 15, 0, 30)
    rpb = np.asarray(inp["na_rpb"], np.float32)
    tbg = rpb[:, :, :, cidx]
    m["natb"] = f32(tbg.transpose(0, 3, 1, 2, 4).reshape(2, 64, 8 * 15 * 64))
    ws_ = np.clip(qc - 8, 0, 48)
    m["namask"] = f32(np.where((kc >= ws_) & (kc < ws_ + 16), 0.0, -1e30))
    lre = np.asarray(inp["s5_lam_re"], np.float32)
    lim = np.asarray(inp["s5_lam_im"], np.float32)
    ldt = np.asarray(inp["s5_log_dt"], np.float32)
    bre = np.asarray(inp["s5_b_re"], np.float32)
    bim = np.asarray(inp["s5_b_im"], np.float32)

    def layA(a):
        a = a.reshape(2, 2, 8, 8, 64)
        a = np.broadcast_to(a[:, :, :, :, None, :], (2, 2, 8, 8, 16, 64))
        return f32(a.transpose(0, 3, 4, 1, 2, 5).reshape(2, 128, 1024))

    def layB(a):
        a = a.reshape(2, 2, 32, 2, 64)
        return f32(a.transpose(0, 3, 4, 1, 2).reshape(2, 128, 64))

    ldt4 = np.broadcast_to(ldt[:, :, :, None], (2, 2, 64, 64))
    m["lamA_re"], m["lamA_im"], m["dtA"] = layA(lre), layA(lim), layA(ldt4)
    m["lamB_re"], m["lamB_im"], m["dtB"] = layB(lre), layB(lim), layB(ldt4)

    def layBT(a):
        a = a.reshape(2, 2, 8, 8, 64, 16)
        return f32(a.transpose(0, 3, 5, 1, 2, 4).reshape(2, 128, 1024))
    m["bTre"], m["bTim"] = layBT(bre), layBT(bim)
    cre = np.asarray(inp["s5_c_re"], np.float32)
    cim = np.asarray(inp["s5_c_im"], np.float32)
    LC = np.zeros((2, 2, 32, 2, 64, 2, 128), np.float32)
    for g in range(64):
        gp, gi = g // 2, g % 2
        c0 = 16 * (g % 8)
        LC[:, :, gp, gi, :, 0, c0:c0 + 16] = cre[:, :, g].transpose(0, 1, 3, 2)
        LC[:, :, gp, gi, :, 1, c0:c0 + 16] = cim[:, :, g].transpose(0, 1, 3, 2)
    m["LC"] = f32(LC.reshape(2, 64, 128, 256))
    kk = np.arange(128)[:, None] // 16
    m["maskA"] = f32(kk == np.arange(8)[None, :])
    m["s5d"] = f32(np.asarray(inp["s5_d"], np.float32).reshape(2, 8, 128).transpose(2, 0, 1))
    return m


def kernel(**inputs):
    inp = {k: np.asarray(v) for k, v in inputs.items()}
    if "kb" not in _CACHE:
        _CACHE["kb"] = build_program()
    kb = _CACHE["kb"]
    sh = shared_inputs(inp)
    in_maps = []
    for c in range(8):
        m = dict(sh)
        m.update(host_inputs(inp, c))
        in_maps.append(m)
    res = run_bass_kernel_spmd(kb.nc, in_maps, core_ids=list(range(8)))
    out = np.empty((16, LAT, D), np.float32)
    for c in range(8):
        o = np.asarray(res.results[c]["outT"])
        for i in range(2):
            out[2 * c + i] = o[i].T
    return out
```
